# Optimizing a Trainium2 kernel written in Bass

```python
import jax, jax.numpy as jnp
from jax import lax
import numpy as np

D_MODEL = 2048
BATCH = 8
SEQ = 2048
DEPTH = 1

HEAD_DIM = 128
N_HEADS_SB = 8
N_HEADS_FOX = 8
D_SB = N_HEADS_SB * HEAD_DIM
D_FOX = N_HEADS_FOX * HEAD_DIM
D_FF = -(-8 * D_MODEL // (3 * 256)) * 256
Q_BLOCK = 128
RMS_EPS = 1e-6
SPLIT_SIZES = (D_SB, D_SB, D_SB, D_FOX, D_FOX, D_FOX, N_HEADS_FOX, D_MODEL, D_MODEL)
SPLIT_POINTS = (D_SB, 2 * D_SB, 3 * D_SB,
                3 * D_SB + D_FOX, 3 * D_SB + 2 * D_FOX, 3 * D_SB + 3 * D_FOX,
                3 * D_SB + 3 * D_FOX + N_HEADS_FOX,
                3 * D_SB + 3 * D_FOX + N_HEADS_FOX + D_MODEL)
D_IN = 3 * D_SB + 3 * D_FOX + N_HEADS_FOX + 2 * D_MODEL

kernel_name = "hybrid_stickbreak_forgetting_gated_block"


def rms_norm(x, g):
    xf = x.astype(jnp.float32)
    y = xf * lax.rsqrt(jnp.mean(xf * xf, axis=-1, keepdims=True) + RMS_EPS)
    return (y * g.astype(jnp.float32)).astype(x.dtype)


def split_heads(t, n_heads):
    b, s, _ = t.shape
    return t.reshape(b, s, n_heads, HEAD_DIM).transpose(0, 2, 1, 3)


def merge_heads(t):
    b, h, s, d = t.shape
    return t.transpose(0, 2, 1, 3).reshape(b, s, h * d)


def stick_breaking_block(q_blk, k, v, q_start):
    tq, tk = q_blk.shape[2], k.shape[2]
    z = jnp.einsum("bhqd,bhkd->bhqk", q_blk, k).astype(jnp.float32) * (HEAD_DIM ** -0.5)
    t_idx = q_start + jnp.arange(tq)[:, None]
    s_idx = jnp.arange(tk)[None, :]
    mask = s_idx < t_idx
    log_keep = jnp.where(mask, jax.nn.log_sigmoid(-z), 0.0)
    between = lax.cumsum(log_keep, axis=3, reverse=True) - log_keep
    w = jnp.where(mask, jnp.exp(jax.nn.log_sigmoid(z) + between), 0.0)
    return jnp.einsum("bhqk,bhkd->bhqd", w.astype(v.dtype), v)


def forgetting_block(q_blk, k, v, cum_q, cum_k, q_start):
    tq, tk = q_blk.shape[2], k.shape[2]
    logits = jnp.einsum("bhqd,bhkd->bhqk", q_blk, k).astype(jnp.float32) * (HEAD_DIM ** -0.5)
    logits = logits + cum_q[..., :, None] - cum_k[..., None, :]
    t_idx = q_start + jnp.arange(tq)[:, None]
    s_idx = jnp.arange(tk)[None, :]
    logits = jnp.where(s_idx <= t_idx, logits, -jnp.inf)
    p = jax.nn.softmax(logits, axis=-1)
    return jnp.einsum("bhqk,bhkd->bhqd", p.astype(v.dtype), v)


def token_mixer(u, w_in, b_forget, w_branch_sb, w_branch_fox, w_out):
    s = u.shape[1]
    proj = u @ w_in
    q_sb, k_sb, v_sb, q_fx, k_fx, v_fx, f_logit, g_sb, g_fx = jnp.split(proj, SPLIT_POINTS, axis=-1)
    q_sb, k_sb, v_sb = (split_heads(t, N_HEADS_SB) for t in (q_sb, k_sb, v_sb))
    q_fx, k_fx, v_fx = (split_heads(t, N_HEADS_FOX) for t in (q_fx, k_fx, v_fx))
    log_f = jax.nn.log_sigmoid((f_logit + b_forget).astype(jnp.float32))
    cum = lax.cumsum(log_f.transpose(0, 2, 1), axis=2)

    outs_sb, outs_fx = [], []
    for i in range(s // Q_BLOCK):
        q0, q1 = i * Q_BLOCK, (i + 1) * Q_BLOCK
        outs_sb.append(stick_breaking_block(q_sb[:, :, q0:q1], k_sb[:, :, :q1], v_sb[:, :, :q1], q0))
        outs_fx.append(forgetting_block(q_fx[:, :, q0:q1], k_fx[:, :, :q1], v_fx[:, :, :q1],
                                        cum[:, :, q0:q1], cum[:, :, :q1], q0))
    o_sb = merge_heads(jnp.concatenate(outs_sb, axis=2))
    o_fx = merge_heads(jnp.concatenate(outs_fx, axis=2))

    merged = jax.nn.sigmoid(g_sb) * (o_sb @ w_branch_sb) + jax.nn.sigmoid(g_fx) * (o_fx @ w_branch_fox)
    return merged @ w_out


def swiglu(u, w_gate, w_up, w_down):
    return (jax.nn.silu(u @ w_gate) * (u @ w_up)) @ w_down


def setup_inputs(seed: int = 0) -> dict:
    key = jax.random.key(seed)
    ks = jax.random.split(key, 14)
    f32 = jnp.float32

    def dense(k, fan_in, fan_out):
        return jax.random.normal(k, (DEPTH, fan_in, fan_out), f32) * (fan_in ** -0.5)

    def gain(k):
        return 1.0 + 0.02 * jax.random.normal(k, (DEPTH, D_MODEL), f32)

    return {
        "x": jax.random.normal(ks[0], (BATCH, SEQ, D_MODEL), f32),
        "norm_mix_pre": gain(ks[1]),
        "norm_mix_post": gain(ks[2]),
        "w_in": dense(ks[3], D_MODEL, D_IN),
        "b_forget": 3.0 + 0.1 * jax.random.normal(ks[4], (DEPTH, N_HEADS_FOX), f32),
        "w_branch_sb": dense(ks[5], D_SB, D_MODEL),
        "w_branch_fox": dense(ks[6], D_FOX, D_MODEL),
        "w_out": dense(ks[7], D_MODEL, D_MODEL),
        "norm_ffn_pre": gain(ks[8]),
        "norm_ffn_post": gain(ks[9]),
        "w_ffn_gate": dense(ks[10], D_MODEL, D_FF),
        "w_ffn_up": dense(ks[11], D_MODEL, D_FF),
        "w_ffn_down": dense(ks[12], D_FF, D_MODEL),
    }


def reference(x, norm_mix_pre, norm_mix_post, w_in, b_forget, w_branch_sb, w_branch_fox, w_out,
              norm_ffn_pre, norm_ffn_post, w_ffn_gate, w_ffn_up, w_ffn_down):
    h = x
    for l in range(DEPTH):
        mix = token_mixer(rms_norm(h, norm_mix_pre[l]), w_in[l], b_forget[l],
                          w_branch_sb[l], w_branch_fox[l], w_out[l])
        h = h + rms_norm(mix, norm_mix_post[l])
        ff = swiglu(rms_norm(h, norm_ffn_pre[l]), w_ffn_gate[l], w_ffn_up[l], w_ffn_down[l])
        h = h + rms_norm(ff, norm_ffn_post[l])
    return h
```

```python
import numpy as np
import ml_dtypes
from contextlib import ExitStack
import concourse.bass as bass
import concourse.mybir as mybir
from concourse.bass_utils import run_bass_kernel_spmd

F32 = mybir.dt.float32
BF16 = mybir.dt.bfloat16
AF = mybir.ActivationFunctionType
ALU = mybir.AluOpType

D = 2048
S = 2048
NH = 8
DH = 128
DFF = 5632
DIN = 10248
KC = D // 128
FC = DFF // 128
NB = S // 128
NCH = S // 512
EPS = 1e-6
SCALE = DH ** -0.5
COL_F = 6144
COL_GSB = 6152
COL_GFX = 8200
EPOCH = 30000


class Res:
    __slots__ = ("name", "w", "rs")

    def __init__(self, name):
        self.name = name
        self.w = None
        self.rs = {}


class DSem:
    __slots__ = ("sem", "count")

    def __init__(self, sem):
        self.sem = sem
        self.count = 0


class Eng:
    def __init__(self, kb, name):
        self.kb = kb
        self.name = name
        self.sem = kb.new_sem("e_" + name)
        self.own_sems = {self.sem}
        self.count = 0
        self.waited = {}
        self.prog = []

    def wait(self, ev):
        sem, val = ev
        if self.waited.get(sem, 0) >= val:
            return
        self.prog.append(("w", sem, val))
        self.waited[sem] = val

    def tick(self):
        if self.count >= EPOCH:
            self.sem = self.kb.new_sem("e_" + self.name)
            self.own_sems.add(self.sem)
            self.count = 0
        self.count += 1
        return (self.sem, self.count)

    def replay(self, eng):
        for it in self.prog:
            if it[0] == "w":
                eng.wait_ge(it[1], it[2])
            else:
                _, calls, sem, inc = it
                ins = None
                for meth, kw in calls:
                    ins = getattr(eng, meth)(**kw)
                    if inc == 16:
                        ins.then_inc(sem, 16)
                if inc == 1:
                    ins.then_inc(sem, 1)


class KB:
    def __init__(self, nc, stack):
        self.nc = nc
        self.stack = stack
        self.nsem = 0
        self.pe = Eng(self, "pe")
        self.act = Eng(self, "act")
        self.dve = Eng(self, "dve")
        self.pool = Eng(self, "pool")
        self.sp = Eng(self, "sp")

    def new_sem(self, name):
        self.nsem += 1
        return self.stack.enter_context(self.nc.semaphore(f"{name}_{self.nsem}"))

    def dsem(self, name):
        return DSem(self.new_sem("d_" + name))

    @staticmethod
    def _deps(reads, writes):
        evs = []
        for r in reads:
            if r.w is not None:
                evs.append(r.w)
        for r in writes:
            if r.w is not None:
                evs.append(r.w)
            for s, v in r.rs.items():
                evs.append((s, v))
        return evs

    @staticmethod
    def _commit(ev, reads, writes):
        s, v = ev
        for r in reads:
            if r.rs.get(s, 0) < v:
                r.rs[s] = v
        for r in writes:
            r.w = ev
            r.rs = {}

    def op(self, E, meth, kw, reads=(), writes=()):
        return self.group(E, [(meth, kw)], reads, writes)

    def group(self, E, calls, reads=(), writes=()):
        for ev in self._deps(reads, writes):
            if E is self.pe and ev[0] in E.own_sems:
                continue
            E.wait(ev)
        ev = E.tick()
        E.prog.append(("o", list(calls), ev[0], 1))
        self._commit(ev, reads, writes)
        return ev

    def dma(self, Q, ds, out, in_, reads=(), writes=()):
        return self.dma_multi(Q, ds, [(out, in_)], reads, writes)

    def dma_multi(self, Q, ds, pairs, reads=(), writes=()):
        for ev in self._deps(reads, writes):
            Q.wait(ev)
        Q.prog.append(("o", [("dma_start", dict(out=o, in_=i)) for o, i in pairs], ds.sem, 16))
        ds.count += 16 * len(pairs)
        ev = (ds.sem, ds.count)
        self._commit(ev, reads, writes)
        return ev


class Alloc:
    BASE = 16512
    LIMIT = 16512 + 212736

    def __init__(self, nc):
        self.nc = nc
        self.top = self.BASE
        self.n = 0
        self.live = []
        self.dead = []

    def mark(self):
        return self.top

    def release(self, m):
        keep = []
        for it in self.live:
            (self.dead if it[0] >= m else keep).append(it)
        self.live = keep
        self.top = m

    def kill(self, res):
        keep = []
        for it in self.live:
            (self.dead if it[2] is res else keep).append(it)
        self.live = keep

    def register(self, off, end, res):
        for (o, e, r) in self.dead:
            if o < end and off < e:
                evs = list(r.rs.items())
                if r.w is not None:
                    evs.append(r.w)
                for s_, v in evs:
                    if res.rs.get(s_, 0) < v:
                        res.rs[s_] = v
        self.live.append((off, end, res))

    def alloc(self, name, shape, dtype):
        nbytes = int(np.prod(shape[1:])) * (2 if dtype == BF16 else 4)
        off = (self.top + 63) // 64 * 64
        assert off + nbytes <= self.LIMIT, f"SBUF overflow allocating {name}: {off + nbytes - self.LIMIT} over"
        self.top = off + nbytes
        self.last_off = off
        self.last_end = off + nbytes
        self.n += 1
        return self.nc.alloc_sbuf_tensor_at(f"{name}_{self.n}", shape, dtype, offset=off)


class Filler:
    def __init__(self, items, nticks):
        self.items = list(items)
        self.left = max(1, nticks)
        self.credit = 0.0

    def tick(self):
        if self.items:
            self.credit += len(self.items) / max(1, self.left)
        self.left -= 1
        while self.credit >= 1.0 - 1e-9 and self.items:
            self.items.pop(0)()
            self.credit -= 1.0

    def burst(self, k):
        for _ in range(k):
            if self.items:
                self.items.pop(0)()

    def flush(self):
        while self.items:
            self.items.pop(0)()


class Slot:
    def __init__(self, kb, name, shape, dtype, dma=False):
        self.t = kb.al.alloc(name, shape, dtype)
        self.r = Res(name)
        kb.al.register(kb.al.last_off, kb.al.last_end, self.r)
        self.ds = kb.dsem(name) if dma else None
        self.kb = kb
        self.name = name
        self.off = kb.al.last_off

    def view(self, shape, dtype):
        kb = self.kb
        kb.al.n += 1
        return kb.nc.alloc_sbuf_tensor_at(f"{self.name}_v{kb.al.n}", shape, dtype, offset=self.off)


def build_nc(debug=None, upto=99):
    nc = bass.Bass("TRN2", target_bir_lowering=False)
    G = {}
    G["xT"] = nc.dram_tensor("xT", [D, S], F32, kind="ExternalInput").ap()
    G["w_in"] = nc.dram_tensor("w_in", [D, DIN], F32, kind="ExternalInput").ap()
    G["w_bsb"] = nc.dram_tensor("w_bsb", [NH * DH, D], F32, kind="ExternalInput").ap()
    G["w_bfx"] = nc.dram_tensor("w_bfx", [NH * DH, D], F32, kind="ExternalInput").ap()
    G["w_out"] = nc.dram_tensor("w_out", [D, D], F32, kind="ExternalInput").ap()
    G["w_fg"] = nc.dram_tensor("w_fg", [D, DFF], F32, kind="ExternalInput").ap()
    G["w_fu"] = nc.dram_tensor("w_fu", [D, DFF], F32, kind="ExternalInput").ap()
    G["w_fd"] = nc.dram_tensor("w_fd", [DFF, D], F32, kind="ExternalInput").ap()
    G["gains_d"] = nc.dram_tensor("gains", [128, 4 * KC], F32, kind="ExternalInput").ap()
    G["bf_d"] = nc.dram_tensor("bfg", [128, 128], F32, kind="ExternalInput").ap()
    G["cbf_d"] = nc.dram_tensor("cbf", [128, 7 * 128], BF16, kind="ExternalInput").ap()
    G["cf_d"] = nc.dram_tensor("cf32", [128, 2 * 128], F32, kind="ExternalInput").ap()
    G["outT"] = nc.dram_tensor("outT", [D, S], F32, kind="ExternalOutput").ap()
    G["OTs"] = nc.dram_tensor("OTs", [2 * NH, DH, S], BF16).ap()
    G["mixs"] = nc.dram_tensor("mixs", [D, S], F32).ap()
    G["hs"] = nc.dram_tensor("hs", [D, S], F32).ap()
    G["ffs"] = nc.dram_tensor("ffs", [D, S], F32).ap()
    G["dbg"] = None
    if debug is not None:
        G["dbg"] = {k: nc.dram_tensor("dbg_" + k, shp, dt, kind="ExternalOutput").ap()
                    for k, (shp, dt) in debug.items()}

    with ExitStack() as top:
        arena = top.enter_context(nc.sbuf_tensor("arena", [128, 212736], mybir.dt.uint8))
        kb = KB(nc, top)
        kb.al = Alloc(nc)
        banks = []
        for i in range(8):
            t = top.enter_context(nc.psum_tensor(f"ps{i}", [128, 512], F32))
            banks.append((t, Res(f"ps{i}")))
        _emit(nc, kb, banks, G, upto)
        with nc.Block() as block:
            @block.sync
            def _(e):
                kb.sp.replay(e)

            @block.tensor
            def _(e):
                kb.pe.replay(e)

            @block.scalar
            def _(e):
                kb.act.replay(e)

            @block.vector
            def _(e):
                kb.dve.replay(e)

            @block.gpsimd
            def _(e):
                kb.pool.replay(e)
    return nc


def _emit(nc, kb, banks, G, upto):
    PE, ACT, DVE, POOL, SP = kb.pe, kb.act, kb.dve, kb.pool, kb.sp
    al = kb.al
    xT, w_in, w_bsb, w_bfx, w_out = G["xT"], G["w_in"], G["w_bsb"], G["w_bfx"], G["w_out"]
    w_fg, w_fu, w_fd = G["w_fg"], G["w_fu"], G["w_fd"]
    gains_d, bf_d, cbf_d, cf_d, outT = G["gains_d"], G["bf_d"], G["cbf_d"], G["cf_d"], G["outT"]
    OTs, mixs, hs, ffs, dbg = G["OTs"], G["mixs"], G["hs"], G["ffs"], G["dbg"]
    final_events = []

    cbf = Slot(kb, "cbf", [128, 7 * 128], BF16, dma=True)
    cf = Slot(kb, "cf", [128, 2 * 128], F32, dma=True)
    gains = Slot(kb, "gains", [128, 4 * KC], F32, dma=True)
    bfg = Slot(kb, "bfg", [128, 128], F32, dma=True)
    kb.dma(SP, cbf.ds, cbf.t[:, :], cbf_d, writes=[cbf.r])
    kb.dma(SP, cf.ds, cf.t[:, :], cf_d, writes=[cf.r])
    kb.dma(SP, gains.ds, gains.t[:, :], gains_d, writes=[gains.r])
    kb.dma(SP, bfg.ds, bfg.t[:, :], bf_d, writes=[bfg.r])
    ones_bf = cbf.t[:, 0:128]
    negtri_bf = cbf.t[:, 128:256]
    negones_bf = cbf.t[:, 256:384]
    mstrict_bf = cbf.t[:, 384:512]
    mincl_bf = cbf.t[:, 512:640]
    zeros_bf = cbf.t[:, 640:768]
    ident_bf = cbf.t[:, 768:896]
    tri_f = cf.t[:, 0:128]
    ones_f = cf.t[:, 128:256]

    rstd2 = Slot(kb, "rstd2", [128, S], F32)
    rstdm = Slot(kb, "rstdm", [128, S], F32)
    m_u = al.mark()
    biasT = Slot(kb, "biasT", [128, NCH, NH, NB], F32)
    uT = Slot(kb, "uT", [128, KC, S], BF16)
    m_w = al.mark()
    wf32 = Slot(kb, "wf32", [128, KC, 8], F32, dma=True)
    wfb = Slot(kb, "wfb", [128, KC, 8], BF16)
    zf = Slot(kb, "zf", [128, 128], F32)
    tots = Slot(kb, "tots", [128, 128], F32)
    pre = Slot(kb, "pre", [128, 136], F32)
    cumL = Slot(kb, "cumL", [128, 128], F32)
    wst = [Slot(kb, f"wst{i}", [128, KC, 128], F32, dma=True) for i in range(3)]
    wbf = [[Slot(kb, f"wbf{s}_{j}", [128, KC, 128], BF16) for j in range(3)] for s in range(2)]

    def dump(name, src_ap, src_res, dst_ap=None):
        if dbg is None or name not in dbg:
            return
        ds = kb.dsem("dbg" + name)
        ev = kb.dma(SP, ds, dbg[name] if dst_ap is None else dst_ap, src_ap, reads=[src_res])
        final_events.append(ev)

    def finish():
        for ev in final_events:
            SP.wait(ev)

    def rstd_from_banks(bank_ids, dst_t, dst_r, col0, tmp):
        for i, bi in enumerate(bank_ids):
            bt, br = banks[bi]
            sl = slice(col0 + i * 512, col0 + (i + 1) * 512)
            kb.op(DVE, "tensor_scalar", dict(out=tmp.t[:, 0:512], in0=bt[:, :], scalar1=1.0 / D, scalar2=EPS,
                                             op0=ALU.mult, op1=ALU.add), writes=[br, tmp.r])
            kb.op(ACT, "activation", dict(out=tmp.t[:, 0:512], in_=tmp.t[:, 0:512], func=AF.Ln), writes=[tmp.r])
            kb.op(ACT, "activation", dict(out=dst_t[:, sl], in_=tmp.t[:, 0:512], func=AF.Exp, scale=-0.5),
                  reads=[tmp.r], writes=[dst_r])

    m0 = al.mark()
    xs = [Slot(kb, f"xs{i}", [128, S], F32, dma=True) for i in range(2)]
    sq = [Slot(kb, f"sq{i}", [128, S], BF16) for i in range(2)]
    rstd1 = Slot(kb, "rstd1", [128, S], F32)
    tmp = Slot(kb, "tmp1", [128, 512], F32)
    for c in range(KC):
        x = xs[c % 2]
        q = sq[c % 2]
        kb.dma(SP, x.ds, x.t[:, :], xT[c * 128:(c + 1) * 128, :], writes=[x.r])
        kb.op(ACT, "activation", dict(out=q.t[:, :], in_=x.t[:, :], func=AF.Square),
              reads=[x.r], writes=[q.r])
        for tc in range(NCH):
            bt, br = banks[tc]
            kb.op(PE, "matmul", dict(out=bt[:, :], lhsT=ones_bf, rhs=q.t[:, tc * 512:(tc + 1) * 512],
                                     start=(c == 0), stop=(c == KC - 1)),
                  reads=[q.r, cbf.r], writes=[br])
    rstd_from_banks([0, 1, 2, 3], rstd1.t, rstd1.r, 0, tmp)
    for c in range(KC):
        x = xs[c % 2]
        kb.dma(SP, x.ds, x.t[:, :], xT[c * 128:(c + 1) * 128, :], writes=[x.r])
        kb.op(DVE, "scalar_tensor_tensor", dict(out=uT.t[:, c, :], in0=x.t[:, :], scalar=gains.t[:, c:c + 1],
                                                in1=rstd1.t[:, :], op0=ALU.mult, op1=ALU.mult),
              reads=[x.r, gains.r, rstd1.r], writes=[uT.r])
    dump("rstd1", rstd1.t[:, :], rstd1.r)
    if dbg is not None and "uT" in dbg:
        dump("uT", uT.t[:, :, :], uT.r, dbg["uT"].rearrange("(c p) t -> p c t", p=128))
    al.release(m0)
    if upto <= 1:
        return finish()

    def stage2():
        m0 = al.mark()
        tots2, pre2, cumL2 = tots.t, pre.t, cumL.t
        kb.dma(SP, wf32.ds, wf32.t[:, :, :], w_in[:, COL_F:COL_F + 8].rearrange("(k p) c -> p k c", p=128),
               writes=[wf32.r])
        kb.op(DVE, "tensor_copy", dict(out=wfb.t[:, :, :], in_=wf32.t[:, :, :]), reads=[wf32.r], writes=[wfb.r])
        bF, rF = banks[4]
        for blk in range(NB):
            calls = [("matmul", dict(out=bF[:, blk * 8:(blk + 1) * 8], lhsT=uT.t[:, k, blk * 128:(blk + 1) * 128],
                                     rhs=wfb.t[:, k, :], start=(k == 0), stop=(k == KC - 1))) for k in range(KC)]
            kb.group(PE, calls, reads=[uT.r, wfb.r], writes=[rF])
        kb.op(DVE, "tensor_tensor", dict(out=zf.t[:, :], in0=bF[:, 0:128], in1=bfg.t[:, :], op=ALU.add),
              reads=[bfg.r], writes=[rF, zf.r])
        kb.op(ACT, "activation", dict(out=zf.t[:, :], in_=zf.t[:, :], func=AF.Exp, scale=-1.0), writes=[zf.r])
        kb.op(DVE, "tensor_scalar_add", dict(out=zf.t[:, :], in0=zf.t[:, :], scalar1=1.0), writes=[zf.r])
        kb.op(ACT, "activation", dict(out=zf.t[:, :], in_=zf.t[:, :], func=AF.Ln), writes=[zf.r])
        bW, rW = banks[5]
        bT, rT = banks[6]
        kb.op(PE, "matmul", dict(out=bW[:, 0:128], lhsT=tri_f, rhs=zf.t[:, :], start=True, stop=True),
              reads=[zf.r, cf.r], writes=[rW])
        kb.op(PE, "matmul", dict(out=bT[:, 0:128], lhsT=ones_f, rhs=zf.t[:, :], start=True, stop=True),
              reads=[zf.r, cf.r], writes=[rT])
        kb.op(DVE, "tensor_copy", dict(out=tots2[:, :], in_=bT[:, 0:128]), writes=[rT, tots.r])
        kb.op(DVE, "memset", dict(ap=pre2[:, 0:8], constant=0.0), writes=[pre.r])
        for blk in range(1, NB + 1):
            kb.op(DVE, "tensor_tensor", dict(out=pre2[:, blk * 8:(blk + 1) * 8], in0=pre2[:, (blk - 1) * 8:blk * 8],
                                             in1=tots2[:, (blk - 1) * 8:blk * 8], op=ALU.add),
                  reads=[tots.r], writes=[pre.r])
        kb.op(DVE, "tensor_tensor", dict(out=cumL2[:, :], in0=bW[:, 0:128], in1=pre2[:, 0:128], op=ALU.add),
              reads=[pre.r], writes=[rW, cumL.r])
        for c in range(NCH):
            for h in range(NH):
                pc = 4 * (c + 1) * 8 + h
                kb.op(DVE, "tensor_scalar", dict(out=biasT.t[:, c, h, :], in0=cumL.t[:, h:128:8],
                                                 scalar1=pre.t[:, pc:pc + 1], scalar2=None,
                                                 op0=ALU.subtract),
                      reads=[cumL.r, pre.r], writes=[biasT.r])
        dump("cumL", cumL2[:, :], cumL.r)
        if dbg is not None and "biasT" in dbg:
            dump("biasT", biasT.t[:, :, :, :].rearrange("p a b c -> p (a b c)"), biasT.r)
        al.release(m0)


    m0 = al.mark()
    qkv = []
    for par in range(2):
        q_ = Slot(kb, f"qT{par}", [128, S], BF16)
        k_ = Slot(kb, f"kT{par}", [128, S], BF16)
        v_ = Slot(kb, f"V{par}", [128, NB, 128], BF16)
        qkv.append((q_, k_, v_, None))
    VT = Slot(kb, "VT", [128, S], BF16)
    Eb = [Slot(kb, f"E{i}", [128, 512], F32) for i in range(2)]
    SPb = [Slot(kb, f"SP{i}", [128, 512], BF16) for i in range(4)]
    Wb = [Slot(kb, f"W{i}", [128, 512], BF16) for i in range(4)]
    Rb = [[Slot(kb, f"R{p}_{i}", [128, 512], BF16) for i in range(2)] for p in range(2)]
    rec = Slot(kb, "rec", [128, 512], F32)
    ocp = Slot(kb, "ocp", [128, 512], F32)
    Ost = [Slot(kb, f"Ost{i}", [128, 512], BF16, dma=True) for i in range(2)]
    OT_res = [[Res(f"OT{hh}_{c}") for c in range(NCH)] for hh in range(2 * NH)]
    cnt = {"w": 0, "p": 0, "z": 0, "o": 0}
    zb = [banks[2], banks[3], banks[4], banks[5]]
    pb = [banks[0], banks[1]]

    def load_head(hh, ws):
        br, h = divmod(hh, NH)
        for j in range(3):
            col = br * 3072 + j * 1024 + h * 128
            sl = wst[cnt["w"] % 3]
            cnt["w"] += 1
            kb.dma(SP, sl.ds, sl.t[:, :, :], w_in[:, col:col + 128].rearrange("(k p) c -> p k c", p=128),
                   writes=[sl.r])
            dst = wbf[ws][j]
            kb.op(DVE, "tensor_copy", dict(out=dst.t[:, :, :], in_=sl.t[:, :, :]), reads=[sl.r], writes=[dst.r])

    def proj_items(ws, par):
        wq, wk, wv = wbf[ws]
        qT, kT, V, V2 = qkv[par]
        items = []

        def qk_item(wsl, dst, tc, sc):
            st_ = {}

            def f0():
                bt, br_ = pb[cnt["p"] % 2]
                cnt["p"] += 1
                st_["b"] = (bt, br_)
                kb.group(PE, [("matmul", dict(out=bt[:, :], lhsT=wsl.t[:, k, :],
                                              rhs=uT.t[:, k, tc * 512:(tc + 1) * 512],
                                              start=(k == 0), stop=False)) for k in range(KC // 2)],
                         reads=[wsl.r, uT.r], writes=[br_])

            def f1():
                bt, br_ = st_["b"]
                kb.group(PE, [("matmul", dict(out=bt[:, :], lhsT=wsl.t[:, k, :],
                                              rhs=uT.t[:, k, tc * 512:(tc + 1) * 512],
                                              start=False, stop=(k == KC - 1))) for k in range(KC // 2, KC)],
                         reads=[wsl.r, uT.r], writes=[br_])
                kb.op(DVE, "tensor_scalar", dict(out=dst.t[:, tc * 512:(tc + 1) * 512], in0=bt[:, :],
                                                 scalar1=sc, scalar2=None, op0=ALU.mult),
                      writes=[br_, dst.r])
            return [f0, f1]

        def vt_items():
            its = []
            for tc in range(NCH):
                its += qk_item(wv, VT, tc, 1.0)

            def tr(g):
                def f():
                    bt, br_ = pb[cnt["p"] % 2]
                    cnt["p"] += 1
                    btb = bt[:, :].bitcast(BF16)
                    kb.group(PE, [("transpose", dict(out=btb[:, j * 128:(j + 1) * 128],
                                                     in_=VT.t[:, (8 * g + j) * 128:(8 * g + j + 1) * 128],
                                                     identity=ident_bf)) for j in range(8)],
                             reads=[VT.r, cbf.r], writes=[br_])
                    kb.op(DVE, "tensor_copy", dict(out=V.t[:, 8 * g:8 * g + 8, :].rearrange("p a b -> p (a b)"),
                                                   in_=btb[:, 0:1024]), writes=[br_, V.r])
                return f
            its += [tr(0), tr(1)]
            return its

        for tc in range(NCH):
            items += qk_item(wk, kT, tc, 1.0)
        items += vt_items()
        for tc in range(NCH):
            items += qk_item(wq, qT, tc, SCALE)
        return items

    def store_O(hh, c, ost):
        kb.dma(SP, ost.ds, OTs[hh, :, c * 512:(c + 1) * 512], ost.t[:, :], reads=[ost.r],
               writes=[OT_res[hh][c]])

    def attn_sb(hh, par, filler):
        qT, kT, V, V2 = qkv[par]
        seq = []
        for c in range(NCH):
            blocks = list(range(4 * c + 3, -1, -1))
            for j, b in enumerate(blocks):
                seq.append((c, j, b, len(blocks)))
        N = len(seq)
        stt = {}
        ste = {}

        def A_pe(g):
            c, j, b, n = seq[g]
            o = max(0, 128 * (b - 4 * c))
            diag = b >= 4 * c
            zi = cnt["z"]
            cnt["z"] += 1
            bz, rz = zb[zi % 4]
            e, sp = Eb[zi % 2], SPb[zi % 4]
            stt[g] = (o, diag, bz, rz, sp, Wb[zi % 4])
            ste[g] = e
            if j == 0:
                bO, rO = banks[6 + c % 2]
                kb.op(PE, "matmul", dict(out=bO[:, :], lhsT=zeros_bf, rhs=uT.t[:, 0, 0:512], start=True, stop=True),
                      reads=[cbf.r, uT.r], writes=[rO])
                for R_ in Rb[c % 2]:
                    kb.op(POOL, "memset", dict(ap=R_.t[:, :], constant=0.0), writes=[R_.r])
            kb.op(PE, "matmul", dict(out=bz[:, o:512], lhsT=kT.t[:, b * 128:(b + 1) * 128],
                                     rhs=qT.t[:, c * 512 + o:(c + 1) * 512], start=True, stop=True),
                  reads=[kT.r, qT.r], writes=[rz])

        def A_act(g):
            o, diag, bz, rz, sp, w = stt[g]
            e = ste.pop(g)
            kb.op(ACT, "activation", dict(out=e.t[:, o:512], in_=bz[:, o:512], func=AF.Exp),
                  writes=[rz, e.r])
            kb.op(ACT, "activation", dict(out=sp.t[:, o:512], in_=e.t[:, o:512], func=AF.Ln,
                                          bias=ones_f[:, 0:1]),
                  reads=[e.r, cf.r], writes=[sp.r])
            if diag:
                kb.op(POOL, "tensor_tensor", dict(out=sp.t[:, o:o + 128], in0=sp.t[:, o:o + 128],
                                                  in1=mstrict_bf, op=ALU.mult),
                      reads=[cbf.r], writes=[sp.r])

        def B1(g):
            c, j, b, n = seq[g]
            o, diag, bz, rz, sp, w = stt[g]
            first = (j == 0)
            R = Rb[c % 2][j % 2]
            calls = [("matmul", dict(out=bz[:, o:512], lhsT=negtri_bf, rhs=sp.t[:, o:512],
                                     start=False, stop=first, skip_group_check=True))]
            if not first:
                calls.append(("matmul", dict(out=bz[:, o:512], lhsT=negones_bf, rhs=R.t[:, o:512],
                                             start=False, stop=True, skip_group_check=True)))
            kb.group(PE, calls, reads=[sp.r, R.r, cbf.r], writes=[rz])
            kb.op(ACT, "activation", dict(out=w.t[:, o:512], in_=bz[:, o:512], func=AF.Exp),
                  writes=[rz, w.r])
            if diag:
                kb.op(POOL, "tensor_tensor", dict(out=w.t[:, o:o + 128], in0=w.t[:, o:o + 128],
                                                  in1=mstrict_bf, op=ALU.mult),
                      reads=[cbf.r], writes=[w.r])
            if j < n - 1:
                Rn = Rb[c % 2][(j + 1) % 2]
                kb.op(POOL, "tensor_tensor", dict(out=Rn.t[:, o:512], in0=R.t[:, o:512], in1=sp.t[:, o:512],
                                                  op=ALU.add), reads=[sp.r, R.r], writes=[Rn.r])

        def B2(g):
            c, j, b, n = seq[g]
            o, diag, bz, rz, sp, w = stt.pop(g)
            bO, rO = banks[6 + c % 2]
            kb.op(PE, "matmul", dict(out=bO[:, o:512], lhsT=V.t[:, b, :], rhs=w.t[:, o:512],
                                     start=False, stop=(j == n - 1), skip_group_check=True),
                  reads=[V.r, w.r], writes=[rO])
            if j == n - 1:
                ost = Ost[cnt["o"] % 2]
                cnt["o"] += 1
                kb.op(DVE, "tensor_copy", dict(out=ost.t[:, :], in_=bO[:, :]), writes=[rO, ost.r])
                store_O(hh, c, ost)

        for g in range(min(3, N)):
            A_pe(g)
        for g in range(min(2, N)):
            A_act(g)
        filler.burst(2)
        for g in range(N):
            if g + 2 < N:
                A_act(g + 2)
            B1(g)
            if g >= 1:
                B2(g - 1)
            if g + 3 < N:
                A_pe(g + 3)
            filler.tick()
        filler.burst(1)
        B2(N - 1)

    def attn_fox(hh, par, filler):
        qT, kT, V, V2 = qkv[par]
        h = hh % NH
        seq = []
        for c in range(NCH):
            for b in range(4 * c + 4):
                seq.append((c, b, 4 * c + 4))
        N = len(seq)
        stt = {}
        zb2 = [banks[2], banks[3]]

        def A_pe(g):
            c, b, n = seq[g]
            o = max(0, 128 * (b - 4 * c))
            zi = cnt["z"]
            cnt["z"] += 1
            bz, rz = zb2[zi % 2]
            stt[g] = (o, bz, rz, Wb[zi % 4])
            if b == 0:
                (bO, rO), (bD, rD) = banks[4 + 2 * (c % 2)], banks[5 + 2 * (c % 2)]
                kb.group(PE, [("matmul", dict(out=bO[:, :], lhsT=zeros_bf, rhs=uT.t[:, 0, 0:512], start=True, stop=True)),
                              ("matmul", dict(out=bD[:, :], lhsT=zeros_bf, rhs=uT.t[:, 0, 0:512], start=True, stop=True))],
                         reads=[cbf.r, uT.r], writes=[rO, rD])
            kb.op(PE, "matmul", dict(out=bz[:, o:512], lhsT=kT.t[:, b * 128:(b + 1) * 128],
                                     rhs=qT.t[:, c * 512 + o:(c + 1) * 512], start=True, stop=True),
                  reads=[kT.r, qT.r], writes=[rz])

        def A_act(g):
            c, b, n = seq[g]
            o, bz, rz, p = stt[g]
            kb.op(ACT, "activation", dict(out=p.t[:, o:512], in_=bz[:, o:512], func=AF.Exp,
                                          bias=biasT.t[:, c, h, b:b + 1]),
                  reads=[biasT.r], writes=[rz, p.r])
            if b >= 4 * c:
                kb.op(POOL, "tensor_tensor", dict(out=p.t[:, o:o + 128], in0=p.t[:, o:o + 128],
                                                  in1=mincl_bf, op=ALU.mult),
                      reads=[cbf.r], writes=[p.r])

        def B(g):
            c, b, n = seq[g]
            o, bz, rz, p = stt.pop(g)
            (bO, rO), (bD, rD) = banks[4 + 2 * (c % 2)], banks[5 + 2 * (c % 2)]
            last = (b == n - 1)
            kb.group(PE, [("matmul", dict(out=bO[:, o:512], lhsT=V.t[:, b, :], rhs=p.t[:, o:512],
                                          start=False, stop=last, skip_group_check=True)),
                          ("matmul", dict(out=bD[:, o:512], lhsT=ones_bf, rhs=p.t[:, o:512],
                                          start=False, stop=last, skip_group_check=True))],
                     reads=[V.r, p.r, cbf.r], writes=[rO, rD])
            if last:
                ost = Ost[cnt["o"] % 2]
                cnt["o"] += 1
                kb.op(ACT, "activation", dict(out=rec.t[:, :], in_=bD[:, :], func=AF.Copy), writes=[rD, rec.r])
                kb.op(DVE, "tensor_copy", dict(out=ocp.t[:, :], in_=bO[:, :]), writes=[rO, ocp.r])
                kb.op(DVE, "reciprocal", dict(out=rec.t[:, :], in_=rec.t[:, :]), writes=[rec.r])
                kb.op(DVE, "tensor_tensor", dict(out=ost.t[:, :], in0=ocp.t[:, :], in1=rec.t[:, :], op=ALU.mult),
                      reads=[rec.r, ocp.r], writes=[ost.r])
                store_O(hh, c, ost)

        for g in range(min(2, N)):
            A_pe(g)
        A_act(0)
        filler.burst(1)
        for g in range(N):
            if g + 1 < N:
                A_act(g + 1)
            B(g)
            if g + 2 < N:
                A_pe(g + 2)
            filler.tick()

    heads = list(range(2 * NH)) if upto >= 4 else G.get("dbg_heads", [0, 8])
    load_head(heads[0], 0)
    if len(heads) > 1:
        load_head(heads[1], 1)
    for it in proj_items(0, 0):
        it()
    stage2()
    for i, hh in enumerate(heads):
        if i + 2 < len(heads):
            load_head(heads[i + 2], i % 2)
        filler = Filler(proj_items((i + 1) % 2, (i + 1) % 2) if i + 1 < len(heads) else [], 40)
        if hh < NH:
            attn_sb(hh, i % 2, filler)
        else:
            attn_fox(hh, i % 2, filler)
        filler.flush()
    if dbg is not None and "OT" in dbg:
        ds = kb.dsem("dbgOT")
        allr = [OT_res[hh][c] for hh in heads for c in range(NCH)]
        for hh in heads:
            ev = kb.dma(SP, ds, dbg["OT"][hh], OTs[hh], reads=allr)
        final_events.append(ev)
    al.release(m_w)
    if upto <= 3:
        return finish()

    TH = 1024
    NQ = S // 512
    mix_res = [[Res(f"mix{c}_{q}") for q in range(NQ)] for c in range(KC)]
    h_res = [[Res(f"h{c}_{q}") for q in range(NQ)] for c in range(KC)]
    ff_res = [[Res(f"ff{c}_{q}") for q in range(NQ)] for c in range(KC)]
    nofill = Filler([], 1)

    def stat_mm(bi, sq_ap, sq_res, c):
        sb_t, sb_r = banks[bi]
        kb.op(PE, "matmul", dict(out=sb_t[:, :], lhsT=ones_bf, rhs=sq_ap, start=(c == 0), stop=(c == KC - 1)),
              reads=[sq_res, cbf.r], writes=[sb_r])

    def wview(w, c0, kc):
        return w[:, c0:c0 + 128].rearrange("(k p) c -> p k c", p=128)

    def hpass_items(hf, sbank0):
        xh = [Slot(kb, f"xh{i}", [128, 512], F32, dma=True) for i in range(2)]
        mh = [Slot(kb, f"mh{i}", [128, 512], F32, dma=True) for i in range(2)]
        hst = [Slot(kb, f"hst{i}", [128, 512], F32, dma=True) for i in range(2)]
        sqh = [Slot(kb, f"sqh{i}", [128, 512], BF16) for i in range(3)]
        pend = []
        items = []

        def mk(c, tc, n):
            def f():
                q = 2 * hf + tc
                tsl = slice(q * 512, (q + 1) * 512)
                xx, mm, hh_, qq = xh[n % 2], mh[n % 2], hst[n % 2], sqh[n % 3]
                kb.dma(SP, xx.ds, xx.t[:, :], xT[c * 128:(c + 1) * 128, tsl], writes=[xx.r])
                kb.dma(SP, mm.ds, mm.t[:, :], mixs[c * 128:(c + 1) * 128, tsl], reads=[mix_res[c][q]], writes=[mm.r])
                kb.op(DVE, "scalar_tensor_tensor", dict(out=hh_.t[:, :], in0=mm.t[:, :], scalar=gains.t[:, KC + c:KC + c + 1],
                                                        in1=rstdm.t[:, tsl], op0=ALU.mult, op1=ALU.mult),
                      reads=[mm.r, gains.r, rstdm.r], writes=[hh_.r])
                kb.op(POOL, "tensor_tensor", dict(out=hh_.t[:, :], in0=hh_.t[:, :], in1=xx.t[:, :], op=ALU.add),
                      reads=[xx.r], writes=[hh_.r])
                kb.dma(POOL, hh_.ds, hs[c * 128:(c + 1) * 128, tsl], hh_.t[:, :], reads=[hh_.r], writes=[h_res[c][q]])
                kb.op(ACT, "activation", dict(out=qq.t[:, :], in_=hh_.t[:, :], func=AF.Square),
                      reads=[hh_.r], writes=[qq.r])
                if pend:
                    stat_mm(*pend.pop(0))
                pend.append((sbank0 + tc, qq.t[:, :], qq.r, c))
            return f

        n = 0
        for c in range(KC):
            for tc in range(2):
                items.append(mk(c, tc, n))
                n += 1

        def fin():
            while pend:
                stat_mm(*pend.pop(0))
        return items, fin

    def u2_items(hf, u2, nbuf=2):
        hl = [Slot(kb, f"hl{i}", [128, 512], F32, dma=True) for i in range(nbuf)]
        items = []

        def mk(c, tc, n):
            def f():
                q = 2 * hf + tc
                tsl = slice(q * 512, (q + 1) * 512)
                x = hl[n % nbuf]
                kb.dma(SP, x.ds, x.t[:, :], hs[c * 128:(c + 1) * 128, tsl], reads=[h_res[c][q]], writes=[x.r])
                kb.op(DVE, "scalar_tensor_tensor", dict(out=u2.t[:, c, tc * 512:(tc + 1) * 512], in0=x.t[:, :],
                                                        scalar=gains.t[:, 2 * KC + c:2 * KC + c + 1],
                                                        in1=rstd2.t[:, tsl], op0=ALU.mult, op1=ALU.mult),
                      reads=[x.r, gains.r, rstd2.r], writes=[u2.r])
            return f

        n = 0
        for c in range(KC):
            for tc in range(2):
                items.append(mk(c, tc, n))
                n += 1
        return items

    def final_items(hf, rstd3, tcs=(0, 1), nbuf=2):
        hl = [Slot(kb, f"hl2_{i}", [128, 512], F32, dma=True) for i in range(nbuf)]
        fl = [Slot(kb, f"fl{i}", [128, 512], F32, dma=True) for i in range(nbuf)]
        ot = [Slot(kb, f"ot{i}", [128, 512], F32, dma=True) for i in range(nbuf)]
        items = []

        def mk(c, tc, n):
            def f():
                q = 2 * hf + tc
                tsl = slice(q * 512, (q + 1) * 512)
                hh_, ff_, oo = hl[n % nbuf], fl[n % nbuf], ot[n % nbuf]
                kb.dma(SP, hh_.ds, hh_.t[:, :], hs[c * 128:(c + 1) * 128, tsl], reads=[h_res[c][q]], writes=[hh_.r])
                kb.dma(SP, ff_.ds, ff_.t[:, :], ffs[c * 128:(c + 1) * 128, tsl], reads=[ff_res[c][q]], writes=[ff_.r])
                kb.op(DVE, "scalar_tensor_tensor", dict(out=oo.t[:, :], in0=ff_.t[:, :],
                                                        scalar=gains.t[:, 3 * KC + c:3 * KC + c + 1],
                                                        in1=rstd3.t[:, tc * 512:(tc + 1) * 512], op0=ALU.mult, op1=ALU.mult),
                      reads=[ff_.r, gains.r, rstd3.r], writes=[oo.r])
                kb.op(POOL, "tensor_tensor", dict(out=oo.t[:, :], in0=oo.t[:, :], in1=hh_.t[:, :], op=ALU.add),
                      reads=[hh_.r], writes=[oo.r])
                ev = kb.dma(POOL, oo.ds, outT[c * 128:(c + 1) * 128, tsl], oo.t[:, :], reads=[oo.r])
                final_events.append(ev)
            return f

        n = 0
        for c in range(KC):
            for tc in tcs:
                items.append(mk(c, tc, n))
                n += 1
        return items

    def P5(hf, merged):
        t0 = hf * TH
        OTh = Slot(kb, "OTh", [128, 2 * NH, TH], BF16, dma=True)
        wst = [Slot(kb, f"wstA{i}", [128, KC, 128], F32, dma=True) for i in range(2)]
        wg = [[Slot(kb, f"wgA{i}_{j}", [128, KC, 128], BF16) for j in range(3)] for i in range(2)]
        s1 = [Slot(kb, f"s1_{i}", [128, 512], F32) for i in range(2)]
        s2 = [Slot(kb, f"s2_{i}", [128, 512], F32) for i in range(2)]
        m1 = Slot(kb, "m1", [128, 512], F32)
        m2 = Slot(kb, "m2", [128, 512], F32)
        kb.dma(SP, OTh.ds, OTh.t[:, :, :], OTs[:, :, t0:t0 + TH].rearrange("h p t -> p h t"),
               reads=[OT_res[hh][2 * hf + i] for hh in range(2 * NH) for i in range(2)], writes=[OTh.r])
        cntA = {"w": 0, "b": 0}

        def loadA(c):
            fills = [[(w_in, COL_GSB + c * 128, KC, 0)], [(w_in, COL_GFX + c * 128, KC, 0)],
                     [(w_bsb, c * 128, NH, 0), (w_bfx, c * 128, NH, NH)]]
            for j, fill in enumerate(fills):
                sl = wst[cntA["w"] % 2]
                cntA["w"] += 1
                kb.dma_multi(SP, sl.ds, [(sl.t[:, k0:k0 + kc, :], wview(w, c0, kc)) for (w, c0, kc, k0) in fill],
                             writes=[sl.r])
                dst = wg[c % 2][j]
                kb.op(DVE, "tensor_copy", dict(out=dst.t[:, :, :], in_=sl.t[:, :, :]), reads=[sl.r], writes=[dst.r])

        loadA(0)
        for c in range(KC):
            if c + 1 < KC:
                loadA(c + 1)
            wgs, wgf, wb = wg[c % 2]
            for tc in range(2):
                bset = [banks[4 * (cntA["b"] % 2) + i] for i in range(4)]
                cntA["b"] += 1
                tsl = slice(t0 + tc * 512, t0 + (tc + 1) * 512)
                hsl = slice(tc * 512, (tc + 1) * 512)
                for (bt, br_), wsl in ((bset[0], wgs), (bset[1], wgf)):
                    kb.group(PE, [("matmul", dict(out=bt[:, :], lhsT=wsl.t[:, k, :], rhs=uT.t[:, k, tsl],
                                                  start=(k == 0), stop=(k == KC - 1))) for k in range(KC)],
                             reads=[wsl.r, uT.r], writes=[br_])
                for (bt, br_), k0 in ((bset[2], 0), (bset[3], NH)):
                    kb.group(PE, [("matmul", dict(out=bt[:, :], lhsT=wb.t[:, k0 + k, :], rhs=OTh.t[:, k0 + k, hsl],
                                                  start=(k == 0), stop=(k == NH - 1))) for k in range(NH)],
                             reads=[wb.r, OTh.r], writes=[br_])
                a1, a2 = s1[tc], s2[tc]
                kb.op(ACT, "activation", dict(out=a1.t[:, :], in_=bset[0][0][:, :], func=AF.Sigmoid),
                      writes=[bset[0][1], a1.r])
                kb.op(ACT, "activation", dict(out=a2.t[:, :], in_=bset[1][0][:, :], func=AF.Sigmoid),
                      writes=[bset[1][1], a2.r])
                kb.op(DVE, "tensor_tensor", dict(out=m1.t[:, :], in0=bset[2][0][:, :], in1=a1.t[:, :], op=ALU.mult),
                      reads=[a1.r], writes=[bset[2][1], m1.r])
                kb.op(DVE, "tensor_tensor", dict(out=m2.t[:, :], in0=bset[3][0][:, :], in1=a2.t[:, :], op=ALU.mult),
                      reads=[a2.r], writes=[bset[3][1], m2.r])
                kb.op(POOL, "tensor_tensor", dict(out=merged.t[:, c, hsl], in0=m1.t[:, :], in1=m2.t[:, :], op=ALU.add),
                      reads=[m1.r, m2.r], writes=[merged.r])

    def WOUT(hf, merged, filler):
        t0 = hf * TH
        wst = [Slot(kb, f"wstB{i}", [128, KC, 128], F32, dma=True) for i in range(3)]
        wo = [Slot(kb, f"wo{i}", [128, KC, 128], BF16) for i in range(2)]
        mixst = [Slot(kb, f"mixst{i}", [128, 512], F32, dma=True) for i in range(2)]
        sqm = [Slot(kb, f"sqm{i}", [128, 512], BF16) for i in range(3)]
        tmpB = Slot(kb, "tmpB", [128, 512], F32)
        cntB = {"b": 0, "m": 0}
        pend = []

        def loadB(c):
            sl = wst[c % 3]
            kb.dma(SP, sl.ds, sl.t[:, :, :], wview(w_out, c * 128, KC), writes=[sl.r])
            kb.op(DVE, "tensor_copy", dict(out=wo[c % 2].t[:, :, :], in_=sl.t[:, :, :]), reads=[sl.r], writes=[wo[c % 2].r])

        loadB(0)
        for c in range(KC):
            if c + 1 < KC:
                loadB(c + 1)
            for tc in range(2):
                bt, br_ = banks[cntB["b"] % 4]
                cntB["b"] += 1
                hsl = slice(tc * 512, (tc + 1) * 512)
                kb.group(PE, [("matmul", dict(out=bt[:, :], lhsT=wo[c % 2].t[:, k, :], rhs=merged.t[:, k, hsl],
                                              start=(k == 0), stop=(k == KC - 1))) for k in range(KC)],
                         reads=[wo[c % 2].r, merged.r], writes=[br_])
                ms = mixst[cntB["m"] % 2]
                sm = sqm[cntB["m"] % 3]
                cntB["m"] += 1
                kb.op(ACT, "activation", dict(out=ms.t[:, :], in_=bt[:, :], func=AF.Copy), writes=[br_, ms.r])
                kb.dma(ACT, ms.ds, mixs[c * 128:(c + 1) * 128, t0 + tc * 512:t0 + (tc + 1) * 512], ms.t[:, :],
                       reads=[ms.r], writes=[mix_res[c][2 * hf + tc]])
                kb.op(ACT, "activation", dict(out=sm.t[:, :], in_=bt[:, :], func=AF.Square), writes=[br_, sm.r])
                pend.append((6 + tc, sm.t[:, :], sm.r, c))
                if len(pend) > 1:
                    stat_mm(*pend.pop(0))
            filler.tick()
        while pend:
            stat_mm(*pend.pop(0))
        rstd_from_banks([6, 7], rstdm.t, rstdm.r, t0, tmpB)
        return tmpB

    mA = al.mark()
    merged = Slot(kb, "merged", [128, KC, TH], BF16)
    mB = al.mark()
    P5(0, merged)
    al.release(mB)
    WOUT(0, merged, nofill)
    al.release(mB)
    P5(1, merged)
    al.release(mB)
    hp_items, hp_fin = hpass_items(0, 4)
    fill = Filler(hp_items, KC)
    tmpB = WOUT(1, merged, fill)
    fill.flush()
    hp_fin()
    rstd_from_banks([4, 5], rstd2.t, rstd2.r, 0, tmpB)
    if dbg is not None and "rstdm" in dbg:
        dump("rstdm", rstdm.t[:, 0:1024], rstdm.r)
    al.release(mA)

    al.release(m_u)
    u2 = Slot(kb, "u2", [128, KC, TH], BF16)
    aT = Slot(kb, "aT", [128, FC, TH], BF16)
    rstd3_ = Slot(kb, "rstd3", [128, TH], F32)
    rstd3 = [rstd3_, rstd3_]
    tmpD = Slot(kb, "tmpD", [128, 512], F32)
    mB = al.mark()

    def GATEUP(hf, filler):
        wst = [Slot(kb, f"wstC{i}", [128, KC, 128], F32, dma=True) for i in range(3)]
        wgu = [[Slot(kb, f"wgu{i}_{j}", [128, KC, 128], BF16) for j in range(2)] for i in range(2)]
        sg = [Slot(kb, f"sg{i}", [128, 512], F32) for i in range(2)]
        cntC = {"w": 0, "b": 0}

        def loadC(f):
            for j, w in enumerate((w_fg, w_fu)):
                sl = wst[cntC["w"] % 3]
                cntC["w"] += 1
                kb.dma(SP, sl.ds, sl.t[:, :, :], wview(w, f * 128, KC), writes=[sl.r])
                dst = wgu[f % 2][j]
                kb.op(DVE, "tensor_copy", dict(out=dst.t[:, :, :], in_=sl.t[:, :, :]), reads=[sl.r], writes=[dst.r])

        loadC(0)
        for f in range(FC):
            if f + 1 < FC:
                loadC(f + 1)
            wgt, wup = wgu[f % 2]
            for tc in range(2):
                pbi = cntC["b"] % 3
                cntC["b"] += 1
                (bg, rg), (bu, ru) = banks[2 * pbi], banks[2 * pbi + 1]
                hsl = slice(tc * 512, (tc + 1) * 512)
                kb.group(PE, [("matmul", dict(out=bg[:, :], lhsT=wgt.t[:, k, :], rhs=u2.t[:, k, hsl],
                                              start=(k == 0), stop=(k == KC - 1))) for k in range(KC)],
                         reads=[wgt.r, u2.r], writes=[rg])
                kb.group(PE, [("matmul", dict(out=bu[:, :], lhsT=wup.t[:, k, :], rhs=u2.t[:, k, hsl],
                                              start=(k == 0), stop=(k == KC - 1))) for k in range(KC)],
                         reads=[wup.r, u2.r], writes=[ru])
                sgt = sg[tc]
                kb.op(ACT, "activation", dict(out=sgt.t[:, :], in_=bg[:, :], func=AF.Silu), writes=[rg, sgt.r])
                kb.op(DVE, "tensor_tensor", dict(out=aT.t[:, f, hsl], in0=bu[:, :], in1=sgt.t[:, :], op=ALU.mult),
                      reads=[sgt.r], writes=[ru, aT.r])
            filler.tick()

    def DOWN(hf, filler, tc_outer=False):
        t0 = hf * TH
        wst = [Slot(kb, f"wstD{i}", [128, KC, 128], F32, dma=True) for i in range(3)]
        wd = [Slot(kb, f"wd{i}", [128, FC, 128], BF16) for i in range(2)]
        ffst = [Slot(kb, f"ffst{i}", [128, 512], F32, dma=True) for i in range(2)]
        sqf = [Slot(kb, f"sqf{i}", [128, 512], BF16) for i in range(3)]
        cntD = {"w": 0, "b": 0, "m": 0, "l": 0}
        pend = []
        wfd_v = w_fd.rearrange("(f p) c -> p f c", p=128)

        def loadD(c):
            for f0 in (0, 16, 32):
                nf = min(16, FC - f0)
                sl = wst[cntD["w"] % 3]
                cntD["w"] += 1
                kb.dma(SP, sl.ds, sl.t[:, 0:nf, :], wfd_v[:, f0:f0 + nf, c * 128:(c + 1) * 128], writes=[sl.r])
                wdl = wd[cntD["l"] % 2]
                kb.op(DVE, "tensor_copy", dict(out=wdl.t[:, f0:f0 + nf, :], in_=sl.t[:, 0:nf, :]),
                      reads=[sl.r], writes=[wdl.r])
            cntD["l"] += 1

        steps = ([(c, (0, 1)) for c in range(KC)] if not tc_outer
                 else [(c, (0,)) for c in range(KC)] + [(c, (1,)) for c in range(KC)])
        loadD(steps[0][0])
        for si, (c, tcs) in enumerate(steps):
            if si + 1 < len(steps):
                loadD(steps[si + 1][0])
            wdc = wd[si % 2]
            for tc in tcs:
                bt, br_ = banks[cntD["b"] % 4]
                cntD["b"] += 1
                hsl = slice(tc * 512, (tc + 1) * 512)
                kb.group(PE, [("matmul", dict(out=bt[:, :], lhsT=wdc.t[:, f, :], rhs=aT.t[:, f, hsl],
                                              start=(f == 0), stop=(f == FC - 1))) for f in range(FC)],
                         reads=[wdc.r, aT.r], writes=[br_])
                fs = ffst[cntD["m"] % 2]
                sf = sqf[cntD["m"] % 3]
                cntD["m"] += 1
                kb.op(ACT, "activation", dict(out=fs.t[:, :], in_=bt[:, :], func=AF.Copy), writes=[br_, fs.r])
                kb.dma(ACT, fs.ds, ffs[c * 128:(c + 1) * 128, t0 + tc * 512:t0 + (tc + 1) * 512], fs.t[:, :],
                       reads=[fs.r], writes=[ff_res[c][2 * hf + tc]])
                kb.op(ACT, "activation", dict(out=sf.t[:, :], in_=bt[:, :], func=AF.Square), writes=[br_, sf.r])
                pend.append((6 + tc, sf.t[:, :], sf.r, c))
                if len(pend) > 1:
                    stat_mm(*pend.pop(0))
            filler.tick()
            if tc_outer and si == KC - 1:
                while pend:
                    stat_mm(*pend.pop(0))
                filler.flush()
                rstd_from_banks([6], rstd3[hf].t, rstd3[hf].r, 0, tmpD)
                al.kill(u2.r)
                sv = al.top
                al.top = u2.off
                filler = Filler(final_items(hf, rstd3[hf], tcs=(0,)), KC)
                fin_tail = final_items(hf, rstd3[hf], tcs=(1,), nbuf=3)
                al.top = sv
        while pend:
            stat_mm(*pend.pop(0))
        filler.flush()
        if tc_outer:
            rstd_from_banks([7], rstd3[hf].t, rstd3[hf].r, 512, tmpD)
            for it in fin_tail:
                it()
        else:
            rstd_from_banks([6, 7], rstd3[hf].t, rstd3[hf].r, 0, tmpD)

    u2i = u2_items(0, u2, nbuf=6)
    for it in u2i:
        it()
    al.release(mB)
    hp_items, hp_fin = hpass_items(1, 6)
    fill = Filler(hp_items, FC)
    GATEUP(0, fill)
    fill.flush()
    hp_fin()
    rstd_from_banks([6, 7], rstd2.t, rstd2.r, TH, tmpD)
    al.release(mB)
    fill = Filler(u2_items(1, u2), KC)
    DOWN(0, fill)
    fill.flush()
    al.release(mB)
    fill = Filler(final_items(0, rstd3[0]), FC)
    GATEUP(1, fill)
    fill.flush()
    al.release(mB)
    mD = al.mark()
    fbufs_dummy = None
    DOWN(1, nofill, tc_outer=True)

    finish()


def _consts():
    j = np.arange(128)[:, None]
    s = np.arange(128)[None, :]
    cb = np.concatenate([
        np.ones((128, 128)),
        -(j >= s).astype(np.float64),
        -np.ones((128, 128)),
        (j < s).astype(np.float64),
        (j <= s).astype(np.float64),
        np.zeros((128, 128)),
        np.eye(128),
    ], axis=1).astype(ml_dtypes.bfloat16)
    cfm = np.concatenate([(j <= s).astype(np.float32), np.ones((128, 128), np.float32)], axis=1)
    return np.ascontiguousarray(cb), np.ascontiguousarray(cfm)


def make_in_maps(inputs, cores):
    cb, cfm = _consts()
    g = np.stack([np.asarray(inputs[k], np.float32)[0] for k in
                  ("norm_mix_pre", "norm_mix_post", "norm_ffn_pre", "norm_ffn_post")])
    gains = np.ascontiguousarray(g.reshape(4, KC, 128).transpose(2, 0, 1).reshape(128, 4 * KC))
    bfg = np.ascontiguousarray(np.broadcast_to(
        np.tile(np.asarray(inputs["b_forget"], np.float32)[0], NB)[None, :], (128, 128)))
    shared = {
        "w_in": np.ascontiguousarray(np.asarray(inputs["w_in"], np.float32)[0]),
        "w_bsb": np.ascontiguousarray(np.asarray(inputs["w_branch_sb"], np.float32)[0]),
        "w_bfx": np.ascontiguousarray(np.asarray(inputs["w_branch_fox"], np.float32)[0]),
        "w_out": np.ascontiguousarray(np.asarray(inputs["w_out"], np.float32)[0]),
        "w_fg": np.ascontiguousarray(np.asarray(inputs["w_ffn_gate"], np.float32)[0]),
        "w_fu": np.ascontiguousarray(np.asarray(inputs["w_ffn_up"], np.float32)[0]),
        "w_fd": np.ascontiguousarray(np.asarray(inputs["w_ffn_down"], np.float32)[0]),
        "gains": gains, "bfg": bfg, "cbf": cb, "cf32": cfm,
    }
    x = np.asarray(inputs["x"], np.float32)
    maps = []
    for b in cores:
        m = dict(shared)
        m["xT"] = np.ascontiguousarray(x[b].T)
        maps.append(m)
    return maps


def kernel(**inputs):
    nc = build_nc()
    in_maps = make_in_maps(inputs, list(range(8)))
    res = run_bass_kernel_spmd(nc, in_maps, core_ids=list(range(8)))
    out = np.stack([np.ascontiguousarray(np.asarray(r["outT"], np.float32).T) for r in res.results])
    return out
```

```python
import numpy as np
import ml_dtypes
from contextlib import ExitStack
import concourse.bass as bass
import concourse.mybir as mybir
from concourse.bass_utils import run_bass_kernel_spmd

F32 = mybir.dt.float32
BF16 = mybir.dt.bfloat16
AF = mybir.ActivationFunctionType
ALU = mybir.AluOpType

D = 2048
S = 2048
NH = 8
DH = 128
DFF = 5632
DIN = 10248
KC = D // 128
FC = DFF // 128
NB = S // 128
NCH = S // 512
EPS = 1e-6
SCALE = DH ** -0.5
COL_F = 6144
COL_GSB = 6152
COL_GFX = 8200
EPOCH = 30000


class Res:
    __slots__ = ("name", "w", "rs")

    def __init__(self, name):
        self.name = name
        self.w = None
        self.rs = {}


class DSem:
    __slots__ = ("sem", "count")

    def __init__(self, sem):
        self.sem = sem
        self.count = 0


class Eng:
    def __init__(self, kb, name):
        self.kb = kb
        self.name = name
        self.sem = kb.new_sem("e_" + name)
        self.own_sems = {self.sem}
        self.count = 0
        self.waited = {}
        self.prog = []

    def wait(self, ev):
        sem, val = ev
        if self.waited.get(sem, 0) >= val:
            return
        self.prog.append(("w", sem, val))
        self.waited[sem] = val

    def tick(self):
        if self.count >= EPOCH:
            self.sem = self.kb.new_sem("e_" + self.name)
            self.own_sems.add(self.sem)
            self.count = 0
        self.count += 1
        return (self.sem, self.count)

    def replay(self, eng):
        for it in self.prog:
            if it[0] == "w":
                eng.wait_ge(it[1], it[2])
            else:
                _, calls, sem, inc = it
                ins = None
                for meth, kw in calls:
                    ins = getattr(eng, meth)(**kw)
                    if inc == 16:
                        ins.then_inc(sem, 16)
                if inc == 1:
                    ins.then_inc(sem, 1)


class KB:
    def __init__(self, nc, stack):
        self.nc = nc
        self.stack = stack
        self.nsem = 0
        self.pe = Eng(self, "pe")
        self.act = Eng(self, "act")
        self.dve = Eng(self, "dve")
        self.pool = Eng(self, "pool")
        self.sp = Eng(self, "sp")

    def new_sem(self, name):
        self.nsem += 1
        return self.stack.enter_context(self.nc.semaphore(f"{name}_{self.nsem}"))

    def dsem(self, name):
        return DSem(self.new_sem("d_" + name))

    @staticmethod
    def _deps(reads, writes):
        evs = []
        for r in reads:
            if r.w is not None:
                evs.append(r.w)
        for r in writes:
            if r.w is not None:
                evs.append(r.w)
            for s, v in r.rs.items():
                evs.append((s, v))
        return evs

    @staticmethod
    def _commit(ev, reads, writes):
        s, v = ev
        for r in reads:
            if r.rs.get(s, 0) < v:
                r.rs[s] = v
        for r in writes:
            r.w = ev
            r.rs = {}

    def op(self, E, meth, kw, reads=(), writes=()):
        return self.group(E, [(meth, kw)], reads, writes)

    def group(self, E, calls, reads=(), writes=()):
        for ev in self._deps(reads, writes):
            if E is self.pe and ev[0] in E.own_sems:
                continue
            E.wait(ev)
        ev = E.tick()
        E.prog.append(("o", list(calls), ev[0], 1))
        self._commit(ev, reads, writes)
        return ev

    def dma(self, Q, ds, out, in_, reads=(), writes=()):
        return self.dma_multi(Q, ds, [(out, in_)], reads, writes)

    def dma_multi(self, Q, ds, pairs, reads=(), writes=()):
        for ev in self._deps(reads, writes):
            Q.wait(ev)
        Q.prog.append(("o", [("dma_start", dict(out=o, in_=i)) for o, i in pairs], ds.sem, 16))
        ds.count += 16 * len(pairs)
        ev = (ds.sem, ds.count)
        self._commit(ev, reads, writes)
        return ev


class Alloc:
    BASE = 16512
    LIMIT = 16512 + 212736

    def __init__(self, nc):
        self.nc = nc
        self.top = self.BASE
        self.n = 0
        self.live = []
        self.dead = []

    def mark(self):
        return self.top

    def release(self, m):
        keep = []
        for it in self.live:
            (self.dead if it[0] >= m else keep).append(it)
        self.live = keep
        self.top = m

    def kill(self, res):
        keep = []
        for it in self.live:
            (self.dead if it[2] is res else keep).append(it)
        self.live = keep

    def register(self, off, end, res):
        for (o, e, r) in self.dead:
            if o < end and off < e:
                evs = list(r.rs.items())
                if r.w is not None:
                    evs.append(r.w)
                for s_, v in evs:
                    if res.rs.get(s_, 0) < v:
                        res.rs[s_] = v
        self.live.append((off, end, res))

    def alloc(self, name, shape, dtype):
        nbytes = int(np.prod(shape[1:])) * (2 if dtype == BF16 else 4)
        off = (self.top + 63) // 64 * 64
        assert off + nbytes <= self.LIMIT, f"SBUF overflow allocating {name}: {off + nbytes - self.LIMIT} over"
        self.top = off + nbytes
        self.last_off = off
        self.last_end = off + nbytes
        self.n += 1
        return self.nc.alloc_sbuf_tensor_at(f"{name}_{self.n}", shape, dtype, offset=off)


class Filler:
    def __init__(self, items, nticks):
        self.items = list(items)
        self.left = max(1, nticks)
        self.credit = 0.0

    def tick(self):
        if self.items:
            self.credit += len(self.items) / max(1, self.left)
        self.left -= 1
        while self.credit >= 1.0 - 1e-9 and self.items:
            self.items.pop(0)()
            self.credit -= 1.0

    def burst(self, k):
        for _ in range(k):
            if self.items:
                self.items.pop(0)()

    def flush(self):
        while self.items:
            self.items.pop(0)()


class Slot:
    def __init__(self, kb, name, shape, dtype, dma=False):
        self.t = kb.al.alloc(name, shape, dtype)
        self.r = Res(name)
        kb.al.register(kb.al.last_off, kb.al.last_end, self.r)
        self.ds = kb.dsem(name) if dma else None
        self.kb = kb
        self.name = name
        self.off = kb.al.last_off

    def view(self, shape, dtype):
        kb = self.kb
        kb.al.n += 1
        return kb.nc.alloc_sbuf_tensor_at(f"{self.name}_v{kb.al.n}", shape, dtype, offset=self.off)


def build_nc(debug=None, upto=99):
    nc = bass.Bass("TRN2", target_bir_lowering=False)
    G = {}
    G["xT"] = nc.dram_tensor("xT", [D, S], F32, kind="ExternalInput").ap()
    G["w_in"] = nc.dram_tensor("w_in", [D, DIN], F32, kind="ExternalInput").ap()
    G["w_bsb"] = nc.dram_tensor("w_bsb", [NH * DH, D], F32, kind="ExternalInput").ap()
    G["w_bfx"] = nc.dram_tensor("w_bfx", [NH * DH, D], F32, kind="ExternalInput").ap()
    G["w_out"] = nc.dram_tensor("w_out", [D, D], F32, kind="ExternalInput").ap()
    G["w_fg"] = nc.dram_tensor("w_fg", [D, DFF], F32, kind="ExternalInput").ap()
    G["w_fu"] = nc.dram_tensor("w_fu", [D, DFF], F32, kind="ExternalInput").ap()
    G["w_fd"] = nc.dram_tensor("w_fd", [DFF, D], F32, kind="ExternalInput").ap()
    G["gains_d"] = nc.dram_tensor("gains", [128, 4 * KC], F32, kind="ExternalInput").ap()
    G["bf_d"] = nc.dram_tensor("bfg", [128, 128], F32, kind="ExternalInput").ap()
    G["cbf_d"] = nc.dram_tensor("cbf", [128, 7 * 128], BF16, kind="ExternalInput").ap()
    G["cf_d"] = nc.dram_tensor("cf32", [128, 2 * 128], F32, kind="ExternalInput").ap()
    G["outT"] = nc.dram_tensor("outT", [D, S], F32, kind="ExternalOutput").ap()
    G["OTs"] = nc.dram_tensor("OTs", [2 * NH, DH, S], BF16).ap()
    G["mixs"] = nc.dram_tensor("mixs", [D, S], F32).ap()
    G["hs"] = nc.dram_tensor("hs", [D, S], F32).ap()
    G["ffs"] = nc.dram_tensor("ffs", [D, S], F32).ap()
    G["dbg"] = None
    if debug is not None:
        G["dbg"] = {k: nc.dram_tensor("dbg_" + k, shp, dt, kind="ExternalOutput").ap()
                    for k, (shp, dt) in debug.items()}

    with ExitStack() as top:
        arena = top.enter_context(nc.sbuf_tensor("arena", [128, 212736], mybir.dt.uint8))
        kb = KB(nc, top)
        kb.al = Alloc(nc)
        banks = []
        for i in range(8):
            t = top.enter_context(nc.psum_tensor(f"ps{i}", [128, 512], F32))
            banks.append((t, Res(f"ps{i}")))
        _emit(nc, kb, banks, G, upto)
        with nc.Block() as block:
            @block.sync
            def _(e):
                kb.sp.replay(e)

            @block.tensor
            def _(e):
                kb.pe.replay(e)

            @block.scalar
            def _(e):
                kb.act.replay(e)

            @block.vector
            def _(e):
                kb.dve.replay(e)

            @block.gpsimd
            def _(e):
                kb.pool.replay(e)
    return nc


def _emit(nc, kb, banks, G, upto):
    PE, ACT, DVE, POOL, SP = kb.pe, kb.act, kb.dve, kb.pool, kb.sp
    al = kb.al
    xT, w_in, w_bsb, w_bfx, w_out = G["xT"], G["w_in"], G["w_bsb"], G["w_bfx"], G["w_out"]
    w_fg, w_fu, w_fd = G["w_fg"], G["w_fu"], G["w_fd"]
    gains_d, bf_d, cbf_d, cf_d, outT = G["gains_d"], G["bf_d"], G["cbf_d"], G["cf_d"], G["outT"]
    OTs, mixs, hs, ffs, dbg = G["OTs"], G["mixs"], G["hs"], G["ffs"], G["dbg"]
    final_events = []

    cbf = Slot(kb, "cbf", [128, 7 * 128], BF16, dma=True)
    cf = Slot(kb, "cf", [128, 2 * 128], F32, dma=True)
    gains = Slot(kb, "gains", [128, 4 * KC], F32, dma=True)
    bfg = Slot(kb, "bfg", [128, 128], F32, dma=True)
    kb.dma(SP, cbf.ds, cbf.t[:, :], cbf_d, writes=[cbf.r])
    kb.dma(SP, cf.ds, cf.t[:, :], cf_d, writes=[cf.r])
    kb.dma(SP, gains.ds, gains.t[:, :], gains_d, writes=[gains.r])
    kb.dma(SP, bfg.ds, bfg.t[:, :], bf_d, writes=[bfg.r])
    ones_bf = cbf.t[:, 0:128]
    negtri_bf = cbf.t[:, 128:256]
    negones_bf = cbf.t[:, 256:384]
    mstrict_bf = cbf.t[:, 384:512]
    mincl_bf = cbf.t[:, 512:640]
    zeros_bf = cbf.t[:, 640:768]
    ident_bf = cbf.t[:, 768:896]
    tri_f = cf.t[:, 0:128]
    ones_f = cf.t[:, 128:256]

    rstd2 = Slot(kb, "rstd2", [128, S], F32)
    rstdm = Slot(kb, "rstdm", [128, S], F32)
    m_u = al.mark()
    biasT = Slot(kb, "biasT", [128, NCH, NH, NB], F32)
    uT = Slot(kb, "uT", [128, KC, S], BF16)
    m_w = al.mark()
    wf32 = Slot(kb, "wf32", [128, KC, 8], F32, dma=True)
    wfb = Slot(kb, "wfb", [128, KC, 8], BF16)
    zf = Slot(kb, "zf", [128, 128], F32)
    tots = Slot(kb, "tots", [128, 128], F32)
    pre = Slot(kb, "pre", [128, 136], F32)
    cumL = Slot(kb, "cumL", [128, 128], F32)
    wst = [Slot(kb, f"wst{i}", [128, KC, 128], F32, dma=True) for i in range(3)]
    wbf = [[Slot(kb, f"wbf{s}_{j}", [128, KC, 128], BF16) for j in range(3)] for s in range(2)]

    def dump(name, src_ap, src_res, dst_ap=None):
        if dbg is None or name not in dbg:
            return
        ds = kb.dsem("dbg" + name)
        ev = kb.dma(SP, ds, dbg[name] if dst_ap is None else dst_ap, src_ap, reads=[src_res])
        final_events.append(ev)

    def finish():
        for ev in final_events:
            SP.wait(ev)

    def rstd_from_banks(bank_ids, dst_t, dst_r, col0, tmp):
        for i, bi in enumerate(bank_ids):
            bt, br = banks[bi]
            sl = slice(col0 + i * 512, col0 + (i + 1) * 512)
            kb.op(DVE, "tensor_scalar", dict(out=tmp.t[:, 0:512], in0=bt[:, :], scalar1=1.0 / D, scalar2=EPS,
                                             op0=ALU.mult, op1=ALU.add), writes=[br, tmp.r])
            kb.op(ACT, "activation", dict(out=tmp.t[:, 0:512], in_=tmp.t[:, 0:512], func=AF.Ln), writes=[tmp.r])
            kb.op(ACT, "activation", dict(out=dst_t[:, sl], in_=tmp.t[:, 0:512], func=AF.Exp, scale=-0.5),
                  reads=[tmp.r], writes=[dst_r])

    m0 = al.mark()
    xs = [Slot(kb, f"xs{i}", [128, S], F32, dma=True) for i in range(2)]
    sq = [Slot(kb, f"sq{i}", [128, S], BF16) for i in range(2)]
    rstd1 = Slot(kb, "rstd1", [128, S], F32)
    tmp = Slot(kb, "tmp1", [128, 512], F32)
    for c in range(KC):
        x = xs[c % 2]
        q = sq[c % 2]
        kb.dma(SP, x.ds, x.t[:, :], xT[c * 128:(c + 1) * 128, :], writes=[x.r])
        kb.op(ACT, "activation", dict(out=q.t[:, :], in_=x.t[:, :], func=AF.Square),
              reads=[x.r], writes=[q.r])
        for tc in range(NCH):
            bt, br = banks[tc]
            kb.op(PE, "matmul", dict(out=bt[:, :], lhsT=ones_bf, rhs=q.t[:, tc * 512:(tc + 1) * 512],
                                     start=(c == 0), stop=(c == KC - 1)),
                  reads=[q.r, cbf.r], writes=[br])
    rstd_from_banks([0, 1, 2, 3], rstd1.t, rstd1.r, 0, tmp)
    for c in range(KC):
        x = xs[c % 2]
        kb.dma(SP, x.ds, x.t[:, :], xT[c * 128:(c + 1) * 128, :], writes=[x.r])
        kb.op(DVE, "scalar_tensor_tensor", dict(out=uT.t[:, c, :], in0=x.t[:, :], scalar=gains.t[:, c:c + 1],
                                                in1=rstd1.t[:, :], op0=ALU.mult, op1=ALU.mult),
              reads=[x.r, gains.r, rstd1.r], writes=[uT.r])
    dump("rstd1", rstd1.t[:, :], rstd1.r)
    if dbg is not None and "uT" in dbg:
        dump("uT", uT.t[:, :, :], uT.r, dbg["uT"].rearrange("(c p) t -> p c t", p=128))
    al.release(m0)
    if upto <= 1:
        return finish()

    def stage2():
        m0 = al.mark()
        tots2, pre2, cumL2 = tots.t, pre.t, cumL.t
        kb.dma(SP, wf32.ds, wf32.t[:, :, :], w_in[:, COL_F:COL_F + 8].rearrange("(k p) c -> p k c", p=128),
               writes=[wf32.r])
        kb.op(DVE, "tensor_copy", dict(out=wfb.t[:, :, :], in_=wf32.t[:, :, :]), reads=[wf32.r], writes=[wfb.r])
        bF, rF = banks[4]
        for blk in range(NB):
            calls = [("matmul", dict(out=bF[:, blk * 8:(blk + 1) * 8], lhsT=uT.t[:, k, blk * 128:(blk + 1) * 128],
                                     rhs=wfb.t[:, k, :], start=(k == 0), stop=(k == KC - 1))) for k in range(KC)]
            kb.group(PE, calls, reads=[uT.r, wfb.r], writes=[rF])
        kb.op(DVE, "tensor_tensor", dict(out=zf.t[:, :], in0=bF[:, 0:128], in1=bfg.t[:, :], op=ALU.add),
              reads=[bfg.r], writes=[rF, zf.r])
        kb.op(ACT, "activation", dict(out=zf.t[:, :], in_=zf.t[:, :], func=AF.Exp, scale=-1.0), writes=[zf.r])
        kb.op(DVE, "tensor_scalar_add", dict(out=zf.t[:, :], in0=zf.t[:, :], scalar1=1.0), writes=[zf.r])
        kb.op(ACT, "activation", dict(out=zf.t[:, :], in_=zf.t[:, :], func=AF.Ln), writes=[zf.r])
        bW, rW = banks[5]
        bT, rT = banks[6]
        kb.op(PE, "matmul", dict(out=bW[:, 0:128], lhsT=tri_f, rhs=zf.t[:, :], start=True, stop=True),
              reads=[zf.r, cf.r], writes=[rW])
        kb.op(PE, "matmul", dict(out=bT[:, 0:128], lhsT=ones_f, rhs=zf.t[:, :], start=True, stop=True),
              reads=[zf.r, cf.r], writes=[rT])
        kb.op(DVE, "tensor_copy", dict(out=tots2[:, :], in_=bT[:, 0:128]), writes=[rT, tots.r])
        kb.op(DVE, "memset", dict(ap=pre2[:, 0:8], constant=0.0), writes=[pre.r])
        for blk in range(1, NB + 1):
            kb.op(DVE, "tensor_tensor", dict(out=pre2[:, blk * 8:(blk + 1) * 8], in0=pre2[:, (blk - 1) * 8:blk * 8],
                                             in1=tots2[:, (blk - 1) * 8:blk * 8], op=ALU.add),
                  reads=[tots.r], writes=[pre.r])
        kb.op(DVE, "tensor_tensor", dict(out=cumL2[:, :], in0=bW[:, 0:128], in1=pre2[:, 0:128], op=ALU.add),
              reads=[pre.r], writes=[rW, cumL.r])
        for c in range(NCH):
            for h in range(NH):
                pc = 4 * (c + 1) * 8 + h
                kb.op(DVE, "tensor_scalar", dict(out=biasT.t[:, c, h, :], in0=cumL.t[:, h:128:8],
                                                 scalar1=pre.t[:, pc:pc + 1], scalar2=None,
                                                 op0=ALU.subtract),
                      reads=[cumL.r, pre.r], writes=[biasT.r])
        dump("cumL", cumL2[:, :], cumL.r)
        if dbg is not None and "biasT" in dbg:
            dump("biasT", biasT.t[:, :, :, :].rearrange("p a b c -> p (a b c)"), biasT.r)
        al.release(m0)


    m0 = al.mark()
    qkv = []
    for par in range(2):
        q_ = Slot(kb, f"qT{par}", [128, S], BF16)
        k_ = Slot(kb, f"kT{par}", [128, S], BF16)
        v_ = Slot(kb, f"V{par}", [128, NB, 128], BF16)
        qkv.append((q_, k_, v_, None))
    VT = Slot(kb, "VT", [128, S], BF16)
    Eb = [Slot(kb, f"E{i}", [128, 512], F32) for i in range(2)]
    SPb = [Slot(kb, f"SP{i}", [128, 512], BF16) for i in range(4)]
    Wb = [Slot(kb, f"W{i}", [128, 512], BF16) for i in range(4)]
    Rb = [[Slot(kb, f"R{p}_{i}", [128, 512], BF16) for i in range(2)] for p in range(2)]
    rec = Slot(kb, "rec", [128, 512], F32)
    ocp = Slot(kb, "ocp", [128, 512], F32)
    Ost = [Slot(kb, f"Ost{i}", [128, 512], BF16, dma=True) for i in range(2)]
    OT_res = [[Res(f"OT{hh}_{c}") for c in range(NCH)] for hh in range(2 * NH)]
    cnt = {"w": 0, "p": 0, "z": 0, "o": 0}
    zb = [banks[2], banks[3], banks[4], banks[5]]
    pb = [banks[0], banks[1]]

    def load_head(hh, ws):
        br, h = divmod(hh, NH)
        for j in range(3):
            col = br * 3072 + j * 1024 + h * 128
            sl = wst[cnt["w"] % 3]
            cnt["w"] += 1
            kb.dma(SP, sl.ds, sl.t[:, :, :], w_in[:, col:col + 128].rearrange("(k p) c -> p k c", p=128),
                   writes=[sl.r])
            dst = wbf[ws][j]
            kb.op(DVE, "tensor_copy", dict(out=dst.t[:, :, :], in_=sl.t[:, :, :]), reads=[sl.r], writes=[dst.r])

    def proj_items(ws, par):
        wq, wk, wv = wbf[ws]
        qT, kT, V, V2 = qkv[par]
        items = []

        def qk_item(wsl, dst, tc, sc):
            st_ = {}

            def f0():
                bt, br_ = pb[cnt["p"] % 2]
                cnt["p"] += 1
                st_["b"] = (bt, br_)
                kb.group(PE, [("matmul", dict(out=bt[:, :], lhsT=wsl.t[:, k, :],
                                              rhs=uT.t[:, k, tc * 512:(tc + 1) * 512],
                                              start=(k == 0), stop=False)) for k in range(KC // 2)],
                         reads=[wsl.r, uT.r], writes=[br_])

            def f1():
                bt, br_ = st_["b"]
                kb.group(PE, [("matmul", dict(out=bt[:, :], lhsT=wsl.t[:, k, :],
                                              rhs=uT.t[:, k, tc * 512:(tc + 1) * 512],
                                              start=False, stop=(k == KC - 1))) for k in range(KC // 2, KC)],
                         reads=[wsl.r, uT.r], writes=[br_])
                kb.op(DVE, "tensor_scalar", dict(out=dst.t[:, tc * 512:(tc + 1) * 512], in0=bt[:, :],
                                                 scalar1=sc, scalar2=None, op0=ALU.mult),
                      writes=[br_, dst.r])
            return [f0, f1]

        def vt_items():
            its = []
            for tc in range(NCH):
                its += qk_item(wv, VT, tc, 1.0)

            def tr(g):
                def f():
                    bt, br_ = pb[cnt["p"] % 2]
                    cnt["p"] += 1
                    btb = bt[:, :].bitcast(BF16)
                    kb.group(PE, [("transpose", dict(out=btb[:, j * 128:(j + 1) * 128],
                                                     in_=VT.t[:, (8 * g + j) * 128:(8 * g + j + 1) * 128],
                                                     identity=ident_bf)) for j in range(8)],
                             reads=[VT.r, cbf.r], writes=[br_])
                    kb.op(DVE, "tensor_copy", dict(out=V.t[:, 8 * g:8 * g + 8, :].rearrange("p a b -> p (a b)"),
                                                   in_=btb[:, 0:1024]), writes=[br_, V.r])
                return f
            its += [tr(0), tr(1)]
            return its

        for tc in range(NCH):
            items += qk_item(wk, kT, tc, 1.0)
        items += vt_items()
        for tc in range(NCH):
            items += qk_item(wq, qT, tc, SCALE)
        return items

    def store_O(hh, c, ost):
        kb.dma(SP, ost.ds, OTs[hh, :, c * 512:(c + 1) * 512], ost.t[:, :], reads=[ost.r],
               writes=[OT_res[hh][c]])

    def attn_sb(hh, par, filler):
        qT, kT, V, V2 = qkv[par]
        seq = []
        for c in range(NCH):
            blocks = list(range(4 * c + 3, -1, -1))
            for j, b in enumerate(blocks):
                seq.append((c, j, b, len(blocks)))
        N = len(seq)
        stt = {}
        ste = {}

        def A_pe(g):
            c, j, b, n = seq[g]
            o = max(0, 128 * (b - 4 * c))
            diag = b >= 4 * c
            zi = cnt["z"]
            cnt["z"] += 1
            bz, rz = zb[zi % 4]
            e, sp = Eb[zi % 2], SPb[zi % 4]
            stt[g] = (o, diag, bz, rz, sp, Wb[zi % 4])
            ste[g] = e
            if j == 0:
                bO, rO = banks[6 + c % 2]
                kb.op(PE, "matmul", dict(out=bO[:, :], lhsT=zeros_bf, rhs=uT.t[:, 0, 0:512], start=True, stop=True),
                      reads=[cbf.r, uT.r], writes=[rO])
                for R_ in Rb[c % 2]:
                    kb.op(POOL, "memset", dict(ap=R_.t[:, :], constant=0.0), writes=[R_.r])
            kb.op(PE, "matmul", dict(out=bz[:, o:512], lhsT=kT.t[:, b * 128:(b + 1) * 128],
                                     rhs=qT.t[:, c * 512 + o:(c + 1) * 512], start=True, stop=True),
                  reads=[kT.r, qT.r], writes=[rz])

        def A_act(g):
            o, diag, bz, rz, sp, w = stt[g]
            e = ste.pop(g)
            kb.op(ACT, "activation", dict(out=e.t[:, o:512], in_=bz[:, o:512], func=AF.Exp),
                  writes=[rz, e.r])
            kb.op(ACT, "activation", dict(out=sp.t[:, o:512], in_=e.t[:, o:512], func=AF.Ln,
                                          bias=ones_f[:, 0:1]),
                  reads=[e.r, cf.r], writes=[sp.r])
            if diag:
                kb.op(POOL, "tensor_tensor", dict(out=sp.t[:, o:o + 128], in0=sp.t[:, o:o + 128],
                                                  in1=mstrict_bf, op=ALU.mult),
                      reads=[cbf.r], writes=[sp.r])

        def B1(g):
            c, j, b, n = seq[g]
            o, diag, bz, rz, sp, w = stt[g]
            first = (j == 0)
            R = Rb[c % 2][j % 2]
            calls = [("matmul", dict(out=bz[:, o:512], lhsT=negtri_bf, rhs=sp.t[:, o:512],
                                     start=False, stop=first, skip_group_check=True))]
            if not first:
                calls.append(("matmul", dict(out=bz[:, o:512], lhsT=negones_bf, rhs=R.t[:, o:512],
                                             start=False, stop=True, skip_group_check=True)))
            kb.group(PE, calls, reads=[sp.r, R.r, cbf.r], writes=[rz])
            kb.op(ACT, "activation", dict(out=w.t[:, o:512], in_=bz[:, o:512], func=AF.Exp),
                  writes=[rz, w.r])
            if diag:
                kb.op(POOL, "tensor_tensor", dict(out=w.t[:, o:o + 128], in0=w.t[:, o:o + 128],
                                                  in1=mstrict_bf, op=ALU.mult),
                      reads=[cbf.r], writes=[w.r])
            if j < n - 1:
                Rn = Rb[c % 2][(j + 1) % 2]
                kb.op(DVE, "tensor_tensor", dict(out=Rn.t[:, o:512], in0=R.t[:, o:512], in1=sp.t[:, o:512],
                                                 op=ALU.add), reads=[sp.r, R.r], writes=[Rn.r])

        def B2(g):
            c, j, b, n = seq[g]
            o, diag, bz, rz, sp, w = stt.pop(g)
            bO, rO = banks[6 + c % 2]
            kb.op(PE, "matmul", dict(out=bO[:, o:512], lhsT=V.t[:, b, :], rhs=w.t[:, o:512],
                                     start=False, stop=(j == n - 1), skip_group_check=True),
                  reads=[V.r, w.r], writes=[rO])
            if j == n - 1:
                ost = Ost[cnt["o"] % 2]
                cnt["o"] += 1
                kb.op(DVE, "tensor_copy", dict(out=ost.t[:, :], in_=bO[:, :]), writes=[rO, ost.r])
                store_O(hh, c, ost)

        for g in range(min(3, N)):
            A_pe(g)
        for g in range(min(2, N)):
            A_act(g)
        filler.burst(2)
        for g in range(N):
            if g + 2 < N:
                A_act(g + 2)
            B1(g)
            if g >= 1:
                B2(g - 1)
            if g + 3 < N:
                A_pe(g + 3)
            filler.tick()
        filler.burst(1)
        B2(N - 1)

    def attn_fox(hh, par, filler):
        qT, kT, V, V2 = qkv[par]
        h = hh % NH
        seq = []
        for c in range(NCH):
            for b in range(4 * c + 4):
                seq.append((c, b, 4 * c + 4))
        N = len(seq)
        stt = {}
        zb2 = [banks[2], banks[3]]

        def A_pe(g):
            c, b, n = seq[g]
            o = max(0, 128 * (b - 4 * c))
            zi = cnt["z"]
            cnt["z"] += 1
            bz, rz = zb2[zi % 2]
            stt[g] = (o, bz, rz, Wb[zi % 4])
            if b == 0:
                (bO, rO), (bD, rD) = banks[4 + 2 * (c % 2)], banks[5 + 2 * (c % 2)]
                kb.group(PE, [("matmul", dict(out=bO[:, :], lhsT=zeros_bf, rhs=uT.t[:, 0, 0:512], start=True, stop=True)),
                              ("matmul", dict(out=bD[:, :], lhsT=zeros_bf, rhs=uT.t[:, 0, 0:512], start=True, stop=True))],
                         reads=[cbf.r, uT.r], writes=[rO, rD])
            kb.op(PE, "matmul", dict(out=bz[:, o:512], lhsT=kT.t[:, b * 128:(b + 1) * 128],
                                     rhs=qT.t[:, c * 512 + o:(c + 1) * 512], start=True, stop=True),
                  reads=[kT.r, qT.r], writes=[rz])

        def A_act(g):
            c, b, n = seq[g]
            o, bz, rz, p = stt[g]
            kb.op(ACT, "activation", dict(out=p.t[:, o:512], in_=bz[:, o:512], func=AF.Exp,
                                          bias=biasT.t[:, c, h, b:b + 1]),
                  reads=[biasT.r], writes=[rz, p.r])
            if b >= 4 * c:
                kb.op(POOL, "tensor_tensor", dict(out=p.t[:, o:o + 128], in0=p.t[:, o:o + 128],
                                                  in1=mincl_bf, op=ALU.mult),
                      reads=[cbf.r], writes=[p.r])

        def B(g):
            c, b, n = seq[g]
            o, bz, rz, p = stt.pop(g)
            (bO, rO), (bD, rD) = banks[4 + 2 * (c % 2)], banks[5 + 2 * (c % 2)]
            last = (b == n - 1)
            kb.group(PE, [("matmul", dict(out=bO[:, o:512], lhsT=V.t[:, b, :], rhs=p.t[:, o:512],
                                          start=False, stop=last, skip_group_check=True)),
                          ("matmul", dict(out=bD[:, o:512], lhsT=ones_bf, rhs=p.t[:, o:512],
                                          start=False, stop=last, skip_group_check=True))],
                     reads=[V.r, p.r, cbf.r], writes=[rO, rD])
            if last:
                ost = Ost[cnt["o"] % 2]
                cnt["o"] += 1
                kb.op(ACT, "activation", dict(out=rec.t[:, :], in_=bD[:, :], func=AF.Copy), writes=[rD, rec.r])
                kb.op(DVE, "tensor_copy", dict(out=ocp.t[:, :], in_=bO[:, :]), writes=[rO, ocp.r])
                kb.op(DVE, "reciprocal", dict(out=rec.t[:, :], in_=rec.t[:, :]), writes=[rec.r])
                kb.op(DVE, "tensor_tensor", dict(out=ost.t[:, :], in0=ocp.t[:, :], in1=rec.t[:, :], op=ALU.mult),
                      reads=[rec.r, ocp.r], writes=[ost.r])
                store_O(hh, c, ost)

        for g in range(min(2, N)):
            A_pe(g)
        A_act(0)
        filler.burst(1)
        for g in range(N):
            if g + 1 < N:
                A_act(g + 1)
            B(g)
            if g + 2 < N:
                A_pe(g + 2)
            filler.tick()

    heads = list(range(2 * NH)) if upto >= 4 else G.get("dbg_heads", [0, 8])
    load_head(heads[0], 0)
    if len(heads) > 1:
        load_head(heads[1], 1)
    for it in proj_items(0, 0):
        it()
    stage2()
    for i, hh in enumerate(heads):
        if i + 2 < len(heads):
            load_head(heads[i + 2], i % 2)
        filler = Filler(proj_items((i + 1) % 2, (i + 1) % 2) if i + 1 < len(heads) else [], 40)
        if hh < NH:
            attn_sb(hh, i % 2, filler)
        else:
            attn_fox(hh, i % 2, filler)
        filler.flush()
    if dbg is not None and "OT" in dbg:
        ds = kb.dsem("dbgOT")
        allr = [OT_res[hh][c] for hh in heads for c in range(NCH)]
        for hh in heads:
            ev = kb.dma(SP, ds, dbg["OT"][hh], OTs[hh], reads=allr)
        final_events.append(ev)
    al.release(m_w)
    if upto <= 3:
        return finish()

    TH = 1024
    NQ = S // 512
    mix_res = [[Res(f"mix{c}_{q}") for q in range(NQ)] for c in range(KC)]
    h_res = [[Res(f"h{c}_{q}") for q in range(NQ)] for c in range(KC)]
    ff_res = [[Res(f"ff{c}_{q}") for q in range(NQ)] for c in range(KC)]
    nofill = Filler([], 1)

    def stat_mm(bi, sq_ap, sq_res, c):
        sb_t, sb_r = banks[bi]
        kb.op(PE, "matmul", dict(out=sb_t[:, :], lhsT=ones_bf, rhs=sq_ap, start=(c == 0), stop=(c == KC - 1)),
              reads=[sq_res, cbf.r], writes=[sb_r])

    def wview(w, c0, kc):
        return w[:, c0:c0 + 128].rearrange("(k p) c -> p k c", p=128)

    def hpass_items(hf, sbank0):
        xh = [Slot(kb, f"xh{i}", [128, 512], F32, dma=True) for i in range(2)]
        mh = [Slot(kb, f"mh{i}", [128, 512], F32, dma=True) for i in range(2)]
        hst = [Slot(kb, f"hst{i}", [128, 512], F32, dma=True) for i in range(2)]
        sqh = [Slot(kb, f"sqh{i}", [128, 512], BF16) for i in range(3)]
        pend = []
        items = []

        def mk(c, tc, n):
            def f():
                q = 2 * hf + tc
                tsl = slice(q * 512, (q + 1) * 512)
                xx, mm, hh_, qq = xh[n % 2], mh[n % 2], hst[n % 2], sqh[n % 3]
                kb.dma(SP, xx.ds, xx.t[:, :], xT[c * 128:(c + 1) * 128, tsl], writes=[xx.r])
                kb.dma(SP, mm.ds, mm.t[:, :], mixs[c * 128:(c + 1) * 128, tsl], reads=[mix_res[c][q]], writes=[mm.r])
                kb.op(DVE, "scalar_tensor_tensor", dict(out=hh_.t[:, :], in0=mm.t[:, :], scalar=gains.t[:, KC + c:KC + c + 1],
                                                        in1=rstdm.t[:, tsl], op0=ALU.mult, op1=ALU.mult),
                      reads=[mm.r, gains.r, rstdm.r], writes=[hh_.r])
                kb.op(POOL, "tensor_tensor", dict(out=hh_.t[:, :], in0=hh_.t[:, :], in1=xx.t[:, :], op=ALU.add),
                      reads=[xx.r], writes=[hh_.r])
                kb.dma(POOL, hh_.ds, hs[c * 128:(c + 1) * 128, tsl], hh_.t[:, :], reads=[hh_.r], writes=[h_res[c][q]])
                kb.op(ACT, "activation", dict(out=qq.t[:, :], in_=hh_.t[:, :], func=AF.Square),
                      reads=[hh_.r], writes=[qq.r])
                if pend:
                    stat_mm(*pend.pop(0))
                pend.append((sbank0 + tc, qq.t[:, :], qq.r, c))
            return f

        n = 0
        for c in range(KC):
            for tc in range(2):
                items.append(mk(c, tc, n))
                n += 1

        def fin():
            while pend:
                stat_mm(*pend.pop(0))
        return items, fin

    def u2_items(hf, u2, nbuf=2):
        hl = [Slot(kb, f"hl{i}", [128, 512], F32, dma=True) for i in range(nbuf)]
        items = []

        def mk(c, tc, n):
            def f():
                q = 2 * hf + tc
                tsl = slice(q * 512, (q + 1) * 512)
                x = hl[n % nbuf]
                kb.dma(SP, x.ds, x.t[:, :], hs[c * 128:(c + 1) * 128, tsl], reads=[h_res[c][q]], writes=[x.r])
                kb.op(DVE, "scalar_tensor_tensor", dict(out=u2.t[:, c, tc * 512:(tc + 1) * 512], in0=x.t[:, :],
                                                        scalar=gains.t[:, 2 * KC + c:2 * KC + c + 1],
                                                        in1=rstd2.t[:, tsl], op0=ALU.mult, op1=ALU.mult),
                      reads=[x.r, gains.r, rstd2.r], writes=[u2.r])
            return f

        n = 0
        for c in range(KC):
            for tc in range(2):
                items.append(mk(c, tc, n))
                n += 1
        return items

    def final_items(hf, rstd3, tcs=(0, 1), nbuf=2):
        hl = [Slot(kb, f"hl2_{i}", [128, 512], F32, dma=True) for i in range(nbuf)]
        fl = [Slot(kb, f"fl{i}", [128, 512], F32, dma=True) for i in range(nbuf)]
        ot = [Slot(kb, f"ot{i}", [128, 512], F32, dma=True) for i in range(nbuf)]
        items = []

        def mk(c, tc, n):
            def f():
                q = 2 * hf + tc
                tsl = slice(q * 512, (q + 1) * 512)
                hh_, ff_, oo = hl[n % nbuf], fl[n % nbuf], ot[n % nbuf]
                kb.dma(SP, hh_.ds, hh_.t[:, :], hs[c * 128:(c + 1) * 128, tsl], reads=[h_res[c][q]], writes=[hh_.r])
                kb.dma(SP, ff_.ds, ff_.t[:, :], ffs[c * 128:(c + 1) * 128, tsl], reads=[ff_res[c][q]], writes=[ff_.r])
                kb.op(DVE, "scalar_tensor_tensor", dict(out=oo.t[:, :], in0=ff_.t[:, :],
                                                        scalar=gains.t[:, 3 * KC + c:3 * KC + c + 1],
                                                        in1=rstd3.t[:, tc * 512:(tc + 1) * 512], op0=ALU.mult, op1=ALU.mult),
                      reads=[ff_.r, gains.r, rstd3.r], writes=[oo.r])
                kb.op(POOL, "tensor_tensor", dict(out=oo.t[:, :], in0=oo.t[:, :], in1=hh_.t[:, :], op=ALU.add),
                      reads=[hh_.r], writes=[oo.r])
                ev = kb.dma(POOL, oo.ds, outT[c * 128:(c + 1) * 128, tsl], oo.t[:, :], reads=[oo.r])
                final_events.append(ev)
            return f

        n = 0
        for c in range(KC):
            for tc in tcs:
                items.append(mk(c, tc, n))
                n += 1
        return items

    def P5(hf, merged):
        t0 = hf * TH
        OTh = Slot(kb, "OTh", [128, 2 * NH, TH], BF16, dma=True)
        wst = [Slot(kb, f"wstA{i}", [128, KC, 128], F32, dma=True) for i in range(2)]
        wg = [[Slot(kb, f"wgA{i}_{j}", [128, KC, 128], BF16) for j in range(3)] for i in range(2)]
        s1 = [Slot(kb, f"s1_{i}", [128, 512], F32) for i in range(2)]
        s2 = [Slot(kb, f"s2_{i}", [128, 512], F32) for i in range(2)]
        m1 = Slot(kb, "m1", [128, 512], F32)
        m2 = Slot(kb, "m2", [128, 512], F32)
        kb.dma(SP, OTh.ds, OTh.t[:, :, :], OTs[:, :, t0:t0 + TH].rearrange("h p t -> p h t"),
               reads=[OT_res[hh][2 * hf + i] for hh in range(2 * NH) for i in range(2)], writes=[OTh.r])
        cntA = {"w": 0, "b": 0}

        def loadA(c):
            fills = [[(w_in, COL_GSB + c * 128, KC, 0)], [(w_in, COL_GFX + c * 128, KC, 0)],
                     [(w_bsb, c * 128, NH, 0), (w_bfx, c * 128, NH, NH)]]
            for j, fill in enumerate(fills):
                sl = wst[cntA["w"] % 2]
                cntA["w"] += 1
                kb.dma_multi(SP, sl.ds, [(sl.t[:, k0:k0 + kc, :], wview(w, c0, kc)) for (w, c0, kc, k0) in fill],
                             writes=[sl.r])
                dst = wg[c % 2][j]
                kb.op(DVE, "tensor_copy", dict(out=dst.t[:, :, :], in_=sl.t[:, :, :]), reads=[sl.r], writes=[dst.r])

        loadA(0)
        for c in range(KC):
            if c + 1 < KC:
                loadA(c + 1)
            wgs, wgf, wb = wg[c % 2]
            for tc in range(2):
                bset = [banks[4 * (cntA["b"] % 2) + i] for i in range(4)]
                cntA["b"] += 1
                tsl = slice(t0 + tc * 512, t0 + (tc + 1) * 512)
                hsl = slice(tc * 512, (tc + 1) * 512)
                for (bt, br_), wsl in ((bset[0], wgs), (bset[1], wgf)):
                    kb.group(PE, [("matmul", dict(out=bt[:, :], lhsT=wsl.t[:, k, :], rhs=uT.t[:, k, tsl],
                                                  start=(k == 0), stop=(k == KC - 1))) for k in range(KC)],
                             reads=[wsl.r, uT.r], writes=[br_])
                for (bt, br_), k0 in ((bset[2], 0), (bset[3], NH)):
                    kb.group(PE, [("matmul", dict(out=bt[:, :], lhsT=wb.t[:, k0 + k, :], rhs=OTh.t[:, k0 + k, hsl],
                                                  start=(k == 0), stop=(k == NH - 1))) for k in range(NH)],
                             reads=[wb.r, OTh.r], writes=[br_])
                a1, a2 = s1[tc], s2[tc]
                kb.op(ACT, "activation", dict(out=a1.t[:, :], in_=bset[0][0][:, :], func=AF.Sigmoid),
                      writes=[bset[0][1], a1.r])
                kb.op(ACT, "activation", dict(out=a2.t[:, :], in_=bset[1][0][:, :], func=AF.Sigmoid),
                      writes=[bset[1][1], a2.r])
                kb.op(DVE, "tensor_tensor", dict(out=m1.t[:, :], in0=bset[2][0][:, :], in1=a1.t[:, :], op=ALU.mult),
                      reads=[a1.r], writes=[bset[2][1], m1.r])
                kb.op(DVE, "tensor_tensor", dict(out=m2.t[:, :], in0=bset[3][0][:, :], in1=a2.t[:, :], op=ALU.mult),
                      reads=[a2.r], writes=[bset[3][1], m2.r])
                kb.op(POOL, "tensor_tensor", dict(out=merged.t[:, c, hsl], in0=m1.t[:, :], in1=m2.t[:, :], op=ALU.add),
                      reads=[m1.r, m2.r], writes=[merged.r])

    def WOUT(hf, merged, filler):
        t0 = hf * TH
        wst = [Slot(kb, f"wstB{i}", [128, KC, 128], F32, dma=True) for i in range(3)]
        wo = [Slot(kb, f"wo{i}", [128, KC, 128], BF16) for i in range(2)]
        mixst = [Slot(kb, f"mixst{i}", [128, 512], F32, dma=True) for i in range(2)]
        sqm = [Slot(kb, f"sqm{i}", [128, 512], BF16) for i in range(3)]
        tmpB = Slot(kb, "tmpB", [128, 512], F32)
        cntB = {"b": 0, "m": 0}
        pend = []

        def loadB(c):
            sl = wst[c % 3]
            kb.dma(SP, sl.ds, sl.t[:, :, :], wview(w_out, c * 128, KC), writes=[sl.r])
            kb.op(DVE, "tensor_copy", dict(out=wo[c % 2].t[:, :, :], in_=sl.t[:, :, :]), reads=[sl.r], writes=[wo[c % 2].r])

        loadB(0)
        for c in range(KC):
            if c + 1 < KC:
                loadB(c + 1)
            for tc in range(2):
                bt, br_ = banks[cntB["b"] % 4]
                cntB["b"] += 1
                hsl = slice(tc * 512, (tc + 1) * 512)
                kb.group(PE, [("matmul", dict(out=bt[:, :], lhsT=wo[c % 2].t[:, k, :], rhs=merged.t[:, k, hsl],
                                              start=(k == 0), stop=(k == KC - 1))) for k in range(KC)],
                         reads=[wo[c % 2].r, merged.r], writes=[br_])
                ms = mixst[cntB["m"] % 2]
                sm = sqm[cntB["m"] % 3]
                cntB["m"] += 1
                kb.op(ACT, "activation", dict(out=ms.t[:, :], in_=bt[:, :], func=AF.Copy), writes=[br_, ms.r])
                kb.dma(ACT, ms.ds, mixs[c * 128:(c + 1) * 128, t0 + tc * 512:t0 + (tc + 1) * 512], ms.t[:, :],
                       reads=[ms.r], writes=[mix_res[c][2 * hf + tc]])
                kb.op(ACT, "activation", dict(out=sm.t[:, :], in_=bt[:, :], func=AF.Square), writes=[br_, sm.r])
                pend.append((6 + tc, sm.t[:, :], sm.r, c))
                if len(pend) > 1:
                    stat_mm(*pend.pop(0))
            filler.tick()
        while pend:
            stat_mm(*pend.pop(0))
        rstd_from_banks([6, 7], rstdm.t, rstdm.r, t0, tmpB)
        return tmpB

    mA = al.mark()
    merged = Slot(kb, "merged", [128, KC, TH], BF16)
    mB = al.mark()
    P5(0, merged)
    al.release(mB)
    WOUT(0, merged, nofill)
    al.release(mB)
    P5(1, merged)
    al.release(mB)
    hp_items, hp_fin = hpass_items(0, 4)
    fill = Filler(hp_items, KC)
    tmpB = WOUT(1, merged, fill)
    fill.flush()
    hp_fin()
    rstd_from_banks([4, 5], rstd2.t, rstd2.r, 0, tmpB)
    if dbg is not None and "rstdm" in dbg:
        dump("rstdm", rstdm.t[:, 0:1024], rstdm.r)
    al.release(mA)

    al.release(m_u)
    u2 = Slot(kb, "u2", [128, KC, TH], BF16)
    aT = Slot(kb, "aT", [128, FC, TH], BF16)
    rstd3_ = Slot(kb, "rstd3", [128, TH], F32)
    rstd3 = [rstd3_, rstd3_]
    tmpD = Slot(kb, "tmpD", [128, 512], F32)
    mB = al.mark()

    def GATEUP(hf, filler):
        wst = [Slot(kb, f"wstC{i}", [128, KC, 128], F32, dma=True) for i in range(3)]
        wgu = [[Slot(kb, f"wgu{i}_{j}", [128, KC, 128], BF16) for j in range(2)] for i in range(2)]
        sg = [Slot(kb, f"sg{i}", [128, 512], F32) for i in range(2)]
        cntC = {"w": 0, "b": 0}

        def loadC(f):
            for j, w in enumerate((w_fg, w_fu)):
                sl = wst[cntC["w"] % 3]
                cntC["w"] += 1
                kb.dma(SP, sl.ds, sl.t[:, :, :], wview(w, f * 128, KC), writes=[sl.r])
                dst = wgu[f % 2][j]
                kb.op(DVE, "tensor_copy", dict(out=dst.t[:, :, :], in_=sl.t[:, :, :]), reads=[sl.r], writes=[dst.r])

        loadC(0)
        for f in range(FC):
            if f + 1 < FC:
                loadC(f + 1)
            wgt, wup = wgu[f % 2]
            for tc in range(2):
                pbi = cntC["b"] % 3
                cntC["b"] += 1
                (bg, rg), (bu, ru) = banks[2 * pbi], banks[2 * pbi + 1]
                hsl = slice(tc * 512, (tc + 1) * 512)
                kb.group(PE, [("matmul", dict(out=bg[:, :], lhsT=wgt.t[:, k, :], rhs=u2.t[:, k, hsl],
                                              start=(k == 0), stop=(k == KC - 1))) for k in range(KC)],
                         reads=[wgt.r, u2.r], writes=[rg])
                kb.group(PE, [("matmul", dict(out=bu[:, :], lhsT=wup.t[:, k, :], rhs=u2.t[:, k, hsl],
                                              start=(k == 0), stop=(k == KC - 1))) for k in range(KC)],
                         reads=[wup.r, u2.r], writes=[ru])
                sgt = sg[tc]
                kb.op(ACT, "activation", dict(out=sgt.t[:, :], in_=bg[:, :], func=AF.Silu), writes=[rg, sgt.r])
                kb.op(DVE, "tensor_tensor", dict(out=aT.t[:, f, hsl], in0=bu[:, :], in1=sgt.t[:, :], op=ALU.mult),
                      reads=[sgt.r], writes=[ru, aT.r])
            filler.tick()

    def DOWN(hf, filler, tc_outer=False):
        t0 = hf * TH
        wst = [Slot(kb, f"wstD{i}", [128, KC, 128], F32, dma=True) for i in range(3)]
        wd = [Slot(kb, f"wd{i}", [128, FC, 128], BF16) for i in range(2)]
        ffst = [Slot(kb, f"ffst{i}", [128, 512], F32, dma=True) for i in range(2)]
        sqf = [Slot(kb, f"sqf{i}", [128, 512], BF16) for i in range(3)]
        cntD = {"w": 0, "b": 0, "m": 0, "l": 0}
        pend = []
        wfd_v = w_fd.rearrange("(f p) c -> p f c", p=128)

        def loadD(c):
            for f0 in (0, 16, 32):
                nf = min(16, FC - f0)
                sl = wst[cntD["w"] % 3]
                cntD["w"] += 1
                kb.dma(SP, sl.ds, sl.t[:, 0:nf, :], wfd_v[:, f0:f0 + nf, c * 128:(c + 1) * 128], writes=[sl.r])
                wdl = wd[cntD["l"] % 2]
                kb.op(DVE, "tensor_copy", dict(out=wdl.t[:, f0:f0 + nf, :], in_=sl.t[:, 0:nf, :]),
                      reads=[sl.r], writes=[wdl.r])
            cntD["l"] += 1

        steps = ([(c, (0, 1)) for c in range(KC)] if not tc_outer
                 else [(c, (0,)) for c in range(KC)] + [(c, (1,)) for c in range(KC)])
        loadD(steps[0][0])
        for si, (c, tcs) in enumerate(steps):
            if si + 1 < len(steps):
                loadD(steps[si + 1][0])
            wdc = wd[si % 2]
            for tc in tcs:
                bt, br_ = banks[cntD["b"] % 4]
                cntD["b"] += 1
                hsl = slice(tc * 512, (tc + 1) * 512)
                kb.group(PE, [("matmul", dict(out=bt[:, :], lhsT=wdc.t[:, f, :], rhs=aT.t[:, f, hsl],
                                              start=(f == 0), stop=(f == FC - 1))) for f in range(FC)],
                         reads=[wdc.r, aT.r], writes=[br_])
                fs = ffst[cntD["m"] % 2]
                sf = sqf[cntD["m"] % 3]
                cntD["m"] += 1
                kb.op(ACT, "activation", dict(out=fs.t[:, :], in_=bt[:, :], func=AF.Copy), writes=[br_, fs.r])
                kb.dma(ACT, fs.ds, ffs[c * 128:(c + 1) * 128, t0 + tc * 512:t0 + (tc + 1) * 512], fs.t[:, :],
                       reads=[fs.r], writes=[ff_res[c][2 * hf + tc]])
                kb.op(ACT, "activation", dict(out=sf.t[:, :], in_=bt[:, :], func=AF.Square), writes=[br_, sf.r])
                pend.append((6 + tc, sf.t[:, :], sf.r, c))
                if len(pend) > 1:
                    stat_mm(*pend.pop(0))
            filler.tick()
            if tc_outer and si == KC - 1:
                while pend:
                    stat_mm(*pend.pop(0))
                filler.flush()
                rstd_from_banks([6], rstd3[hf].t, rstd3[hf].r, 0, tmpD)
                al.kill(u2.r)
                sv = al.top
                al.top = u2.off
                filler = Filler(final_items(hf, rstd3[hf], tcs=(0,)), KC)
                fin_tail = final_items(hf, rstd3[hf], tcs=(1,), nbuf=3)
                al.top = sv
        while pend:
            stat_mm(*pend.pop(0))
        filler.flush()
        if tc_outer:
            rstd_from_banks([7], rstd3[hf].t, rstd3[hf].r, 512, tmpD)
            for it in fin_tail:
                it()
        else:
            rstd_from_banks([6, 7], rstd3[hf].t, rstd3[hf].r, 0, tmpD)

    u2i = u2_items(0, u2, nbuf=6)
    for it in u2i:
        it()
    al.release(mB)
    hp_items, hp_fin = hpass_items(1, 6)
    fill = Filler(hp_items, FC)
    GATEUP(0, fill)
    fill.flush()
    hp_fin()
    rstd_from_banks([6, 7], rstd2.t, rstd2.r, TH, tmpD)
    al.release(mB)
    fill = Filler(u2_items(1, u2), KC)
    DOWN(0, fill)
    fill.flush()
    al.release(mB)
    fill = Filler(final_items(0, rstd3[0]), FC)
    GATEUP(1, fill)
    fill.flush()
    al.release(mB)
    mD = al.mark()
    fbufs_dummy = None
    DOWN(1, nofill, tc_outer=True)

    finish()


def _consts():
    j = np.arange(128)[:, None]
    s = np.arange(128)[None, :]
    cb = np.concatenate([
        np.ones((128, 128)),
        -(j >= s).astype(np.float64),
        -np.ones((128, 128)),
        (j < s).astype(np.float64),
        (j <= s).astype(np.float64),
        np.zeros((128, 128)),
        np.eye(128),
    ], axis=1).astype(ml_dtypes.bfloat16)
    cfm = np.concatenate([(j <= s).astype(np.float32), np.ones((128, 128), np.float32)], axis=1)
    return np.ascontiguousarray(cb), np.ascontiguousarray(cfm)


def make_in_maps(inputs, cores):
    cb, cfm = _consts()
    g = np.stack([np.asarray(inputs[k], np.float32)[0] for k in
                  ("norm_mix_pre", "norm_mix_post", "norm_ffn_pre", "norm_ffn_post")])
    gains = np.ascontiguousarray(g.reshape(4, KC, 128).transpose(2, 0, 1).reshape(128, 4 * KC))
    bfg = np.ascontiguousarray(np.broadcast_to(
        np.tile(np.asarray(inputs["b_forget"], np.float32)[0], NB)[None, :], (128, 128)))
    shared = {
        "w_in": np.ascontiguousarray(np.asarray(inputs["w_in"], np.float32)[0]),
        "w_bsb": np.ascontiguousarray(np.asarray(inputs["w_branch_sb"], np.float32)[0]),
        "w_bfx": np.ascontiguousarray(np.asarray(inputs["w_branch_fox"], np.float32)[0]),
        "w_out": np.ascontiguousarray(np.asarray(inputs["w_out"], np.float32)[0]),
        "w_fg": np.ascontiguousarray(np.asarray(inputs["w_ffn_gate"], np.float32)[0]),
        "w_fu": np.ascontiguousarray(np.asarray(inputs["w_ffn_up"], np.float32)[0]),
        "w_fd": np.ascontiguousarray(np.asarray(inputs["w_ffn_down"], np.float32)[0]),
        "gains": gains, "bfg": bfg, "cbf": cb, "cf32": cfm,
    }
    x = np.asarray(inputs["x"], np.float32)
    maps = []
    for b in cores:
        m = dict(shared)
        m["xT"] = np.ascontiguousarray(x[b].T)
        maps.append(m)
    return maps


def kernel(**inputs):
    nc = build_nc()
    in_maps = make_in_maps(inputs, list(range(8)))
    res = run_bass_kernel_spmd(nc, in_maps, core_ids=list(range(8)))
    out = np.stack([np.ascontiguousarray(np.asarray(r["outT"], np.float32).T) for r in res.results])
    return out
```

```python
import numpy as np
import ml_dtypes
from contextlib import ExitStack
import concourse.bass as bass
import concourse.mybir as mybir
from concourse.bass_utils import run_bass_kernel_spmd

F32 = mybir.dt.float32
BF16 = mybir.dt.bfloat16
AF = mybir.ActivationFunctionType
ALU = mybir.AluOpType

D = 2048
S = 2048
NH = 8
DH = 128
DFF = 5632
DIN = 10248
KC = D // 128
FC = DFF // 128
NB = S // 128
NCH = S // 512
EPS = 1e-6
SCALE = DH ** -0.5
COL_F = 6144
COL_GSB = 6152
COL_GFX = 8200
EPOCH = 30000


class Res:
    __slots__ = ("name", "w", "rs")

    def __init__(self, name):
        self.name = name
        self.w = None
        self.rs = {}


class DSem:
    __slots__ = ("sem", "count")

    def __init__(self, sem):
        self.sem = sem
        self.count = 0


class Eng:
    def __init__(self, kb, name):
        self.kb = kb
        self.name = name
        self.sem = kb.new_sem("e_" + name)
        self.own_sems = {self.sem}
        self.count = 0
        self.waited = {}
        self.prog = []

    def wait(self, ev):
        sem, val = ev
        if self.waited.get(sem, 0) >= val:
            return
        self.prog.append(("w", sem, val))
        self.waited[sem] = val

    def tick(self):
        if self.count >= EPOCH:
            self.sem = self.kb.new_sem("e_" + self.name)
            self.own_sems.add(self.sem)
            self.count = 0
        self.count += 1
        return (self.sem, self.count)

    def replay(self, eng):
        for it in self.prog:
            if it[0] == "w":
                eng.wait_ge(it[1], it[2])
            else:
                _, calls, sem, inc = it
                ins = None
                for meth, kw in calls:
                    ins = getattr(eng, meth)(**kw)
                    if inc == 16:
                        ins.then_inc(sem, 16)
                if inc == 1:
                    ins.then_inc(sem, 1)


class KB:
    def __init__(self, nc, stack):
        self.nc = nc
        self.stack = stack
        self.nsem = 0
        self.pe = Eng(self, "pe")
        self.act = Eng(self, "act")
        self.dve = Eng(self, "dve")
        self.pool = Eng(self, "pool")
        self.sp = Eng(self, "sp")

    def new_sem(self, name):
        self.nsem += 1
        return self.stack.enter_context(self.nc.semaphore(f"{name}_{self.nsem}"))

    def dsem(self, name):
        return DSem(self.new_sem("d_" + name))

    @staticmethod
    def _deps(reads, writes):
        evs = []
        for r in reads:
            if r.w is not None:
                evs.append(r.w)
        for r in writes:
            if r.w is not None:
                evs.append(r.w)
            for s, v in r.rs.items():
                evs.append((s, v))
        return evs

    @staticmethod
    def _commit(ev, reads, writes):
        s, v = ev
        for r in reads:
            if r.rs.get(s, 0) < v:
                r.rs[s] = v
        for r in writes:
            r.w = ev
            r.rs = {}

    def op(self, E, meth, kw, reads=(), writes=()):
        return self.group(E, [(meth, kw)], reads, writes)

    def group(self, E, calls, reads=(), writes=()):
        for ev in self._deps(reads, writes):
            if E is self.pe and ev[0] in E.own_sems:
                continue
            E.wait(ev)
        ev = E.tick()
        E.prog.append(("o", list(calls), ev[0], 1))
        self._commit(ev, reads, writes)
        return ev

    def dma(self, Q, ds, out, in_, reads=(), writes=()):
        return self.dma_multi(Q, ds, [(out, in_)], reads, writes)

    def dma_multi(self, Q, ds, pairs, reads=(), writes=()):
        for ev in self._deps(reads, writes):
            Q.wait(ev)
        Q.prog.append(("o", [("dma_start", dict(out=o, in_=i)) for o, i in pairs], ds.sem, 16))
        ds.count += 16 * len(pairs)
        ev = (ds.sem, ds.count)
        self._commit(ev, reads, writes)
        return ev


class Alloc:
    BASE = 16512
    LIMIT = 16512 + 212736

    def __init__(self, nc):
        self.nc = nc
        self.top = self.BASE
        self.n = 0
        self.live = []
        self.dead = []

    def mark(self):
        return self.top

    def release(self, m):
        keep = []
        for it in self.live:
            (self.dead if it[0] >= m else keep).append(it)
        self.live = keep
        self.top = m

    def kill(self, res):
        keep = []
        for it in self.live:
            (self.dead if it[2] is res else keep).append(it)
        self.live = keep

    def register(self, off, end, res):
        for (o, e, r) in self.dead:
            if o < end and off < e:
                evs = list(r.rs.items())
                if r.w is not None:
                    evs.append(r.w)
                for s_, v in evs:
                    if res.rs.get(s_, 0) < v:
                        res.rs[s_] = v
        self.live.append((off, end, res))

    def alloc(self, name, shape, dtype):
        nbytes = int(np.prod(shape[1:])) * (2 if dtype == BF16 else 4)
        off = (self.top + 63) // 64 * 64
        assert off + nbytes <= self.LIMIT, f"SBUF overflow allocating {name}: {off + nbytes - self.LIMIT} over"
        self.top = off + nbytes
        self.last_off = off
        self.last_end = off + nbytes
        self.n += 1
        return self.nc.alloc_sbuf_tensor_at(f"{name}_{self.n}", shape, dtype, offset=off)


class Filler:
    def __init__(self, items, nticks):
        self.items = list(items)
        self.left = max(1, nticks)
        self.credit = 0.0

    def tick(self):
        if self.items:
            self.credit += len(self.items) / max(1, self.left)
        self.left -= 1
        while self.credit >= 1.0 - 1e-9 and self.items:
            self.items.pop(0)()
            self.credit -= 1.0

    def burst(self, k):
        for _ in range(k):
            if self.items:
                self.items.pop(0)()

    def flush(self):
        while self.items:
            self.items.pop(0)()


class Slot:
    def __init__(self, kb, name, shape, dtype, dma=False):
        self.t = kb.al.alloc(name, shape, dtype)
        self.r = Res(name)
        kb.al.register(kb.al.last_off, kb.al.last_end, self.r)
        self.ds = kb.dsem(name) if dma else None
        self.kb = kb
        self.name = name
        self.off = kb.al.last_off

    def view(self, shape, dtype):
        kb = self.kb
        kb.al.n += 1
        return kb.nc.alloc_sbuf_tensor_at(f"{self.name}_v{kb.al.n}", shape, dtype, offset=self.off)


def build_nc(debug=None, upto=99):
    nc = bass.Bass("TRN2", target_bir_lowering=False)
    G = {}
    G["xT"] = nc.dram_tensor("xT", [D, S], F32, kind="ExternalInput").ap()
    G["w_in"] = nc.dram_tensor("w_in", [D, DIN], F32, kind="ExternalInput").ap()
    G["w_bsb"] = nc.dram_tensor("w_bsb", [NH * DH, D], F32, kind="ExternalInput").ap()
    G["w_bfx"] = nc.dram_tensor("w_bfx", [NH * DH, D], F32, kind="ExternalInput").ap()
    G["w_out"] = nc.dram_tensor("w_out", [D, D], F32, kind="ExternalInput").ap()
    G["w_fg"] = nc.dram_tensor("w_fg", [D, DFF], F32, kind="ExternalInput").ap()
    G["w_fu"] = nc.dram_tensor("w_fu", [D, DFF], F32, kind="ExternalInput").ap()
    G["w_fd"] = nc.dram_tensor("w_fd", [DFF, D], F32, kind="ExternalInput").ap()
    G["gains_d"] = nc.dram_tensor("gains", [128, 4 * KC], F32, kind="ExternalInput").ap()
    G["bf_d"] = nc.dram_tensor("bfg", [128, 128], F32, kind="ExternalInput").ap()
    G["cbf_d"] = nc.dram_tensor("cbf", [128, 7 * 128], BF16, kind="ExternalInput").ap()
    G["cf_d"] = nc.dram_tensor("cf32", [128, 2 * 128], F32, kind="ExternalInput").ap()
    G["outT"] = nc.dram_tensor("outT", [D, S], F32, kind="ExternalOutput").ap()
    G["OTs"] = nc.dram_tensor("OTs", [2 * NH, DH, S], BF16).ap()
    G["mixs"] = nc.dram_tensor("mixs", [D, S], F32).ap()
    G["hs"] = nc.dram_tensor("hs", [D, S], F32).ap()
    G["ffs"] = nc.dram_tensor("ffs", [D, S], F32).ap()
    G["dbg"] = None
    if debug is not None:
        G["dbg"] = {k: nc.dram_tensor("dbg_" + k, shp, dt, kind="ExternalOutput").ap()
                    for k, (shp, dt) in debug.items()}

    with ExitStack() as top:
        arena = top.enter_context(nc.sbuf_tensor("arena", [128, 212736], mybir.dt.uint8))
        kb = KB(nc, top)
        kb.al = Alloc(nc)
        banks = []
        for i in range(8):
            t = top.enter_context(nc.psum_tensor(f"ps{i}", [128, 512], F32))
            banks.append((t, Res(f"ps{i}")))
        _emit(nc, kb, banks, G, upto)
        with nc.Block() as block:
            @block.sync
            def _(e):
                kb.sp.replay(e)

            @block.tensor
            def _(e):
                kb.pe.replay(e)

            @block.scalar
            def _(e):
                kb.act.replay(e)

            @block.vector
            def _(e):
                kb.dve.replay(e)

            @block.gpsimd
            def _(e):
                kb.pool.replay(e)
    return nc


def _emit(nc, kb, banks, G, upto):
    PE, ACT, DVE, POOL, SP = kb.pe, kb.act, kb.dve, kb.pool, kb.sp
    al = kb.al
    xT, w_in, w_bsb, w_bfx, w_out = G["xT"], G["w_in"], G["w_bsb"], G["w_bfx"], G["w_out"]
    w_fg, w_fu, w_fd = G["w_fg"], G["w_fu"], G["w_fd"]
    gains_d, bf_d, cbf_d, cf_d, outT = G["gains_d"], G["bf_d"], G["cbf_d"], G["cf_d"], G["outT"]
    OTs, mixs, hs, ffs, dbg = G["OTs"], G["mixs"], G["hs"], G["ffs"], G["dbg"]
    final_events = []

    cbf = Slot(kb, "cbf", [128, 7 * 128], BF16, dma=True)
    cf = Slot(kb, "cf", [128, 2 * 128], F32, dma=True)
    gains = Slot(kb, "gains", [128, 4 * KC], F32, dma=True)
    bfg = Slot(kb, "bfg", [128, 128], F32, dma=True)
    kb.dma(SP, cbf.ds, cbf.t[:, :], cbf_d, writes=[cbf.r])
    kb.dma(SP, cf.ds, cf.t[:, :], cf_d, writes=[cf.r])
    kb.dma(SP, gains.ds, gains.t[:, :], gains_d, writes=[gains.r])
    kb.dma(SP, bfg.ds, bfg.t[:, :], bf_d, writes=[bfg.r])
    ones_bf = cbf.t[:, 0:128]
    negtri_bf = cbf.t[:, 128:256]
    negones_bf = cbf.t[:, 256:384]
    mstrict_bf = cbf.t[:, 384:512]
    mincl_bf = cbf.t[:, 512:640]
    zeros_bf = cbf.t[:, 640:768]
    ident_bf = cbf.t[:, 768:896]
    tri_f = cf.t[:, 0:128]
    ones_f = cf.t[:, 128:256]

    rstd2 = Slot(kb, "rstd2", [128, S], F32)
    rstdm = Slot(kb, "rstdm", [128, S], F32)
    m_u = al.mark()
    biasT = Slot(kb, "biasT", [128, NCH, NH, NB], F32)
    uT = Slot(kb, "uT", [128, KC, S], BF16)
    m_w = al.mark()
    wf32 = Slot(kb, "wf32", [128, KC, 8], F32, dma=True)
    wfb = Slot(kb, "wfb", [128, KC, 8], BF16)
    zf = Slot(kb, "zf", [128, 128], F32)
    tots = Slot(kb, "tots", [128, 128], F32)
    pre = Slot(kb, "pre", [128, 136], F32)
    cumL = Slot(kb, "cumL", [128, 128], F32)
    wst = [Slot(kb, f"wst{i}", [128, KC, 128], F32, dma=True) for i in range(3)]
    wbf = [[Slot(kb, f"wbf{s}_{j}", [128, KC, 128], BF16) for j in range(3)] for s in range(2)]

    def dump(name, src_ap, src_res, dst_ap=None):
        if dbg is None or name not in dbg:
            return
        ds = kb.dsem("dbg" + name)
        ev = kb.dma(SP, ds, dbg[name] if dst_ap is None else dst_ap, src_ap, reads=[src_res])
        final_events.append(ev)

    def finish():
        for ev in final_events:
            SP.wait(ev)

    def rstd_from_banks(bank_ids, dst_t, dst_r, col0, tmp):
        for i, bi in enumerate(bank_ids):
            bt, br = banks[bi]
            sl = slice(col0 + i * 512, col0 + (i + 1) * 512)
            kb.op(DVE, "tensor_scalar", dict(out=tmp.t[:, 0:512], in0=bt[:, :], scalar1=1.0 / D, scalar2=EPS,
                                             op0=ALU.mult, op1=ALU.add), writes=[br, tmp.r])
            kb.op(ACT, "activation", dict(out=tmp.t[:, 0:512], in_=tmp.t[:, 0:512], func=AF.Ln), writes=[tmp.r])
            kb.op(ACT, "activation", dict(out=dst_t[:, sl], in_=tmp.t[:, 0:512], func=AF.Exp, scale=-0.5),
                  reads=[tmp.r], writes=[dst_r])

    m0 = al.mark()
    xs = [Slot(kb, f"xs{i}", [128, S], F32, dma=True) for i in range(2)]
    sq = [Slot(kb, f"sq{i}", [128, S], BF16) for i in range(2)]
    rstd1 = Slot(kb, "rstd1", [128, S], F32)
    tmp = Slot(kb, "tmp1", [128, 512], F32)
    for c in range(KC):
        x = xs[c % 2]
        q = sq[c % 2]
        kb.dma(SP, x.ds, x.t[:, :], xT[c * 128:(c + 1) * 128, :], writes=[x.r])
        kb.op(ACT, "activation", dict(out=q.t[:, :], in_=x.t[:, :], func=AF.Square),
              reads=[x.r], writes=[q.r])
        for tc in range(NCH):
            bt, br = banks[tc]
            kb.op(PE, "matmul", dict(out=bt[:, :], lhsT=ones_bf, rhs=q.t[:, tc * 512:(tc + 1) * 512],
                                     start=(c == 0), stop=(c == KC - 1)),
                  reads=[q.r, cbf.r], writes=[br])
    rstd_from_banks([0, 1, 2, 3], rstd1.t, rstd1.r, 0, tmp)
    for c in range(KC):
        x = xs[c % 2]
        kb.dma(SP, x.ds, x.t[:, :], xT[c * 128:(c + 1) * 128, :], writes=[x.r])
        kb.op(DVE, "scalar_tensor_tensor", dict(out=uT.t[:, c, :], in0=x.t[:, :], scalar=gains.t[:, c:c + 1],
                                                in1=rstd1.t[:, :], op0=ALU.mult, op1=ALU.mult),
              reads=[x.r, gains.r, rstd1.r], writes=[uT.r])
    dump("rstd1", rstd1.t[:, :], rstd1.r)
    if dbg is not None and "uT" in dbg:
        dump("uT", uT.t[:, :, :], uT.r, dbg["uT"].rearrange("(c p) t -> p c t", p=128))
    al.release(m0)
    if upto <= 1:
        return finish()

    def stage2():
        m0 = al.mark()
        tots2, pre2, cumL2 = tots.t, pre.t, cumL.t
        kb.dma(SP, wf32.ds, wf32.t[:, :, :], w_in[:, COL_F:COL_F + 8].rearrange("(k p) c -> p k c", p=128),
               writes=[wf32.r])
        kb.op(DVE, "tensor_copy", dict(out=wfb.t[:, :, :], in_=wf32.t[:, :, :]), reads=[wf32.r], writes=[wfb.r])
        bF, rF = banks[4]
        for blk in range(NB):
            calls = [("matmul", dict(out=bF[:, blk * 8:(blk + 1) * 8], lhsT=uT.t[:, k, blk * 128:(blk + 1) * 128],
                                     rhs=wfb.t[:, k, :], start=(k == 0), stop=(k == KC - 1))) for k in range(KC)]
            kb.group(PE, calls, reads=[uT.r, wfb.r], writes=[rF])
        kb.op(DVE, "tensor_tensor", dict(out=zf.t[:, :], in0=bF[:, 0:128], in1=bfg.t[:, :], op=ALU.add),
              reads=[bfg.r], writes=[rF, zf.r])
        kb.op(ACT, "activation", dict(out=zf.t[:, :], in_=zf.t[:, :], func=AF.Exp, scale=-1.0), writes=[zf.r])
        kb.op(DVE, "tensor_scalar_add", dict(out=zf.t[:, :], in0=zf.t[:, :], scalar1=1.0), writes=[zf.r])
        kb.op(ACT, "activation", dict(out=zf.t[:, :], in_=zf.t[:, :], func=AF.Ln), writes=[zf.r])
        bW, rW = banks[5]
        bT, rT = banks[6]
        kb.op(PE, "matmul", dict(out=bW[:, 0:128], lhsT=tri_f, rhs=zf.t[:, :], start=True, stop=True),
              reads=[zf.r, cf.r], writes=[rW])
        kb.op(PE, "matmul", dict(out=bT[:, 0:128], lhsT=ones_f, rhs=zf.t[:, :], start=True, stop=True),
              reads=[zf.r, cf.r], writes=[rT])
        kb.op(DVE, "tensor_copy", dict(out=tots2[:, :], in_=bT[:, 0:128]), writes=[rT, tots.r])
        kb.op(DVE, "memset", dict(ap=pre2[:, 0:8], constant=0.0), writes=[pre.r])
        for blk in range(1, NB + 1):
            kb.op(DVE, "tensor_tensor", dict(out=pre2[:, blk * 8:(blk + 1) * 8], in0=pre2[:, (blk - 1) * 8:blk * 8],
                                             in1=tots2[:, (blk - 1) * 8:blk * 8], op=ALU.add),
                  reads=[tots.r], writes=[pre.r])
        kb.op(DVE, "tensor_tensor", dict(out=cumL2[:, :], in0=bW[:, 0:128], in1=pre2[:, 0:128], op=ALU.add),
              reads=[pre.r], writes=[rW, cumL.r])
        for c in range(NCH):
            for h in range(NH):
                pc = 4 * (c + 1) * 8 + h
                kb.op(DVE, "tensor_scalar", dict(out=biasT.t[:, c, h, :], in0=cumL.t[:, h:128:8],
                                                 scalar1=pre.t[:, pc:pc + 1], scalar2=None,
                                                 op0=ALU.subtract),
                      reads=[cumL.r, pre.r], writes=[biasT.r])
        dump("cumL", cumL2[:, :], cumL.r)
        if dbg is not None and "biasT" in dbg:
            dump("biasT", biasT.t[:, :, :, :].rearrange("p a b c -> p (a b c)"), biasT.r)
        al.release(m0)


    m0 = al.mark()
    qkv = []
    for par in range(2):
        q_ = Slot(kb, f"qT{par}", [128, S], BF16)
        k_ = Slot(kb, f"kT{par}", [128, S], BF16)
        v_ = Slot(kb, f"V{par}", [128, NB, 128], BF16)
        qkv.append((q_, k_, v_, None))
    VT = Slot(kb, "VT", [128, S], BF16)
    Eb = [Slot(kb, f"E{i}", [128, 512], F32) for i in range(2)]
    SPb = [Slot(kb, f"SP{i}", [128, 512], BF16) for i in range(4)]
    Wb = [Slot(kb, f"W{i}", [128, 512], BF16) for i in range(4)]
    Rb = [[Slot(kb, f"R{p}_{i}", [128, 512], BF16) for i in range(2)] for p in range(2)]
    rec = Slot(kb, "rec", [128, 512], F32)
    ocp = Slot(kb, "ocp", [128, 512], F32)
    Ost = [Slot(kb, f"Ost{i}", [128, 512], BF16, dma=True) for i in range(2)]
    OT_res = [[Res(f"OT{hh}_{c}") for c in range(NCH)] for hh in range(2 * NH)]
    cnt = {"w": 0, "p": 0, "z": 0, "o": 0}
    zb = [banks[2], banks[3], banks[4], banks[5]]
    pb = [banks[0], banks[1]]

    def load_head(hh, ws):
        br, h = divmod(hh, NH)
        casts = []
        for j in range(3):
            col = br * 3072 + j * 1024 + h * 128
            sl = wst[cnt["w"] % 3]
            cnt["w"] += 1
            kb.dma(SP, sl.ds, sl.t[:, :, :], w_in[:, col:col + 128].rearrange("(k p) c -> p k c", p=128),
                   writes=[sl.r])
            dst = wbf[ws][j]

            def cast(sl=sl, dst=dst):
                kb.op(DVE, "tensor_copy", dict(out=dst.t[:, :, :], in_=sl.t[:, :, :]), reads=[sl.r], writes=[dst.r])
            casts.append(cast)
        return casts

    def proj_items(ws, par):
        wq, wk, wv = wbf[ws]
        qT, kT, V, V2 = qkv[par]
        items = []

        def qk_item(wsl, dst, tc, sc):
            st_ = {}

            def f0():
                bt, br_ = pb[cnt["p"] % 2]
                cnt["p"] += 1
                st_["b"] = (bt, br_)
                kb.group(PE, [("matmul", dict(out=bt[:, :], lhsT=wsl.t[:, k, :],
                                              rhs=uT.t[:, k, tc * 512:(tc + 1) * 512],
                                              start=(k == 0), stop=False)) for k in range(KC // 2)],
                         reads=[wsl.r, uT.r], writes=[br_])

            def f1():
                bt, br_ = st_["b"]
                kb.group(PE, [("matmul", dict(out=bt[:, :], lhsT=wsl.t[:, k, :],
                                              rhs=uT.t[:, k, tc * 512:(tc + 1) * 512],
                                              start=False, stop=(k == KC - 1))) for k in range(KC // 2, KC)],
                         reads=[wsl.r, uT.r], writes=[br_])
                kb.op(DVE, "tensor_scalar", dict(out=dst.t[:, tc * 512:(tc + 1) * 512], in0=bt[:, :],
                                                 scalar1=sc, scalar2=None, op0=ALU.mult),
                      writes=[br_, dst.r])
            return [f0, f1]

        def vt_items():
            its = []
            for tc in range(NCH):
                its += qk_item(wv, VT, tc, 1.0)

            def tr(g):
                def f():
                    bt, br_ = pb[cnt["p"] % 2]
                    cnt["p"] += 1
                    btb = bt[:, :].bitcast(BF16)
                    kb.group(PE, [("transpose", dict(out=btb[:, j * 128:(j + 1) * 128],
                                                     in_=VT.t[:, (8 * g + j) * 128:(8 * g + j + 1) * 128],
                                                     identity=ident_bf)) for j in range(8)],
                             reads=[VT.r, cbf.r], writes=[br_])
                    kb.op(DVE, "tensor_copy", dict(out=V.t[:, 8 * g:8 * g + 8, :].rearrange("p a b -> p (a b)"),
                                                   in_=btb[:, 0:1024]), writes=[br_, V.r])
                return f
            its += [tr(0), tr(1)]
            return its

        for tc in range(NCH):
            items += qk_item(wk, kT, tc, 1.0)
        items += vt_items()
        for tc in range(NCH):
            items += qk_item(wq, qT, tc, SCALE)
        return items

    def store_O(hh, c, ost):
        kb.dma(SP, ost.ds, OTs[hh, :, c * 512:(c + 1) * 512], ost.t[:, :], reads=[ost.r],
               writes=[OT_res[hh][c]])

    def attn_sb(hh, par, filler):
        qT, kT, V, V2 = qkv[par]
        seq = []
        for c in range(NCH):
            blocks = list(range(4 * c + 3, -1, -1))
            for j, b in enumerate(blocks):
                seq.append((c, j, b, len(blocks)))
        N = len(seq)
        stt = {}
        ste = {}

        def A_pe(g):
            c, j, b, n = seq[g]
            o = max(0, 128 * (b - 4 * c))
            diag = b >= 4 * c
            zi = cnt["z"]
            cnt["z"] += 1
            bz, rz = zb[zi % 4]
            e, sp = Eb[zi % 2], SPb[zi % 4]
            stt[g] = (o, diag, bz, rz, sp, Wb[zi % 4])
            ste[g] = e
            if j == 0:
                bO, rO = banks[6 + c % 2]
                kb.op(PE, "matmul", dict(out=bO[:, :], lhsT=zeros_bf, rhs=uT.t[:, 0, 0:512], start=True, stop=True),
                      reads=[cbf.r, uT.r], writes=[rO])
                for R_ in Rb[c % 2]:
                    kb.op(POOL, "memset", dict(ap=R_.t[:, :], constant=0.0), writes=[R_.r])
            kb.op(PE, "matmul", dict(out=bz[:, o:512], lhsT=kT.t[:, b * 128:(b + 1) * 128],
                                     rhs=qT.t[:, c * 512 + o:(c + 1) * 512], start=True, stop=True),
                  reads=[kT.r, qT.r], writes=[rz])

        def A_act(g):
            o, diag, bz, rz, sp, w = stt[g]
            e = ste.pop(g)
            kb.op(ACT, "activation", dict(out=e.t[:, o:512], in_=bz[:, o:512], func=AF.Exp),
                  writes=[rz, e.r])
            kb.op(ACT, "activation", dict(out=sp.t[:, o:512], in_=e.t[:, o:512], func=AF.Ln,
                                          bias=ones_f[:, 0:1]),
                  reads=[e.r, cf.r], writes=[sp.r])
            if diag:
                kb.op(POOL, "tensor_tensor", dict(out=sp.t[:, o:o + 128], in0=sp.t[:, o:o + 128],
                                                  in1=mstrict_bf, op=ALU.mult),
                      reads=[cbf.r], writes=[sp.r])

        def B1(g):
            c, j, b, n = seq[g]
            o, diag, bz, rz, sp, w = stt[g]
            first = (j == 0)
            R = Rb[c % 2][j % 2]
            calls = [("matmul", dict(out=bz[:, o:512], lhsT=negtri_bf, rhs=sp.t[:, o:512],
                                     start=False, stop=first, skip_group_check=True))]
            if not first:
                calls.append(("matmul", dict(out=bz[:, o:512], lhsT=negones_bf, rhs=R.t[:, o:512],
                                             start=False, stop=True, skip_group_check=True)))
            kb.group(PE, calls, reads=[sp.r, R.r, cbf.r], writes=[rz])
            kb.op(ACT, "activation", dict(out=w.t[:, o:512], in_=bz[:, o:512], func=AF.Exp),
                  writes=[rz, w.r])
            if diag:
                kb.op(POOL, "tensor_tensor", dict(out=w.t[:, o:o + 128], in0=w.t[:, o:o + 128],
                                                  in1=mstrict_bf, op=ALU.mult),
                      reads=[cbf.r], writes=[w.r])
            if j < n - 1:
                Rn = Rb[c % 2][(j + 1) % 2]
                kb.op(DVE, "tensor_tensor", dict(out=Rn.t[:, o:512], in0=R.t[:, o:512], in1=sp.t[:, o:512],
                                                 op=ALU.add), reads=[sp.r, R.r], writes=[Rn.r])

        def B2(g):
            c, j, b, n = seq[g]
            o, diag, bz, rz, sp, w = stt.pop(g)
            bO, rO = banks[6 + c % 2]
            kb.op(PE, "matmul", dict(out=bO[:, o:512], lhsT=V.t[:, b, :], rhs=w.t[:, o:512],
                                     start=False, stop=(j == n - 1), skip_group_check=True),
                  reads=[V.r, w.r], writes=[rO])
            if j == n - 1:
                ost = Ost[cnt["o"] % 2]
                cnt["o"] += 1
                kb.op(DVE, "tensor_copy", dict(out=ost.t[:, :], in_=bO[:, :]), writes=[rO, ost.r])
                store_O(hh, c, ost)

        for g in range(min(3, N)):
            A_pe(g)
        for g in range(min(2, N)):
            A_act(g)
        filler.burst(2)
        for g in range(N):
            if g + 2 < N:
                A_act(g + 2)
            B1(g)
            if g >= 1:
                B2(g - 1)
            if g + 3 < N:
                A_pe(g + 3)
            filler.tick()
        filler.burst(1)
        B2(N - 1)

    def attn_fox(hh, par, filler):
        qT, kT, V, V2 = qkv[par]
        h = hh % NH
        seq = []
        for c in range(NCH):
            for b in range(4 * c + 4):
                seq.append((c, b, 4 * c + 4))
        N = len(seq)
        stt = {}
        zb2 = [banks[2], banks[3]]

        def A_pe(g):
            c, b, n = seq[g]
            o = max(0, 128 * (b - 4 * c))
            zi = cnt["z"]
            cnt["z"] += 1
            bz, rz = zb2[zi % 2]
            stt[g] = (o, bz, rz, Wb[zi % 4])
            if b == 0:
                (bO, rO), (bD, rD) = banks[4 + 2 * (c % 2)], banks[5 + 2 * (c % 2)]
                kb.group(PE, [("matmul", dict(out=bO[:, :], lhsT=zeros_bf, rhs=uT.t[:, 0, 0:512], start=True, stop=True)),
                              ("matmul", dict(out=bD[:, :], lhsT=zeros_bf, rhs=uT.t[:, 0, 0:512], start=True, stop=True))],
                         reads=[cbf.r, uT.r], writes=[rO, rD])
            kb.op(PE, "matmul", dict(out=bz[:, o:512], lhsT=kT.t[:, b * 128:(b + 1) * 128],
                                     rhs=qT.t[:, c * 512 + o:(c + 1) * 512], start=True, stop=True),
                  reads=[kT.r, qT.r], writes=[rz])

        def A_act(g):
            c, b, n = seq[g]
            o, bz, rz, p = stt[g]
            kb.op(ACT, "activation", dict(out=p.t[:, o:512], in_=bz[:, o:512], func=AF.Exp,
                                          bias=biasT.t[:, c, h, b:b + 1]),
                  reads=[biasT.r], writes=[rz, p.r])
            if b >= 4 * c:
                kb.op(POOL, "tensor_tensor", dict(out=p.t[:, o:o + 128], in0=p.t[:, o:o + 128],
                                                  in1=mincl_bf, op=ALU.mult),
                      reads=[cbf.r], writes=[p.r])

        def B(g):
            c, b, n = seq[g]
            o, bz, rz, p = stt.pop(g)
            (bO, rO), (bD, rD) = banks[4 + 2 * (c % 2)], banks[5 + 2 * (c % 2)]
            last = (b == n - 1)
            kb.group(PE, [("matmul", dict(out=bO[:, o:512], lhsT=V.t[:, b, :], rhs=p.t[:, o:512],
                                          start=False, stop=last, skip_group_check=True)),
                          ("matmul", dict(out=bD[:, o:512], lhsT=ones_bf, rhs=p.t[:, o:512],
                                          start=False, stop=last, skip_group_check=True))],
                     reads=[V.r, p.r, cbf.r], writes=[rO, rD])
            if last:
                ost = Ost[cnt["o"] % 2]
                cnt["o"] += 1
                kb.op(ACT, "activation", dict(out=rec.t[:, :], in_=bD[:, :], func=AF.Copy), writes=[rD, rec.r])
                kb.op(DVE, "tensor_copy", dict(out=ocp.t[:, :], in_=bO[:, :]), writes=[rO, ocp.r])
                kb.op(DVE, "reciprocal", dict(out=rec.t[:, :], in_=rec.t[:, :]), writes=[rec.r])
                kb.op(DVE, "tensor_tensor", dict(out=ost.t[:, :], in0=ocp.t[:, :], in1=rec.t[:, :], op=ALU.mult),
                      reads=[rec.r, ocp.r], writes=[ost.r])
                store_O(hh, c, ost)

        for g in range(min(2, N)):
            A_pe(g)
        A_act(0)
        filler.burst(1)
        for g in range(N):
            if g + 1 < N:
                A_act(g + 1)
            B(g)
            if g + 2 < N:
                A_pe(g + 2)
            filler.tick()

    heads = list(range(2 * NH)) if upto >= 4 else G.get("dbg_heads", [0, 8])
    for cst in load_head(heads[0], 0):
        cst()
    if len(heads) > 1:
        for cst in load_head(heads[1], 1):
            cst()
    for it in proj_items(0, 0):
        it()
    stage2()
    for i, hh in enumerate(heads):
        items = proj_items((i + 1) % 2, (i + 1) % 2) if i + 1 < len(heads) else []
        if i + 2 < len(heads):
            casts = load_head(heads[i + 2], i % 2)
            items[8:8] = casts
        filler = Filler(items, 40)
        if hh < NH:
            attn_sb(hh, i % 2, filler)
        else:
            attn_fox(hh, i % 2, filler)
        filler.flush()
    if dbg is not None and "OT" in dbg:
        ds = kb.dsem("dbgOT")
        allr = [OT_res[hh][c] for hh in heads for c in range(NCH)]
        for hh in heads:
            ev = kb.dma(SP, ds, dbg["OT"][hh], OTs[hh], reads=allr)
        final_events.append(ev)
    al.release(m_w)
    if upto <= 3:
        return finish()

    TH = 1024
    NQ = S // 512
    mix_res = [[Res(f"mix{c}_{q}") for q in range(NQ)] for c in range(KC)]
    h_res = [[Res(f"h{c}_{q}") for q in range(NQ)] for c in range(KC)]
    ff_res = [[Res(f"ff{c}_{q}") for q in range(NQ)] for c in range(KC)]
    nofill = Filler([], 1)

    def stat_mm(bi, sq_ap, sq_res, c):
        sb_t, sb_r = banks[bi]
        kb.op(PE, "matmul", dict(out=sb_t[:, :], lhsT=ones_bf, rhs=sq_ap, start=(c == 0), stop=(c == KC - 1)),
              reads=[sq_res, cbf.r], writes=[sb_r])

    def wview(w, c0, kc):
        return w[:, c0:c0 + 128].rearrange("(k p) c -> p k c", p=128)

    def hpass_items(hf, sbank0):
        xh = [Slot(kb, f"xh{i}", [128, 512], F32, dma=True) for i in range(2)]
        mh = [Slot(kb, f"mh{i}", [128, 512], F32, dma=True) for i in range(2)]
        hst = [Slot(kb, f"hst{i}", [128, 512], F32, dma=True) for i in range(2)]
        sqh = [Slot(kb, f"sqh{i}", [128, 512], BF16) for i in range(3)]
        pend = []
        items = []

        def mk(c, tc, n):
            def f():
                q = 2 * hf + tc
                tsl = slice(q * 512, (q + 1) * 512)
                xx, mm, hh_, qq = xh[n % 2], mh[n % 2], hst[n % 2], sqh[n % 3]
                kb.dma(SP, xx.ds, xx.t[:, :], xT[c * 128:(c + 1) * 128, tsl], writes=[xx.r])
                kb.dma(SP, mm.ds, mm.t[:, :], mixs[c * 128:(c + 1) * 128, tsl], reads=[mix_res[c][q]], writes=[mm.r])
                kb.op(DVE, "scalar_tensor_tensor", dict(out=hh_.t[:, :], in0=mm.t[:, :], scalar=gains.t[:, KC + c:KC + c + 1],
                                                        in1=rstdm.t[:, tsl], op0=ALU.mult, op1=ALU.mult),
                      reads=[mm.r, gains.r, rstdm.r], writes=[hh_.r])
                kb.op(POOL, "tensor_tensor", dict(out=hh_.t[:, :], in0=hh_.t[:, :], in1=xx.t[:, :], op=ALU.add),
                      reads=[xx.r], writes=[hh_.r])
                kb.dma(POOL, hh_.ds, hs[c * 128:(c + 1) * 128, tsl], hh_.t[:, :], reads=[hh_.r], writes=[h_res[c][q]])
                kb.op(ACT, "activation", dict(out=qq.t[:, :], in_=hh_.t[:, :], func=AF.Square),
                      reads=[hh_.r], writes=[qq.r])
                if pend:
                    stat_mm(*pend.pop(0))
                pend.append((sbank0 + tc, qq.t[:, :], qq.r, c))
            return f

        n = 0
        for c in range(KC):
            for tc in range(2):
                items.append(mk(c, tc, n))
                n += 1

        def fin():
            while pend:
                stat_mm(*pend.pop(0))
        return items, fin

    def u2_items(hf, u2, nbuf=2):
        hl = [Slot(kb, f"hl{i}", [128, 512], F32, dma=True) for i in range(nbuf)]
        items = []

        def mk(c, tc, n):
            def f():
                q = 2 * hf + tc
                tsl = slice(q * 512, (q + 1) * 512)
                x = hl[n % nbuf]
                kb.dma(SP, x.ds, x.t[:, :], hs[c * 128:(c + 1) * 128, tsl], reads=[h_res[c][q]], writes=[x.r])
                kb.op(DVE, "scalar_tensor_tensor", dict(out=u2.t[:, c, tc * 512:(tc + 1) * 512], in0=x.t[:, :],
                                                        scalar=gains.t[:, 2 * KC + c:2 * KC + c + 1],
                                                        in1=rstd2.t[:, tsl], op0=ALU.mult, op1=ALU.mult),
                      reads=[x.r, gains.r, rstd2.r], writes=[u2.r])
            return f

        n = 0
        for c in range(KC):
            for tc in range(2):
                items.append(mk(c, tc, n))
                n += 1
        return items

    def final_items(hf, rstd3, tcs=(0, 1), nbuf=2):
        hl = [Slot(kb, f"hl2_{i}", [128, 512], F32, dma=True) for i in range(nbuf)]
        fl = [Slot(kb, f"fl{i}", [128, 512], F32, dma=True) for i in range(nbuf)]
        ot = [Slot(kb, f"ot{i}", [128, 512], F32, dma=True) for i in range(nbuf)]
        items = []

        def mk(c, tc, n):
            def f():
                q = 2 * hf + tc
                tsl = slice(q * 512, (q + 1) * 512)
                hh_, ff_, oo = hl[n % nbuf], fl[n % nbuf], ot[n % nbuf]
                kb.dma(SP, hh_.ds, hh_.t[:, :], hs[c * 128:(c + 1) * 128, tsl], reads=[h_res[c][q]], writes=[hh_.r])
                kb.dma(SP, ff_.ds, ff_.t[:, :], ffs[c * 128:(c + 1) * 128, tsl], reads=[ff_res[c][q]], writes=[ff_.r])
                kb.op(DVE, "scalar_tensor_tensor", dict(out=oo.t[:, :], in0=ff_.t[:, :],
                                                        scalar=gains.t[:, 3 * KC + c:3 * KC + c + 1],
                                                        in1=rstd3.t[:, tc * 512:(tc + 1) * 512], op0=ALU.mult, op1=ALU.mult),
                      reads=[ff_.r, gains.r, rstd3.r], writes=[oo.r])
                kb.op(POOL, "tensor_tensor", dict(out=oo.t[:, :], in0=oo.t[:, :], in1=hh_.t[:, :], op=ALU.add),
                      reads=[hh_.r], writes=[oo.r])
                ev = kb.dma(POOL, oo.ds, outT[c * 128:(c + 1) * 128, tsl], oo.t[:, :], reads=[oo.r])
                final_events.append(ev)
            return f

        n = 0
        for c in range(KC):
            for tc in tcs:
                items.append(mk(c, tc, n))
                n += 1
        return items

    def P5(hf, merged):
        t0 = hf * TH
        OTh = Slot(kb, "OTh", [128, 2 * NH, TH], BF16, dma=True)
        wst = [Slot(kb, f"wstA{i}", [128, KC, 128], F32, dma=True) for i in range(2)]
        wg = [[Slot(kb, f"wgA{i}_{j}", [128, KC, 128], BF16) for j in range(3)] for i in range(2)]
        s1 = [Slot(kb, f"s1_{i}", [128, 512], F32) for i in range(2)]
        s2 = [Slot(kb, f"s2_{i}", [128, 512], F32) for i in range(2)]
        m1 = Slot(kb, "m1", [128, 512], F32)
        m2 = Slot(kb, "m2", [128, 512], F32)
        kb.dma(SP, OTh.ds, OTh.t[:, :, :], OTs[:, :, t0:t0 + TH].rearrange("h p t -> p h t"),
               reads=[OT_res[hh][2 * hf + i] for hh in range(2 * NH) for i in range(2)], writes=[OTh.r])
        cntA = {"w": 0, "b": 0}

        fillsA = lambda c: [[(w_in, COL_GSB + c * 128, KC, 0)], [(w_in, COL_GFX + c * 128, KC, 0)],
                            [(w_bsb, c * 128, NH, 0), (w_bfx, c * 128, NH, NH)]]
        pendA = {}

        def fillA(c, j):
            sl = wst[cntA["w"] % 2]
            cntA["w"] += 1
            kb.dma_multi(SP, sl.ds, [(sl.t[:, k0:k0 + kc, :], wview(w, c0, kc)) for (w, c0, kc, k0) in fillsA(c)[j]],
                         writes=[sl.r])
            pendA[(c, j)] = sl

        def castA(c, j):
            sl = pendA.pop((c, j))
            dst = wg[c % 2][j]
            kb.op(DVE, "tensor_copy", dict(out=dst.t[:, :, :], in_=sl.t[:, :, :]), reads=[sl.r], writes=[dst.r])

        fillA(0, 0)
        fillA(0, 1)
        castA(0, 0)
        castA(0, 1)
        fillA(0, 2)
        castA(0, 2)
        for c in range(KC):
            wgs, wgf, wb = wg[c % 2]
            if c + 1 < KC:
                fillA(c + 1, 0)
                fillA(c + 1, 1)
            for tc in range(2):
                bset = [banks[4 * (cntA["b"] % 2) + i] for i in range(4)]
                cntA["b"] += 1
                tsl = slice(t0 + tc * 512, t0 + (tc + 1) * 512)
                hsl = slice(tc * 512, (tc + 1) * 512)
                for (bt, br_), wsl in ((bset[0], wgs), (bset[1], wgf)):
                    kb.group(PE, [("matmul", dict(out=bt[:, :], lhsT=wsl.t[:, k, :], rhs=uT.t[:, k, tsl],
                                                  start=(k == 0), stop=(k == KC - 1))) for k in range(KC)],
                             reads=[wsl.r, uT.r], writes=[br_])
                for (bt, br_), k0 in ((bset[2], 0), (bset[3], NH)):
                    kb.group(PE, [("matmul", dict(out=bt[:, :], lhsT=wb.t[:, k0 + k, :], rhs=OTh.t[:, k0 + k, hsl],
                                                  start=(k == 0), stop=(k == NH - 1))) for k in range(NH)],
                             reads=[wb.r, OTh.r], writes=[br_])
                a1, a2 = s1[tc], s2[tc]
                kb.op(ACT, "activation", dict(out=a1.t[:, :], in_=bset[0][0][:, :], func=AF.Sigmoid),
                      writes=[bset[0][1], a1.r])
                kb.op(ACT, "activation", dict(out=a2.t[:, :], in_=bset[1][0][:, :], func=AF.Sigmoid),
                      writes=[bset[1][1], a2.r])
                kb.op(DVE, "tensor_tensor", dict(out=m1.t[:, :], in0=bset[2][0][:, :], in1=a1.t[:, :], op=ALU.mult),
                      reads=[a1.r], writes=[bset[2][1], m1.r])
                kb.op(DVE, "tensor_tensor", dict(out=m2.t[:, :], in0=bset[3][0][:, :], in1=a2.t[:, :], op=ALU.mult),
                      reads=[a2.r], writes=[bset[3][1], m2.r])
                kb.op(POOL, "tensor_tensor", dict(out=merged.t[:, c, hsl], in0=m1.t[:, :], in1=m2.t[:, :], op=ALU.add),
                      reads=[m1.r, m2.r], writes=[merged.r])
                if c + 1 < KC:
                    if tc == 0:
                        castA(c + 1, 0)
                        castA(c + 1, 1)
                        fillA(c + 1, 2)
                    else:
                        castA(c + 1, 2)

    def WOUT(hf, merged, filler):
        t0 = hf * TH
        wst = [Slot(kb, f"wstB{i}", [128, KC, 128], F32, dma=True) for i in range(3)]
        wo = [Slot(kb, f"wo{i}", [128, KC, 128], BF16) for i in range(2)]
        mixst = [Slot(kb, f"mixst{i}", [128, 512], F32, dma=True) for i in range(2)]
        sqm = [Slot(kb, f"sqm{i}", [128, 512], BF16) for i in range(3)]
        tmpB = Slot(kb, "tmpB", [128, 512], F32)
        cntB = {"b": 0, "m": 0}
        pend = []

        def fillB(c):
            sl = wst[c % 3]
            kb.dma(SP, sl.ds, sl.t[:, :, :], wview(w_out, c * 128, KC), writes=[sl.r])

        def castB(c):
            sl = wst[c % 3]
            kb.op(DVE, "tensor_copy", dict(out=wo[c % 2].t[:, :, :], in_=sl.t[:, :, :]), reads=[sl.r], writes=[wo[c % 2].r])

        fillB(0)
        castB(0)
        if KC > 1:
            fillB(1)
        for c in range(KC):
            if c + 2 < KC:
                fillB(c + 2)
            for tc in range(2):
                if tc == 1 and c + 1 < KC:
                    castB(c + 1)
                bt, br_ = banks[cntB["b"] % 4]
                cntB["b"] += 1
                hsl = slice(tc * 512, (tc + 1) * 512)
                kb.group(PE, [("matmul", dict(out=bt[:, :], lhsT=wo[c % 2].t[:, k, :], rhs=merged.t[:, k, hsl],
                                              start=(k == 0), stop=(k == KC - 1))) for k in range(KC)],
                         reads=[wo[c % 2].r, merged.r], writes=[br_])
                ms = mixst[cntB["m"] % 2]
                sm = sqm[cntB["m"] % 3]
                cntB["m"] += 1
                kb.op(ACT, "activation", dict(out=ms.t[:, :], in_=bt[:, :], func=AF.Copy), writes=[br_, ms.r])
                kb.dma(ACT, ms.ds, mixs[c * 128:(c + 1) * 128, t0 + tc * 512:t0 + (tc + 1) * 512], ms.t[:, :],
                       reads=[ms.r], writes=[mix_res[c][2 * hf + tc]])
                kb.op(ACT, "activation", dict(out=sm.t[:, :], in_=bt[:, :], func=AF.Square), writes=[br_, sm.r])
                pend.append((6 + tc, sm.t[:, :], sm.r, c))
                if len(pend) > 1:
                    stat_mm(*pend.pop(0))
            filler.tick()
        while pend:
            stat_mm(*pend.pop(0))
        rstd_from_banks([6, 7], rstdm.t, rstdm.r, t0, tmpB)
        return tmpB

    mA = al.mark()
    merged = Slot(kb, "merged", [128, KC, TH], BF16)
    mB = al.mark()
    P5(0, merged)
    al.release(mB)
    WOUT(0, merged, nofill)
    al.release(mB)
    P5(1, merged)
    al.release(mB)
    hp_items, hp_fin = hpass_items(0, 4)
    fill = Filler(hp_items, KC)
    tmpB = WOUT(1, merged, fill)
    fill.flush()
    hp_fin()
    rstd_from_banks([4, 5], rstd2.t, rstd2.r, 0, tmpB)
    if dbg is not None and "rstdm" in dbg:
        dump("rstdm", rstdm.t[:, 0:1024], rstdm.r)
    al.release(mA)

    al.release(m_u)
    u2 = Slot(kb, "u2", [128, KC, TH], BF16)
    aT = Slot(kb, "aT", [128, FC, TH], BF16)
    rstd3_ = Slot(kb, "rstd3", [128, TH], F32)
    rstd3 = [rstd3_, rstd3_]
    tmpD = Slot(kb, "tmpD", [128, 512], F32)
    mB = al.mark()

    def GATEUP(hf, filler):
        wst = [Slot(kb, f"wstC{i}", [128, KC, 128], F32, dma=True) for i in range(3)]
        wgu = [[Slot(kb, f"wgu{i}_{j}", [128, KC, 128], BF16) for j in range(2)] for i in range(2)]
        sg = [Slot(kb, f"sg{i}", [128, 512], F32) for i in range(2)]
        cntC = {"w": 0, "b": 0}

        pendC = {}

        def fillC(f):
            for j, w in enumerate((w_fg, w_fu)):
                sl = wst[cntC["w"] % 3]
                cntC["w"] += 1
                kb.dma(SP, sl.ds, sl.t[:, :, :], wview(w, f * 128, KC), writes=[sl.r])
                pendC[(f, j)] = sl

        def castC(f):
            for j in range(2):
                sl = pendC.pop((f, j))
                dst = wgu[f % 2][j]
                kb.op(DVE, "tensor_copy", dict(out=dst.t[:, :, :], in_=sl.t[:, :, :]), reads=[sl.r], writes=[dst.r])

        fillC(0)
        castC(0)
        for f in range(FC):
            if f + 1 < FC:
                fillC(f + 1)
            wgt, wup = wgu[f % 2]
            for tc in range(2):
                if tc == 1 and f + 1 < FC:
                    castC(f + 1)
                pbi = cntC["b"] % 3
                cntC["b"] += 1
                (bg, rg), (bu, ru) = banks[2 * pbi], banks[2 * pbi + 1]
                hsl = slice(tc * 512, (tc + 1) * 512)
                kb.group(PE, [("matmul", dict(out=bg[:, :], lhsT=wgt.t[:, k, :], rhs=u2.t[:, k, hsl],
                                              start=(k == 0), stop=(k == KC - 1))) for k in range(KC)],
                         reads=[wgt.r, u2.r], writes=[rg])
                kb.group(PE, [("matmul", dict(out=bu[:, :], lhsT=wup.t[:, k, :], rhs=u2.t[:, k, hsl],
                                              start=(k == 0), stop=(k == KC - 1))) for k in range(KC)],
                         reads=[wup.r, u2.r], writes=[ru])
                sgt = sg[tc]
                kb.op(ACT, "activation", dict(out=sgt.t[:, :], in_=bg[:, :], func=AF.Silu), writes=[rg, sgt.r])
                kb.op(DVE, "tensor_tensor", dict(out=aT.t[:, f, hsl], in0=bu[:, :], in1=sgt.t[:, :], op=ALU.mult),
                      reads=[sgt.r], writes=[ru, aT.r])
            filler.tick()

    def DOWN(hf, filler, tc_outer=False):
        t0 = hf * TH
        wst = [Slot(kb, f"wstD{i}", [128, KC, 128], F32, dma=True) for i in range(3)]
        wd = [Slot(kb, f"wd{i}", [128, FC, 128], BF16) for i in range(2)]
        ffst = [Slot(kb, f"ffst{i}", [128, 512], F32, dma=True) for i in range(2)]
        sqf = [Slot(kb, f"sqf{i}", [128, 512], BF16) for i in range(3)]
        cntD = {"w": 0, "b": 0, "m": 0, "l": 0}
        pend = []
        wfd_v = w_fd.rearrange("(f p) c -> p f c", p=128)

        pendD = []

        def fillD(c):
            for f0 in (0, 16, 32):
                nf = min(16, FC - f0)
                sl = wst[cntD["w"] % 3]
                cntD["w"] += 1
                kb.dma(SP, sl.ds, sl.t[:, 0:nf, :], wfd_v[:, f0:f0 + nf, c * 128:(c + 1) * 128], writes=[sl.r])
                pendD.append((sl, f0, nf))

        def castD():
            wdl = wd[cntD["l"] % 2]
            while pendD:
                sl, f0, nf = pendD.pop(0)
                kb.op(DVE, "tensor_copy", dict(out=wdl.t[:, f0:f0 + nf, :], in_=sl.t[:, 0:nf, :]),
                      reads=[sl.r], writes=[wdl.r])
            cntD["l"] += 1

        def loadD(c):
            fillD(c)
            castD()

        steps = ([(c, (0, 1)) for c in range(KC)] if not tc_outer
                 else [(c, (0,)) for c in range(KC)] + [(c, (1,)) for c in range(KC)])
        loadD(steps[0][0])
        for si, (c, tcs) in enumerate(steps):
            if si + 1 < len(steps):
                fillD(steps[si + 1][0])
            wdc = wd[si % 2]
            for ti, tc in enumerate(tcs):
                if ti == len(tcs) - 1 and si + 1 < len(steps):
                    castD()
                bt, br_ = banks[cntD["b"] % 4]
                cntD["b"] += 1
                hsl = slice(tc * 512, (tc + 1) * 512)
                kb.group(PE, [("matmul", dict(out=bt[:, :], lhsT=wdc.t[:, f, :], rhs=aT.t[:, f, hsl],
                                              start=(f == 0), stop=(f == FC - 1))) for f in range(FC)],
                         reads=[wdc.r, aT.r], writes=[br_])
                fs = ffst[cntD["m"] % 2]
                sf = sqf[cntD["m"] % 3]
                cntD["m"] += 1
                kb.op(ACT, "activation", dict(out=fs.t[:, :], in_=bt[:, :], func=AF.Copy), writes=[br_, fs.r])
                kb.dma(ACT, fs.ds, ffs[c * 128:(c + 1) * 128, t0 + tc * 512:t0 + (tc + 1) * 512], fs.t[:, :],
                       reads=[fs.r], writes=[ff_res[c][2 * hf + tc]])
                kb.op(ACT, "activation", dict(out=sf.t[:, :], in_=bt[:, :], func=AF.Square), writes=[br_, sf.r])
                pend.append((6 + tc, sf.t[:, :], sf.r, c))
                if len(pend) > 1:
                    stat_mm(*pend.pop(0))
            filler.tick()
            if tc_outer and si == KC - 1:
                while pend:
                    stat_mm(*pend.pop(0))
                filler.flush()
                rstd_from_banks([6], rstd3[hf].t, rstd3[hf].r, 0, tmpD)
                al.kill(u2.r)
                sv = al.top
                al.top = u2.off
                filler = Filler(final_items(hf, rstd3[hf], tcs=(0,)), KC)
                fin_tail = final_items(hf, rstd3[hf], tcs=(1,), nbuf=3)
                al.top = sv
        while pend:
            stat_mm(*pend.pop(0))
        filler.flush()
        if tc_outer:
            rstd_from_banks([7], rstd3[hf].t, rstd3[hf].r, 512, tmpD)
            for it in fin_tail:
                it()
        else:
            rstd_from_banks([6, 7], rstd3[hf].t, rstd3[hf].r, 0, tmpD)

    u2i = u2_items(0, u2, nbuf=6)
    for it in u2i:
        it()
    al.release(mB)
    hp_items, hp_fin = hpass_items(1, 6)
    fill = Filler(hp_items, FC)
    GATEUP(0, fill)
    fill.flush()
    hp_fin()
    rstd_from_banks([6, 7], rstd2.t, rstd2.r, TH, tmpD)
    al.release(mB)
    fill = Filler(u2_items(1, u2), KC)
    DOWN(0, fill)
    fill.flush()
    al.release(mB)
    fill = Filler(final_items(0, rstd3[0]), FC)
    GATEUP(1, fill)
    fill.flush()
    al.release(mB)
    mD = al.mark()
    fbufs_dummy = None
    DOWN(1, nofill, tc_outer=True)

    finish()


def _consts():
    j = np.arange(128)[:, None]
    s = np.arange(128)[None, :]
    cb = np.concatenate([
        np.ones((128, 128)),
        -(j >= s).astype(np.float64),
        -np.ones((128, 128)),
        (j < s).astype(np.float64),
        (j <= s).astype(np.float64),
        np.zeros((128, 128)),
        np.eye(128),
    ], axis=1).astype(ml_dtypes.bfloat16)
    cfm = np.concatenate([(j <= s).astype(np.float32), np.ones((128, 128), np.float32)], axis=1)
    return np.ascontiguousarray(cb), np.ascontiguousarray(cfm)


def make_in_maps(inputs, cores):
    cb, cfm = _consts()
    g = np.stack([np.asarray(inputs[k], np.float32)[0] for k in
                  ("norm_mix_pre", "norm_mix_post", "norm_ffn_pre", "norm_ffn_post")])
    gains = np.ascontiguousarray(g.reshape(4, KC, 128).transpose(2, 0, 1).reshape(128, 4 * KC))
    bfg = np.ascontiguousarray(np.broadcast_to(
        np.tile(np.asarray(inputs["b_forget"], np.float32)[0], NB)[None, :], (128, 128)))
    shared = {
        "w_in": np.ascontiguousarray(np.asarray(inputs["w_in"], np.float32)[0]),
        "w_bsb": np.ascontiguousarray(np.asarray(inputs["w_branch_sb"], np.float32)[0]),
        "w_bfx": np.ascontiguousarray(np.asarray(inputs["w_branch_fox"], np.float32)[0]),
        "w_out": np.ascontiguousarray(np.asarray(inputs["w_out"], np.float32)[0]),
        "w_fg": np.ascontiguousarray(np.asarray(inputs["w_ffn_gate"], np.float32)[0]),
        "w_fu": np.ascontiguousarray(np.asarray(inputs["w_ffn_up"], np.float32)[0]),
        "w_fd": np.ascontiguousarray(np.asarray(inputs["w_ffn_down"], np.float32)[0]),
        "gains": gains, "bfg": bfg, "cbf": cb, "cf32": cfm,
    }
    x = np.asarray(inputs["x"], np.float32)
    maps = []
    for b in cores:
        m = dict(shared)
        m["xT"] = np.ascontiguousarray(x[b].T)
        maps.append(m)
    return maps


def kernel(**inputs):
    nc = build_nc()
    in_maps = make_in_maps(inputs, list(range(8)))
    res = run_bass_kernel_spmd(nc, in_maps, core_ids=list(range(8)))
    out = np.stack([np.ascontiguousarray(np.asarray(r["outT"], np.float32).T) for r in res.results])
    return out
```

```python
import numpy as np
import ml_dtypes
from contextlib import ExitStack
import concourse.bass as bass
import concourse.mybir as mybir
from concourse.bass_utils import run_bass_kernel_spmd

F32 = mybir.dt.float32
BF16 = mybir.dt.bfloat16
AF = mybir.ActivationFunctionType
ALU = mybir.AluOpType

D = 2048
S = 2048
NH = 8
DH = 128
DFF = 5632
DIN = 10248
KC = D // 128
FC = DFF // 128
NB = S // 128
NCH = S // 512
EPS = 1e-6
SCALE = DH ** -0.5
COL_F = 6144
COL_GSB = 6152
COL_GFX = 8200
EPOCH = 30000


class Res:
    __slots__ = ("name", "w", "rs")

    def __init__(self, name):
        self.name = name
        self.w = None
        self.rs = {}


class DSem:
    __slots__ = ("sem", "count")

    def __init__(self, sem):
        self.sem = sem
        self.count = 0


class Eng:
    def __init__(self, kb, name):
        self.kb = kb
        self.name = name
        self.sem = kb.new_sem("e_" + name)
        self.own_sems = {self.sem}
        self.count = 0
        self.waited = {}
        self.prog = []

    def wait(self, ev):
        sem, val = ev
        if self.waited.get(sem, 0) >= val:
            return
        self.prog.append(("w", sem, val))
        self.waited[sem] = val

    def tick(self):
        if self.count >= EPOCH:
            self.sem = self.kb.new_sem("e_" + self.name)
            self.own_sems.add(self.sem)
            self.count = 0
        self.count += 1
        return (self.sem, self.count)

    def replay(self, eng):
        for it in self.prog:
            if it[0] == "w":
                eng.wait_ge(it[1], it[2])
            else:
                _, calls, sem, inc = it
                ins = None
                for meth, kw in calls:
                    ins = getattr(eng, meth)(**kw)
                    if inc == 16:
                        ins.then_inc(sem, 16)
                if inc == 1:
                    ins.then_inc(sem, 1)


class KB:
    def __init__(self, nc, stack):
        self.nc = nc
        self.stack = stack
        self.nsem = 0
        self.pe = Eng(self, "pe")
        self.act = Eng(self, "act")
        self.dve = Eng(self, "dve")
        self.pool = Eng(self, "pool")
        self.sp = Eng(self, "sp")

    def new_sem(self, name):
        self.nsem += 1
        return self.stack.enter_context(self.nc.semaphore(f"{name}_{self.nsem}"))

    def dsem(self, name):
        return DSem(self.new_sem("d_" + name))

    @staticmethod
    def _deps(reads, writes):
        evs = []
        for r in reads:
            if r.w is not None:
                evs.append(r.w)
        for r in writes:
            if r.w is not None:
                evs.append(r.w)
            for s, v in r.rs.items():
                evs.append((s, v))
        return evs

    @staticmethod
    def _commit(ev, reads, writes):
        s, v = ev
        for r in reads:
            if r.rs.get(s, 0) < v:
                r.rs[s] = v
        for r in writes:
            r.w = ev
            r.rs = {}

    def op(self, E, meth, kw, reads=(), writes=()):
        return self.group(E, [(meth, kw)], reads, writes)

    def group(self, E, calls, reads=(), writes=()):
        for ev in self._deps(reads, writes):
            if E is self.pe and ev[0] in E.own_sems:
                continue
            E.wait(ev)
        ev = E.tick()
        E.prog.append(("o", list(calls), ev[0], 1))
        self._commit(ev, reads, writes)
        return ev

    def dma(self, Q, ds, out, in_, reads=(), writes=()):
        return self.dma_multi(Q, ds, [(out, in_)], reads, writes)

    def dma_multi(self, Q, ds, pairs, reads=(), writes=()):
        for ev in self._deps(reads, writes):
            Q.wait(ev)
        Q.prog.append(("o", [("dma_start", dict(out=o, in_=i)) for o, i in pairs], ds.sem, 16))
        ds.count += 16 * len(pairs)
        ev = (ds.sem, ds.count)
        self._commit(ev, reads, writes)
        return ev


class Alloc:
    BASE = 16512
    LIMIT = 16512 + 212736

    def __init__(self, nc):
        self.nc = nc
        self.top = self.BASE
        self.n = 0
        self.live = []
        self.dead = []

    def mark(self):
        return self.top

    def release(self, m):
        keep = []
        for it in self.live:
            (self.dead if it[0] >= m else keep).append(it)
        self.live = keep
        self.top = m

    def kill(self, res):
        keep = []
        for it in self.live:
            (self.dead if it[2] is res else keep).append(it)
        self.live = keep

    def register(self, off, end, res):
        for (o, e, r) in self.dead:
            if o < end and off < e:
                evs = list(r.rs.items())
                if r.w is not None:
                    evs.append(r.w)
                for s_, v in evs:
                    if res.rs.get(s_, 0) < v:
                        res.rs[s_] = v
        self.live.append((off, end, res))

    def alloc(self, name, shape, dtype):
        nbytes = int(np.prod(shape[1:])) * (2 if dtype == BF16 else 4)
        off = (self.top + 63) // 64 * 64
        assert off + nbytes <= self.LIMIT, f"SBUF overflow allocating {name}: {off + nbytes - self.LIMIT} over"
        self.top = off + nbytes
        self.last_off = off
        self.last_end = off + nbytes
        self.n += 1
        return self.nc.alloc_sbuf_tensor_at(f"{name}_{self.n}", shape, dtype, offset=off)


class Filler:
    def __init__(self, items, nticks):
        self.items = list(items)
        self.left = max(1, nticks)
        self.credit = 0.0

    def tick(self):
        if self.items:
            self.credit += len(self.items) / max(1, self.left)
        self.left -= 1
        while self.credit >= 1.0 - 1e-9 and self.items:
            self.items.pop(0)()
            self.credit -= 1.0

    def burst(self, k):
        for _ in range(k):
            if self.items:
                self.items.pop(0)()

    def flush(self):
        while self.items:
            self.items.pop(0)()


class Slot:
    def __init__(self, kb, name, shape, dtype, dma=False):
        self.t = kb.al.alloc(name, shape, dtype)
        self.r = Res(name)
        kb.al.register(kb.al.last_off, kb.al.last_end, self.r)
        self.ds = kb.dsem(name) if dma else None
        self.kb = kb
        self.name = name
        self.off = kb.al.last_off

    def view(self, shape, dtype):
        kb = self.kb
        kb.al.n += 1
        return kb.nc.alloc_sbuf_tensor_at(f"{self.name}_v{kb.al.n}", shape, dtype, offset=self.off)


def build_nc(debug=None, upto=99):
    nc = bass.Bass("TRN2", target_bir_lowering=False)
    G = {}
    G["xT"] = nc.dram_tensor("xT", [D, S], F32, kind="ExternalInput").ap()
    G["wqkv_t"] = nc.dram_tensor("wqkv_t", [48, 128, KC, 128], F32, kind="ExternalInput").ap()
    G["wf_t"] = nc.dram_tensor("wf_t", [128, KC, 8], F32, kind="ExternalInput").ap()
    G["wg_t"] = nc.dram_tensor("wg_t", [32, 128, KC, 128], F32, kind="ExternalInput").ap()
    G["wb_t"] = nc.dram_tensor("wb_t", [KC, 128, 2 * NH, 128], F32, kind="ExternalInput").ap()
    G["wo_t"] = nc.dram_tensor("wo_t", [KC, 128, KC, 128], F32, kind="ExternalInput").ap()
    G["wfg_t"] = nc.dram_tensor("wfg_t", [FC, 128, KC, 128], F32, kind="ExternalInput").ap()
    G["wfu_t"] = nc.dram_tensor("wfu_t", [FC, 128, KC, 128], F32, kind="ExternalInput").ap()
    G["wfd_t"] = nc.dram_tensor("wfd_t", [KC, 128, FC, 128], F32, kind="ExternalInput").ap()
    G["gains_d"] = nc.dram_tensor("gains", [128, 4 * KC], F32, kind="ExternalInput").ap()
    G["bf_d"] = nc.dram_tensor("bfg", [128, 128], F32, kind="ExternalInput").ap()
    G["cbf_d"] = nc.dram_tensor("cbf", [128, 7 * 128], BF16, kind="ExternalInput").ap()
    G["cf_d"] = nc.dram_tensor("cf32", [128, 2 * 128], F32, kind="ExternalInput").ap()
    G["outT"] = nc.dram_tensor("outT", [D, S], F32, kind="ExternalOutput").ap()
    G["OTs"] = nc.dram_tensor("OTs", [2 * NH, DH, S], BF16).ap()
    G["mixs"] = nc.dram_tensor("mixs", [D, S], F32).ap()
    G["hs"] = nc.dram_tensor("hs", [D, S], F32).ap()
    G["ffs"] = nc.dram_tensor("ffs", [D, S], F32).ap()
    G["dbg"] = None
    if debug is not None:
        G["dbg"] = {k: nc.dram_tensor("dbg_" + k, shp, dt, kind="ExternalOutput").ap()
                    for k, (shp, dt) in debug.items()}

    with ExitStack() as top:
        arena = top.enter_context(nc.sbuf_tensor("arena", [128, 212736], mybir.dt.uint8))
        kb = KB(nc, top)
        kb.al = Alloc(nc)
        banks = []
        for i in range(8):
            t = top.enter_context(nc.psum_tensor(f"ps{i}", [128, 512], F32))
            banks.append((t, Res(f"ps{i}")))
        _emit(nc, kb, banks, G, upto)
        with nc.Block() as block:
            @block.sync
            def _(e):
                kb.sp.replay(e)

            @block.tensor
            def _(e):
                kb.pe.replay(e)

            @block.scalar
            def _(e):
                kb.act.replay(e)

            @block.vector
            def _(e):
                kb.dve.replay(e)

            @block.gpsimd
            def _(e):
                kb.pool.replay(e)
    return nc


def _emit(nc, kb, banks, G, upto):
    PE, ACT, DVE, POOL, SP = kb.pe, kb.act, kb.dve, kb.pool, kb.sp
    al = kb.al
    xT = G["xT"]
    wqkv_t, wf_t, wg_t, wb_t, wo_t = G["wqkv_t"], G["wf_t"], G["wg_t"], G["wb_t"], G["wo_t"]
    wfg_t, wfu_t, wfd_t = G["wfg_t"], G["wfu_t"], G["wfd_t"]
    gains_d, bf_d, cbf_d, cf_d, outT = G["gains_d"], G["bf_d"], G["cbf_d"], G["cf_d"], G["outT"]
    OTs, mixs, hs, ffs, dbg = G["OTs"], G["mixs"], G["hs"], G["ffs"], G["dbg"]
    final_events = []

    cbf = Slot(kb, "cbf", [128, 7 * 128], BF16, dma=True)
    cf = Slot(kb, "cf", [128, 2 * 128], F32, dma=True)
    gains = Slot(kb, "gains", [128, 4 * KC], F32, dma=True)
    bfg = Slot(kb, "bfg", [128, 128], F32, dma=True)
    kb.dma(SP, cbf.ds, cbf.t[:, :], cbf_d, writes=[cbf.r])
    kb.dma(SP, cf.ds, cf.t[:, :], cf_d, writes=[cf.r])
    kb.dma(SP, gains.ds, gains.t[:, :], gains_d, writes=[gains.r])
    kb.dma(SP, bfg.ds, bfg.t[:, :], bf_d, writes=[bfg.r])
    ones_bf = cbf.t[:, 0:128]
    negtri_bf = cbf.t[:, 128:256]
    negones_bf = cbf.t[:, 256:384]
    mstrict_bf = cbf.t[:, 384:512]
    mincl_bf = cbf.t[:, 512:640]
    zeros_bf = cbf.t[:, 640:768]
    ident_bf = cbf.t[:, 768:896]
    tri_f = cf.t[:, 0:128]
    ones_f = cf.t[:, 128:256]

    rstd2 = Slot(kb, "rstd2", [128, S], F32)
    rstdm = Slot(kb, "rstdm", [128, S], F32)
    m_u = al.mark()
    biasT = Slot(kb, "biasT", [128, NCH, NH, NB], F32)
    uT = Slot(kb, "uT", [128, KC, S], BF16)
    m_w = al.mark()
    wf32 = Slot(kb, "wf32", [128, KC, 8], F32, dma=True)
    wfb = Slot(kb, "wfb", [128, KC, 8], BF16)
    zf = Slot(kb, "zf", [128, 128], F32)
    tots = Slot(kb, "tots", [128, 128], F32)
    pre = Slot(kb, "pre", [128, 136], F32)
    cumL = Slot(kb, "cumL", [128, 128], F32)
    wst = [Slot(kb, f"wst{i}", [128, KC, 128], F32, dma=True) for i in range(3)]
    wbf = [[Slot(kb, f"wbf{s}_{j}", [128, KC, 128], BF16) for j in range(3)] for s in range(2)]

    def dump(name, src_ap, src_res, dst_ap=None):
        if dbg is None or name not in dbg:
            return
        ds = kb.dsem("dbg" + name)
        ev = kb.dma(SP, ds, dbg[name] if dst_ap is None else dst_ap, src_ap, reads=[src_res])
        final_events.append(ev)

    def finish():
        for ev in final_events:
            SP.wait(ev)

    def rstd_from_banks(bank_ids, dst_t, dst_r, col0, tmp):
        for i, bi in enumerate(bank_ids):
            bt, br = banks[bi]
            sl = slice(col0 + i * 512, col0 + (i + 1) * 512)
            kb.op(DVE, "tensor_scalar", dict(out=tmp.t[:, 0:512], in0=bt[:, :], scalar1=1.0 / D, scalar2=EPS,
                                             op0=ALU.mult, op1=ALU.add), writes=[br, tmp.r])
            kb.op(ACT, "activation", dict(out=tmp.t[:, 0:512], in_=tmp.t[:, 0:512], func=AF.Ln), writes=[tmp.r])
            kb.op(ACT, "activation", dict(out=dst_t[:, sl], in_=tmp.t[:, 0:512], func=AF.Exp, scale=-0.5),
                  reads=[tmp.r], writes=[dst_r])

    m0 = al.mark()
    xs = [Slot(kb, f"xs{i}", [128, S], F32, dma=True) for i in range(2)]
    sq = [Slot(kb, f"sq{i}", [128, S], BF16) for i in range(2)]
    rstd1 = Slot(kb, "rstd1", [128, S], F32)
    tmp = Slot(kb, "tmp1", [128, 512], F32)
    for c in range(KC):
        x = xs[c % 2]
        q = sq[c % 2]
        kb.dma(SP, x.ds, x.t[:, :], xT[c * 128:(c + 1) * 128, :], writes=[x.r])
        kb.op(ACT, "activation", dict(out=q.t[:, :], in_=x.t[:, :], func=AF.Square),
              reads=[x.r], writes=[q.r])
        for tc in range(NCH):
            bt, br = banks[tc]
            kb.op(PE, "matmul", dict(out=bt[:, :], lhsT=ones_bf, rhs=q.t[:, tc * 512:(tc + 1) * 512],
                                     start=(c == 0), stop=(c == KC - 1)),
                  reads=[q.r, cbf.r], writes=[br])
    rstd_from_banks([0, 1, 2, 3], rstd1.t, rstd1.r, 0, tmp)
    for c in range(KC):
        x = xs[c % 2]
        kb.dma(SP, x.ds, x.t[:, :], xT[c * 128:(c + 1) * 128, :], writes=[x.r])
        kb.op(DVE, "scalar_tensor_tensor", dict(out=uT.t[:, c, :], in0=x.t[:, :], scalar=gains.t[:, c:c + 1],
                                                in1=rstd1.t[:, :], op0=ALU.mult, op1=ALU.mult),
              reads=[x.r, gains.r, rstd1.r], writes=[uT.r])
    dump("rstd1", rstd1.t[:, :], rstd1.r)
    if dbg is not None and "uT" in dbg:
        dump("uT", uT.t[:, :, :], uT.r, dbg["uT"].rearrange("(c p) t -> p c t", p=128))
    al.release(m0)
    if upto <= 1:
        return finish()

    def stage2():
        m0 = al.mark()
        tots2, pre2, cumL2 = tots.t, pre.t, cumL.t
        kb.dma(SP, wf32.ds, wf32.t[:, :, :], wf_t, writes=[wf32.r])
        kb.op(DVE, "tensor_copy", dict(out=wfb.t[:, :, :], in_=wf32.t[:, :, :]), reads=[wf32.r], writes=[wfb.r])
        bF, rF = banks[4]
        for blk in range(NB):
            calls = [("matmul", dict(out=bF[:, blk * 8:(blk + 1) * 8], lhsT=uT.t[:, k, blk * 128:(blk + 1) * 128],
                                     rhs=wfb.t[:, k, :], start=(k == 0), stop=(k == KC - 1))) for k in range(KC)]
            kb.group(PE, calls, reads=[uT.r, wfb.r], writes=[rF])
        kb.op(DVE, "tensor_tensor", dict(out=zf.t[:, :], in0=bF[:, 0:128], in1=bfg.t[:, :], op=ALU.add),
              reads=[bfg.r], writes=[rF, zf.r])
        kb.op(ACT, "activation", dict(out=zf.t[:, :], in_=zf.t[:, :], func=AF.Exp, scale=-1.0), writes=[zf.r])
        kb.op(DVE, "tensor_scalar_add", dict(out=zf.t[:, :], in0=zf.t[:, :], scalar1=1.0), writes=[zf.r])
        kb.op(ACT, "activation", dict(out=zf.t[:, :], in_=zf.t[:, :], func=AF.Ln), writes=[zf.r])
        bW, rW = banks[5]
        bT, rT = banks[6]
        kb.op(PE, "matmul", dict(out=bW[:, 0:128], lhsT=tri_f, rhs=zf.t[:, :], start=True, stop=True),
              reads=[zf.r, cf.r], writes=[rW])
        kb.op(PE, "matmul", dict(out=bT[:, 0:128], lhsT=ones_f, rhs=zf.t[:, :], start=True, stop=True),
              reads=[zf.r, cf.r], writes=[rT])
        kb.op(DVE, "tensor_copy", dict(out=tots2[:, :], in_=bT[:, 0:128]), writes=[rT, tots.r])
        kb.op(DVE, "memset", dict(ap=pre2[:, 0:8], constant=0.0), writes=[pre.r])
        for blk in range(1, NB + 1):
            kb.op(DVE, "tensor_tensor", dict(out=pre2[:, blk * 8:(blk + 1) * 8], in0=pre2[:, (blk - 1) * 8:blk * 8],
                                             in1=tots2[:, (blk - 1) * 8:blk * 8], op=ALU.add),
                  reads=[tots.r], writes=[pre.r])
        kb.op(DVE, "tensor_tensor", dict(out=cumL2[:, :], in0=bW[:, 0:128], in1=pre2[:, 0:128], op=ALU.add),
              reads=[pre.r], writes=[rW, cumL.r])
        for c in range(NCH):
            for h in range(NH):
                pc = 4 * (c + 1) * 8 + h
                kb.op(DVE, "tensor_scalar", dict(out=biasT.t[:, c, h, :], in0=cumL.t[:, h:128:8],
                                                 scalar1=pre.t[:, pc:pc + 1], scalar2=None,
                                                 op0=ALU.subtract),
                      reads=[cumL.r, pre.r], writes=[biasT.r])
        dump("cumL", cumL2[:, :], cumL.r)
        if dbg is not None and "biasT" in dbg:
            dump("biasT", biasT.t[:, :, :, :].rearrange("p a b c -> p (a b c)"), biasT.r)
        al.release(m0)


    m0 = al.mark()
    qkv = []
    for par in range(2):
        q_ = Slot(kb, f"qT{par}", [128, S], BF16)
        k_ = Slot(kb, f"kT{par}", [128, S], BF16)
        v_ = Slot(kb, f"V{par}", [128, NB, 128], BF16)
        qkv.append((q_, k_, v_, None))
    VT = Slot(kb, "VT", [128, S], BF16)
    Eb = [Slot(kb, f"E{i}", [128, 512], F32) for i in range(2)]
    SPb = [Slot(kb, f"SP{i}", [128, 512], BF16) for i in range(4)]
    Wb = [Slot(kb, f"W{i}", [128, 512], BF16) for i in range(4)]
    Rb = [[Slot(kb, f"R{p}_{i}", [128, 512], BF16) for i in range(2)] for p in range(2)]
    rec = Slot(kb, "rec", [128, 512], F32)
    ocp = Slot(kb, "ocp", [128, 512], F32)
    Ost = [Slot(kb, f"Ost{i}", [128, 512], BF16, dma=True) for i in range(2)]
    OT_res = [[Res(f"OT{hh}_{c}") for c in range(NCH)] for hh in range(2 * NH)]
    cnt = {"w": 0, "p": 0, "z": 0, "o": 0}
    zb = [banks[2], banks[3], banks[4], banks[5]]
    pb = [banks[0], banks[1]]

    def load_head(hh, ws):
        br, h = divmod(hh, NH)
        casts = []
        for j in range(3):
            ci = br * 24 + j * 8 + h
            sl = wst[cnt["w"] % 3]
            cnt["w"] += 1
            kb.dma(SP, sl.ds, sl.t[:, :, :], wqkv_t[ci], writes=[sl.r])
            dst = wbf[ws][j]

            def cast(sl=sl, dst=dst):
                kb.op(DVE, "tensor_copy", dict(out=dst.t[:, :, :], in_=sl.t[:, :, :]), reads=[sl.r], writes=[dst.r])
            casts.append(cast)
        return casts

    def proj_items(ws, par):
        wq, wk, wv = wbf[ws]
        qT, kT, V, V2 = qkv[par]
        items = []

        def qk_item(wsl, dst, tc, sc):
            st_ = {}

            def f0():
                bt, br_ = pb[cnt["p"] % 2]
                cnt["p"] += 1
                st_["b"] = (bt, br_)
                kb.group(PE, [("matmul", dict(out=bt[:, :], lhsT=wsl.t[:, k, :],
                                              rhs=uT.t[:, k, tc * 512:(tc + 1) * 512],
                                              start=(k == 0), stop=False)) for k in range(KC // 2)],
                         reads=[wsl.r, uT.r], writes=[br_])

            def f1():
                bt, br_ = st_["b"]
                kb.group(PE, [("matmul", dict(out=bt[:, :], lhsT=wsl.t[:, k, :],
                                              rhs=uT.t[:, k, tc * 512:(tc + 1) * 512],
                                              start=False, stop=(k == KC - 1))) for k in range(KC // 2, KC)],
                         reads=[wsl.r, uT.r], writes=[br_])
                kb.op(DVE, "tensor_scalar", dict(out=dst.t[:, tc * 512:(tc + 1) * 512], in0=bt[:, :],
                                                 scalar1=sc, scalar2=None, op0=ALU.mult),
                      writes=[br_, dst.r])
            return [f0, f1]

        def vt_items():
            its = []
            for tc in range(NCH):
                its += qk_item(wv, VT, tc, 1.0)

            def tr(g):
                def f():
                    bt, br_ = pb[cnt["p"] % 2]
                    cnt["p"] += 1
                    btb = bt[:, :].bitcast(BF16)
                    kb.group(PE, [("transpose", dict(out=btb[:, j * 128:(j + 1) * 128],
                                                     in_=VT.t[:, (8 * g + j) * 128:(8 * g + j + 1) * 128],
                                                     identity=ident_bf)) for j in range(8)],
                             reads=[VT.r, cbf.r], writes=[br_])
                    kb.op(DVE, "tensor_copy", dict(out=V.t[:, 8 * g:8 * g + 8, :].rearrange("p a b -> p (a b)"),
                                                   in_=btb[:, 0:1024]), writes=[br_, V.r])
                return f
            its += [tr(0), tr(1)]
            return its

        for tc in range(NCH):
            items += qk_item(wk, kT, tc, 1.0)
        items += vt_items()
        for tc in range(NCH):
            items += qk_item(wq, qT, tc, SCALE)
        return items

    def store_O(hh, c, ost):
        kb.dma(SP, ost.ds, OTs[hh, :, c * 512:(c + 1) * 512], ost.t[:, :], reads=[ost.r],
               writes=[OT_res[hh][c]])

    def attn_sb(hh, par, filler):
        qT, kT, V, V2 = qkv[par]
        seq = []
        for c in range(NCH):
            blocks = list(range(4 * c + 3, -1, -1))
            for j, b in enumerate(blocks):
                seq.append((c, j, b, len(blocks)))
        N = len(seq)
        stt = {}
        ste = {}

        def A_pe(g):
            c, j, b, n = seq[g]
            o = max(0, 128 * (b - 4 * c))
            diag = b >= 4 * c
            zi = cnt["z"]
            cnt["z"] += 1
            bz, rz = zb[zi % 4]
            e, sp = Eb[zi % 2], SPb[zi % 4]
            stt[g] = (o, diag, bz, rz, sp, Wb[zi % 4])
            ste[g] = e
            if j == 0:
                bO, rO = banks[6 + c % 2]
                kb.op(PE, "matmul", dict(out=bO[:, :], lhsT=zeros_bf, rhs=uT.t[:, 0, 0:512], start=True, stop=True),
                      reads=[cbf.r, uT.r], writes=[rO])
                for R_ in Rb[c % 2]:
                    kb.op(POOL, "memset", dict(ap=R_.t[:, :], constant=0.0), writes=[R_.r])
            kb.op(PE, "matmul", dict(out=bz[:, o:512], lhsT=kT.t[:, b * 128:(b + 1) * 128],
                                     rhs=qT.t[:, c * 512 + o:(c + 1) * 512], start=True, stop=True),
                  reads=[kT.r, qT.r], writes=[rz])

        def A_act(g):
            o, diag, bz, rz, sp, w = stt[g]
            e = ste.pop(g)
            kb.op(ACT, "activation", dict(out=e.t[:, o:512], in_=bz[:, o:512], func=AF.Exp),
                  writes=[rz, e.r])
            kb.op(ACT, "activation", dict(out=sp.t[:, o:512], in_=e.t[:, o:512], func=AF.Ln,
                                          bias=ones_f[:, 0:1]),
                  reads=[e.r, cf.r], writes=[sp.r])
            if diag:
                kb.op(POOL, "tensor_tensor", dict(out=sp.t[:, o:o + 128], in0=sp.t[:, o:o + 128],
                                                  in1=mstrict_bf, op=ALU.mult),
                      reads=[cbf.r], writes=[sp.r])

        def B1(g):
            c, j, b, n = seq[g]
            o, diag, bz, rz, sp, w = stt[g]
            first = (j == 0)
            R = Rb[c % 2][j % 2]
            calls = [("matmul", dict(out=bz[:, o:512], lhsT=negtri_bf, rhs=sp.t[:, o:512],
                                     start=False, stop=first, skip_group_check=True))]
            if not first:
                calls.append(("matmul", dict(out=bz[:, o:512], lhsT=negones_bf, rhs=R.t[:, o:512],
                                             start=False, stop=True, skip_group_check=True)))
            kb.group(PE, calls, reads=[sp.r, R.r, cbf.r], writes=[rz])
            kb.op(ACT, "activation", dict(out=w.t[:, o:512], in_=bz[:, o:512], func=AF.Exp),
                  writes=[rz, w.r])
            if diag:
                kb.op(POOL, "tensor_tensor", dict(out=w.t[:, o:o + 128], in0=w.t[:, o:o + 128],
                                                  in1=mstrict_bf, op=ALU.mult),
                      reads=[cbf.r], writes=[w.r])
            if j < n - 1:
                Rn = Rb[c % 2][(j + 1) % 2]
                kb.op(DVE, "tensor_tensor", dict(out=Rn.t[:, o:512], in0=R.t[:, o:512], in1=sp.t[:, o:512],
                                                 op=ALU.add), reads=[sp.r, R.r], writes=[Rn.r])

        def B2(g):
            c, j, b, n = seq[g]
            o, diag, bz, rz, sp, w = stt.pop(g)
            bO, rO = banks[6 + c % 2]
            kb.op(PE, "matmul", dict(out=bO[:, o:512], lhsT=V.t[:, b, :], rhs=w.t[:, o:512],
                                     start=False, stop=(j == n - 1), skip_group_check=True),
                  reads=[V.r, w.r], writes=[rO])
            if j == n - 1:
                ost = Ost[cnt["o"] % 2]
                cnt["o"] += 1
                kb.op(DVE, "tensor_copy", dict(out=ost.t[:, :], in_=bO[:, :]), writes=[rO, ost.r])
                store_O(hh, c, ost)

        for g in range(min(3, N)):
            A_pe(g)
        for g in range(min(2, N)):
            A_act(g)
        filler.burst(2)
        for g in range(N):
            if g + 2 < N:
                A_act(g + 2)
            B1(g)
            if g >= 1:
                B2(g - 1)
            if g + 3 < N:
                A_pe(g + 3)
            filler.tick()
        filler.burst(1)
        B2(N - 1)

    def attn_fox(hh, par, filler):
        qT, kT, V, V2 = qkv[par]
        h = hh % NH
        seq = []
        for c in range(NCH):
            for b in range(4 * c + 4):
                seq.append((c, b, 4 * c + 4))
        N = len(seq)
        stt = {}
        zb2 = [banks[2], banks[3]]

        def A_pe(g):
            c, b, n = seq[g]
            o = max(0, 128 * (b - 4 * c))
            zi = cnt["z"]
            cnt["z"] += 1
            bz, rz = zb2[zi % 2]
            stt[g] = (o, bz, rz, Wb[zi % 4])
            if b == 0:
                (bO, rO), (bD, rD) = banks[4 + 2 * (c % 2)], banks[5 + 2 * (c % 2)]
                kb.group(PE, [("matmul", dict(out=bO[:, :], lhsT=zeros_bf, rhs=uT.t[:, 0, 0:512], start=True, stop=True)),
                              ("matmul", dict(out=bD[:, :], lhsT=zeros_bf, rhs=uT.t[:, 0, 0:512], start=True, stop=True))],
                         reads=[cbf.r, uT.r], writes=[rO, rD])
            kb.op(PE, "matmul", dict(out=bz[:, o:512], lhsT=kT.t[:, b * 128:(b + 1) * 128],
                                     rhs=qT.t[:, c * 512 + o:(c + 1) * 512], start=True, stop=True),
                  reads=[kT.r, qT.r], writes=[rz])

        def A_act(g):
            c, b, n = seq[g]
            o, bz, rz, p = stt[g]
            kb.op(ACT, "activation", dict(out=p.t[:, o:512], in_=bz[:, o:512], func=AF.Exp,
                                          bias=biasT.t[:, c, h, b:b + 1]),
                  reads=[biasT.r], writes=[rz, p.r])
            if b >= 4 * c:
                kb.op(POOL, "tensor_tensor", dict(out=p.t[:, o:o + 128], in0=p.t[:, o:o + 128],
                                                  in1=mincl_bf, op=ALU.mult),
                      reads=[cbf.r], writes=[p.r])

        def B(g):
            c, b, n = seq[g]
            o, bz, rz, p = stt.pop(g)
            (bO, rO), (bD, rD) = banks[4 + 2 * (c % 2)], banks[5 + 2 * (c % 2)]
            last = (b == n - 1)
            kb.group(PE, [("matmul", dict(out=bO[:, o:512], lhsT=V.t[:, b, :], rhs=p.t[:, o:512],
                                          start=False, stop=last, skip_group_check=True)),
                          ("matmul", dict(out=bD[:, o:512], lhsT=ones_bf, rhs=p.t[:, o:512],
                                          start=False, stop=last, skip_group_check=True))],
                     reads=[V.r, p.r, cbf.r], writes=[rO, rD])
            if last:
                ost = Ost[cnt["o"] % 2]
                cnt["o"] += 1
                kb.op(ACT, "activation", dict(out=rec.t[:, :], in_=bD[:, :], func=AF.Copy), writes=[rD, rec.r])
                kb.op(DVE, "tensor_copy", dict(out=ocp.t[:, :], in_=bO[:, :]), writes=[rO, ocp.r])
                kb.op(DVE, "reciprocal", dict(out=rec.t[:, :], in_=rec.t[:, :]), writes=[rec.r])
                kb.op(DVE, "tensor_tensor", dict(out=ost.t[:, :], in0=ocp.t[:, :], in1=rec.t[:, :], op=ALU.mult),
                      reads=[rec.r, ocp.r], writes=[ost.r])
                store_O(hh, c, ost)

        for g in range(min(2, N)):
            A_pe(g)
        A_act(0)
        filler.burst(1)
        for g in range(N):
            if g + 1 < N:
                A_act(g + 1)
            B(g)
            if g + 2 < N:
                A_pe(g + 2)
            filler.tick()

    heads = list(range(2 * NH)) if upto >= 4 else G.get("dbg_heads", [0, 8])
    for cst in load_head(heads[0], 0):
        cst()
    if len(heads) > 1:
        for cst in load_head(heads[1], 1):
            cst()
    for it in proj_items(0, 0):
        it()
    stage2()
    for i, hh in enumerate(heads):
        items = proj_items((i + 1) % 2, (i + 1) % 2) if i + 1 < len(heads) else []
        if i + 2 < len(heads):
            casts = load_head(heads[i + 2], i % 2)
            items[8:8] = casts
        filler = Filler(items, 40)
        if hh < NH:
            attn_sb(hh, i % 2, filler)
        else:
            attn_fox(hh, i % 2, filler)
        filler.flush()
    if dbg is not None and "OT" in dbg:
        ds = kb.dsem("dbgOT")
        allr = [OT_res[hh][c] for hh in heads for c in range(NCH)]
        for hh in heads:
            ev = kb.dma(SP, ds, dbg["OT"][hh], OTs[hh], reads=allr)
        final_events.append(ev)
    al.release(m_w)
    if upto <= 3:
        return finish()

    TH = 1024
    NQ = S // 512
    mix_res = [[Res(f"mix{c}_{q}") for q in range(NQ)] for c in range(KC)]
    h_res = [[Res(f"h{c}_{q}") for q in range(NQ)] for c in range(KC)]
    ff_res = [[Res(f"ff{c}_{q}") for q in range(NQ)] for c in range(KC)]
    nofill = Filler([], 1)

    def stat_mm(bi, sq_ap, sq_res, c):
        sb_t, sb_r = banks[bi]
        kb.op(PE, "matmul", dict(out=sb_t[:, :], lhsT=ones_bf, rhs=sq_ap, start=(c == 0), stop=(c == KC - 1)),
              reads=[sq_res, cbf.r], writes=[sb_r])

    def wview(w, c0, kc):
        return w[:, c0:c0 + 128].rearrange("(k p) c -> p k c", p=128)

    def hpass_items(hf, sbank0):
        xh = [Slot(kb, f"xh{i}", [128, 512], F32, dma=True) for i in range(2)]
        mh = [Slot(kb, f"mh{i}", [128, 512], F32, dma=True) for i in range(2)]
        hst = [Slot(kb, f"hst{i}", [128, 512], F32, dma=True) for i in range(2)]
        sqh = [Slot(kb, f"sqh{i}", [128, 512], BF16) for i in range(3)]
        pend = []
        items = []

        def mk(c, tc, n):
            def f():
                q = 2 * hf + tc
                tsl = slice(q * 512, (q + 1) * 512)
                xx, mm, hh_, qq = xh[n % 2], mh[n % 2], hst[n % 2], sqh[n % 3]
                kb.dma(SP, xx.ds, xx.t[:, :], xT[c * 128:(c + 1) * 128, tsl], writes=[xx.r])
                kb.dma(SP, mm.ds, mm.t[:, :], mixs[c * 128:(c + 1) * 128, tsl], reads=[mix_res[c][q]], writes=[mm.r])
                kb.op(DVE, "scalar_tensor_tensor", dict(out=hh_.t[:, :], in0=mm.t[:, :], scalar=gains.t[:, KC + c:KC + c + 1],
                                                        in1=rstdm.t[:, tsl], op0=ALU.mult, op1=ALU.mult),
                      reads=[mm.r, gains.r, rstdm.r], writes=[hh_.r])
                kb.op(POOL, "tensor_tensor", dict(out=hh_.t[:, :], in0=hh_.t[:, :], in1=xx.t[:, :], op=ALU.add),
                      reads=[xx.r], writes=[hh_.r])
                kb.dma(POOL, hh_.ds, hs[c * 128:(c + 1) * 128, tsl], hh_.t[:, :], reads=[hh_.r], writes=[h_res[c][q]])
                kb.op(ACT, "activation", dict(out=qq.t[:, :], in_=hh_.t[:, :], func=AF.Square),
                      reads=[hh_.r], writes=[qq.r])
                if pend:
                    stat_mm(*pend.pop(0))
                pend.append((sbank0 + tc, qq.t[:, :], qq.r, c))
            return f

        n = 0
        for c in range(KC):
            for tc in range(2):
                items.append(mk(c, tc, n))
                n += 1

        def fin():
            while pend:
                stat_mm(*pend.pop(0))
        return items, fin

    def u2_items(hf, u2, nbuf=2):
        hl = [Slot(kb, f"hl{i}", [128, 512], F32, dma=True) for i in range(nbuf)]
        items = []

        def mk(c, tc, n):
            def f():
                q = 2 * hf + tc
                tsl = slice(q * 512, (q + 1) * 512)
                x = hl[n % nbuf]
                kb.dma(SP, x.ds, x.t[:, :], hs[c * 128:(c + 1) * 128, tsl], reads=[h_res[c][q]], writes=[x.r])
                kb.op(DVE, "scalar_tensor_tensor", dict(out=u2.t[:, c, tc * 512:(tc + 1) * 512], in0=x.t[:, :],
                                                        scalar=gains.t[:, 2 * KC + c:2 * KC + c + 1],
                                                        in1=rstd2.t[:, tsl], op0=ALU.mult, op1=ALU.mult),
                      reads=[x.r, gains.r, rstd2.r], writes=[u2.r])
            return f

        n = 0
        for c in range(KC):
            for tc in range(2):
                items.append(mk(c, tc, n))
                n += 1
        return items

    def final_items(hf, rstd3, tcs=(0, 1), nbuf=2):
        hl = [Slot(kb, f"hl2_{i}", [128, 512], F32, dma=True) for i in range(nbuf)]
        fl = [Slot(kb, f"fl{i}", [128, 512], F32, dma=True) for i in range(nbuf)]
        ot = [Slot(kb, f"ot{i}", [128, 512], F32, dma=True) for i in range(nbuf)]
        items = []

        def mk(c, tc, n):
            def f():
                q = 2 * hf + tc
                tsl = slice(q * 512, (q + 1) * 512)
                hh_, ff_, oo = hl[n % nbuf], fl[n % nbuf], ot[n % nbuf]
                kb.dma(SP, hh_.ds, hh_.t[:, :], hs[c * 128:(c + 1) * 128, tsl], reads=[h_res[c][q]], writes=[hh_.r])
                kb.dma(SP, ff_.ds, ff_.t[:, :], ffs[c * 128:(c + 1) * 128, tsl], reads=[ff_res[c][q]], writes=[ff_.r])
                kb.op(DVE, "scalar_tensor_tensor", dict(out=oo.t[:, :], in0=ff_.t[:, :],
                                                        scalar=gains.t[:, 3 * KC + c:3 * KC + c + 1],
                                                        in1=rstd3.t[:, tc * 512:(tc + 1) * 512], op0=ALU.mult, op1=ALU.mult),
                      reads=[ff_.r, gains.r, rstd3.r], writes=[oo.r])
                kb.op(POOL, "tensor_tensor", dict(out=oo.t[:, :], in0=oo.t[:, :], in1=hh_.t[:, :], op=ALU.add),
                      reads=[hh_.r], writes=[oo.r])
                ev = kb.dma(POOL, oo.ds, outT[c * 128:(c + 1) * 128, tsl], oo.t[:, :], reads=[oo.r])
                final_events.append(ev)
            return f

        n = 0
        for c in range(KC):
            for tc in tcs:
                items.append(mk(c, tc, n))
                n += 1
        return items

    def P5(hf, merged):
        t0 = hf * TH
        OTh = Slot(kb, "OTh", [128, 2 * NH, TH], BF16, dma=True)
        wst = [Slot(kb, f"wstA{i}", [128, KC, 128], F32, dma=True) for i in range(2)]
        wg = [[Slot(kb, f"wgA{i}_{j}", [128, KC, 128], BF16) for j in range(3)] for i in range(2)]
        s1 = [Slot(kb, f"s1_{i}", [128, 512], F32) for i in range(2)]
        s2 = [Slot(kb, f"s2_{i}", [128, 512], F32) for i in range(2)]
        m1 = Slot(kb, "m1", [128, 512], F32)
        m2 = Slot(kb, "m2", [128, 512], F32)
        kb.dma(SP, OTh.ds, OTh.t[:, :, :], OTs[:, :, t0:t0 + TH].rearrange("h p t -> p h t"),
               reads=[OT_res[hh][2 * hf + i] for hh in range(2 * NH) for i in range(2)], writes=[OTh.r])
        cntA = {"w": 0, "b": 0}

        fillsA = lambda c: [wg_t[c], wg_t[KC + c], wb_t[c]]
        pendA = {}

        def fillA(c, j):
            sl = wst[cntA["w"] % 2]
            cntA["w"] += 1
            kb.dma(SP, sl.ds, sl.t[:, :, :], fillsA(c)[j], writes=[sl.r])
            pendA[(c, j)] = sl

        def castA(c, j):
            sl = pendA.pop((c, j))
            dst = wg[c % 2][j]
            kb.op(DVE, "tensor_copy", dict(out=dst.t[:, :, :], in_=sl.t[:, :, :]), reads=[sl.r], writes=[dst.r])

        fillA(0, 0)
        fillA(0, 1)
        castA(0, 0)
        castA(0, 1)
        fillA(0, 2)
        castA(0, 2)
        for c in range(KC):
            wgs, wgf, wb = wg[c % 2]
            if c + 1 < KC:
                fillA(c + 1, 0)
                fillA(c + 1, 1)
            for tc in range(2):
                bset = [banks[4 * (cntA["b"] % 2) + i] for i in range(4)]
                cntA["b"] += 1
                tsl = slice(t0 + tc * 512, t0 + (tc + 1) * 512)
                hsl = slice(tc * 512, (tc + 1) * 512)
                for (bt, br_), wsl in ((bset[0], wgs), (bset[1], wgf)):
                    kb.group(PE, [("matmul", dict(out=bt[:, :], lhsT=wsl.t[:, k, :], rhs=uT.t[:, k, tsl],
                                                  start=(k == 0), stop=(k == KC - 1))) for k in range(KC)],
                             reads=[wsl.r, uT.r], writes=[br_])
                for (bt, br_), k0 in ((bset[2], 0), (bset[3], NH)):
                    kb.group(PE, [("matmul", dict(out=bt[:, :], lhsT=wb.t[:, k0 + k, :], rhs=OTh.t[:, k0 + k, hsl],
                                                  start=(k == 0), stop=(k == NH - 1))) for k in range(NH)],
                             reads=[wb.r, OTh.r], writes=[br_])
                a1, a2 = s1[tc], s2[tc]
                kb.op(ACT, "activation", dict(out=a1.t[:, :], in_=bset[0][0][:, :], func=AF.Sigmoid),
                      writes=[bset[0][1], a1.r])
                kb.op(ACT, "activation", dict(out=a2.t[:, :], in_=bset[1][0][:, :], func=AF.Sigmoid),
                      writes=[bset[1][1], a2.r])
                kb.op(DVE, "tensor_tensor", dict(out=m1.t[:, :], in0=bset[2][0][:, :], in1=a1.t[:, :], op=ALU.mult),
                      reads=[a1.r], writes=[bset[2][1], m1.r])
                kb.op(DVE, "tensor_tensor", dict(out=m2.t[:, :], in0=bset[3][0][:, :], in1=a2.t[:, :], op=ALU.mult),
                      reads=[a2.r], writes=[bset[3][1], m2.r])
                kb.op(POOL, "tensor_tensor", dict(out=merged.t[:, c, hsl], in0=m1.t[:, :], in1=m2.t[:, :], op=ALU.add),
                      reads=[m1.r, m2.r], writes=[merged.r])
                if c + 1 < KC:
                    if tc == 0:
                        castA(c + 1, 0)
                        castA(c + 1, 1)
                        fillA(c + 1, 2)
                    else:
                        castA(c + 1, 2)

    def WOUT(hf, merged, filler):
        t0 = hf * TH
        wst = [Slot(kb, f"wstB{i}", [128, KC, 128], F32, dma=True) for i in range(3)]
        wo = [Slot(kb, f"wo{i}", [128, KC, 128], BF16) for i in range(2)]
        mixst = [Slot(kb, f"mixst{i}", [128, 512], F32, dma=True) for i in range(2)]
        sqm = [Slot(kb, f"sqm{i}", [128, 512], BF16) for i in range(3)]
        tmpB = Slot(kb, "tmpB", [128, 512], F32)
        cntB = {"b": 0, "m": 0}
        pend = []

        def fillB(c):
            sl = wst[c % 3]
            kb.dma(SP, sl.ds, sl.t[:, :, :], wo_t[c], writes=[sl.r])

        def castB(c):
            sl = wst[c % 3]
            kb.op(DVE, "tensor_copy", dict(out=wo[c % 2].t[:, :, :], in_=sl.t[:, :, :]), reads=[sl.r], writes=[wo[c % 2].r])

        fillB(0)
        castB(0)
        if KC > 1:
            fillB(1)
        for c in range(KC):
            if c + 2 < KC:
                fillB(c + 2)
            for tc in range(2):
                if tc == 1 and c + 1 < KC:
                    castB(c + 1)
                bt, br_ = banks[cntB["b"] % 4]
                cntB["b"] += 1
                hsl = slice(tc * 512, (tc + 1) * 512)
                kb.group(PE, [("matmul", dict(out=bt[:, :], lhsT=wo[c % 2].t[:, k, :], rhs=merged.t[:, k, hsl],
                                              start=(k == 0), stop=(k == KC - 1))) for k in range(KC)],
                         reads=[wo[c % 2].r, merged.r], writes=[br_])
                ms = mixst[cntB["m"] % 2]
                sm = sqm[cntB["m"] % 3]
                cntB["m"] += 1
                kb.op(ACT, "activation", dict(out=ms.t[:, :], in_=bt[:, :], func=AF.Copy), writes=[br_, ms.r])
                kb.dma(ACT, ms.ds, mixs[c * 128:(c + 1) * 128, t0 + tc * 512:t0 + (tc + 1) * 512], ms.t[:, :],
                       reads=[ms.r], writes=[mix_res[c][2 * hf + tc]])
                kb.op(ACT, "activation", dict(out=sm.t[:, :], in_=bt[:, :], func=AF.Square), writes=[br_, sm.r])
                pend.append((6 + tc, sm.t[:, :], sm.r, c))
                if len(pend) > 1:
                    stat_mm(*pend.pop(0))
            filler.tick()
        while pend:
            stat_mm(*pend.pop(0))
        rstd_from_banks([6, 7], rstdm.t, rstdm.r, t0, tmpB)
        return tmpB

    mA = al.mark()
    merged = Slot(kb, "merged", [128, KC, TH], BF16)
    mB = al.mark()
    P5(0, merged)
    al.release(mB)
    WOUT(0, merged, nofill)
    al.release(mB)
    P5(1, merged)
    al.release(mB)
    hp_items, hp_fin = hpass_items(0, 4)
    fill = Filler(hp_items, KC)
    tmpB = WOUT(1, merged, fill)
    fill.flush()
    hp_fin()
    rstd_from_banks([4, 5], rstd2.t, rstd2.r, 0, tmpB)
    if dbg is not None and "rstdm" in dbg:
        dump("rstdm", rstdm.t[:, 0:1024], rstdm.r)
    al.release(mA)

    al.release(m_u)
    u2 = Slot(kb, "u2", [128, KC, TH], BF16)
    aT = Slot(kb, "aT", [128, FC, TH], BF16)
    rstd3_ = Slot(kb, "rstd3", [128, TH], F32)
    rstd3 = [rstd3_, rstd3_]
    tmpD = Slot(kb, "tmpD", [128, 512], F32)
    mB = al.mark()

    def GATEUP(hf, filler):
        wst = [Slot(kb, f"wstC{i}", [128, KC, 128], F32, dma=True) for i in range(3)]
        wgu = [[Slot(kb, f"wgu{i}_{j}", [128, KC, 128], BF16) for j in range(2)] for i in range(2)]
        sg = [Slot(kb, f"sg{i}", [128, 512], F32) for i in range(2)]
        cntC = {"w": 0, "b": 0}

        pendC = {}

        def fillC(f):
            for j, w in enumerate((wfg_t, wfu_t)):
                sl = wst[cntC["w"] % 3]
                cntC["w"] += 1
                kb.dma(SP, sl.ds, sl.t[:, :, :], w[f], writes=[sl.r])
                pendC[(f, j)] = sl

        def castC(f):
            for j in range(2):
                sl = pendC.pop((f, j))
                dst = wgu[f % 2][j]
                kb.op(DVE, "tensor_copy", dict(out=dst.t[:, :, :], in_=sl.t[:, :, :]), reads=[sl.r], writes=[dst.r])

        fillC(0)
        castC(0)
        for f in range(FC):
            if f + 1 < FC:
                fillC(f + 1)
            wgt, wup = wgu[f % 2]
            for tc in range(2):
                if tc == 1 and f + 1 < FC:
                    castC(f + 1)
                pbi = cntC["b"] % 3
                cntC["b"] += 1
                (bg, rg), (bu, ru) = banks[2 * pbi], banks[2 * pbi + 1]
                hsl = slice(tc * 512, (tc + 1) * 512)
                kb.group(PE, [("matmul", dict(out=bg[:, :], lhsT=wgt.t[:, k, :], rhs=u2.t[:, k, hsl],
                                              start=(k == 0), stop=(k == KC - 1))) for k in range(KC)],
                         reads=[wgt.r, u2.r], writes=[rg])
                kb.group(PE, [("matmul", dict(out=bu[:, :], lhsT=wup.t[:, k, :], rhs=u2.t[:, k, hsl],
                                              start=(k == 0), stop=(k == KC - 1))) for k in range(KC)],
                         reads=[wup.r, u2.r], writes=[ru])
                sgt = sg[tc]
                kb.op(ACT, "activation", dict(out=sgt.t[:, :], in_=bg[:, :], func=AF.Silu), writes=[rg, sgt.r])
                kb.op(DVE, "tensor_tensor", dict(out=aT.t[:, f, hsl], in0=bu[:, :], in1=sgt.t[:, :], op=ALU.mult),
                      reads=[sgt.r], writes=[ru, aT.r])
            filler.tick()

    def DOWN(hf, filler, tc_outer=False):
        t0 = hf * TH
        wst = [Slot(kb, f"wstD{i}", [128, KC, 128], F32, dma=True) for i in range(3)]
        wd = [Slot(kb, f"wd{i}", [128, FC, 128], BF16) for i in range(2)]
        ffst = [Slot(kb, f"ffst{i}", [128, 512], F32, dma=True) for i in range(2)]
        sqf = [Slot(kb, f"sqf{i}", [128, 512], BF16) for i in range(3)]
        cntD = {"w": 0, "b": 0, "m": 0, "l": 0}
        pend = []

        pendD = []

        def fillD(c):
            for f0 in (0, 16, 32):
                nf = min(16, FC - f0)
                sl = wst[cntD["w"] % 3]
                cntD["w"] += 1
                kb.dma(SP, sl.ds, sl.t[:, 0:nf, :], wfd_t[c][:, f0:f0 + nf, :], writes=[sl.r])
                pendD.append((sl, f0, nf))

        def castD():
            wdl = wd[cntD["l"] % 2]
            while pendD:
                sl, f0, nf = pendD.pop(0)
                kb.op(DVE, "tensor_copy", dict(out=wdl.t[:, f0:f0 + nf, :], in_=sl.t[:, 0:nf, :]),
                      reads=[sl.r], writes=[wdl.r])
            cntD["l"] += 1

        def loadD(c):
            fillD(c)
            castD()

        steps = ([(c, (0, 1)) for c in range(KC)] if not tc_outer
                 else [(c, (0,)) for c in range(KC)] + [(c, (1,)) for c in range(KC)])
        loadD(steps[0][0])
        for si, (c, tcs) in enumerate(steps):
            if si + 1 < len(steps):
                fillD(steps[si + 1][0])
            wdc = wd[si % 2]
            for ti, tc in enumerate(tcs):
                if ti == len(tcs) - 1 and si + 1 < len(steps):
                    castD()
                bt, br_ = banks[cntD["b"] % 4]
                cntD["b"] += 1
                hsl = slice(tc * 512, (tc + 1) * 512)
                kb.group(PE, [("matmul", dict(out=bt[:, :], lhsT=wdc.t[:, f, :], rhs=aT.t[:, f, hsl],
                                              start=(f == 0), stop=(f == FC - 1))) for f in range(FC)],
                         reads=[wdc.r, aT.r], writes=[br_])
                fs = ffst[cntD["m"] % 2]
                sf = sqf[cntD["m"] % 3]
                cntD["m"] += 1
                kb.op(ACT, "activation", dict(out=fs.t[:, :], in_=bt[:, :], func=AF.Copy), writes=[br_, fs.r])
                kb.dma(ACT, fs.ds, ffs[c * 128:(c + 1) * 128, t0 + tc * 512:t0 + (tc + 1) * 512], fs.t[:, :],
                       reads=[fs.r], writes=[ff_res[c][2 * hf + tc]])
                kb.op(ACT, "activation", dict(out=sf.t[:, :], in_=bt[:, :], func=AF.Square), writes=[br_, sf.r])
                pend.append((6 + tc, sf.t[:, :], sf.r, c))
                if len(pend) > 1:
                    stat_mm(*pend.pop(0))
            filler.tick()
            if tc_outer and si == KC - 1:
                while pend:
                    stat_mm(*pend.pop(0))
                filler.flush()
                rstd_from_banks([6], rstd3[hf].t, rstd3[hf].r, 0, tmpD)
                al.kill(u2.r)
                sv = al.top
                al.top = u2.off
                filler = Filler(final_items(hf, rstd3[hf], tcs=(0,)), KC)
                fin_tail = final_items(hf, rstd3[hf], tcs=(1,), nbuf=3)
                al.top = sv
        while pend:
            stat_mm(*pend.pop(0))
        filler.flush()
        if tc_outer:
            rstd_from_banks([7], rstd3[hf].t, rstd3[hf].r, 512, tmpD)
            for it in fin_tail:
                it()
        else:
            rstd_from_banks([6, 7], rstd3[hf].t, rstd3[hf].r, 0, tmpD)

    u2i = u2_items(0, u2, nbuf=6)
    for it in u2i:
        it()
    al.release(mB)
    hp_items, hp_fin = hpass_items(1, 6)
    fill = Filler(hp_items, FC)
    GATEUP(0, fill)
    fill.flush()
    hp_fin()
    rstd_from_banks([6, 7], rstd2.t, rstd2.r, TH, tmpD)
    al.release(mB)
    fill = Filler(u2_items(1, u2), KC)
    DOWN(0, fill)
    fill.flush()
    al.release(mB)
    fill = Filler(final_items(0, rstd3[0]), FC)
    GATEUP(1, fill)
    fill.flush()
    al.release(mB)
    mD = al.mark()
    fbufs_dummy = None
    DOWN(1, nofill, tc_outer=True)

    finish()


def _consts():
    j = np.arange(128)[:, None]
    s = np.arange(128)[None, :]
    cb = np.concatenate([
        np.ones((128, 128)),
        -(j >= s).astype(np.float64),
        -np.ones((128, 128)),
        (j < s).astype(np.float64),
        (j <= s).astype(np.float64),
        np.zeros((128, 128)),
        np.eye(128),
    ], axis=1).astype(ml_dtypes.bfloat16)
    cfm = np.concatenate([(j <= s).astype(np.float32), np.ones((128, 128), np.float32)], axis=1)
    return np.ascontiguousarray(cb), np.ascontiguousarray(cfm)


def make_in_maps(inputs, cores):
    cb, cfm = _consts()
    g = np.stack([np.asarray(inputs[k], np.float32)[0] for k in
                  ("norm_mix_pre", "norm_mix_post", "norm_ffn_pre", "norm_ffn_post")])
    gains = np.ascontiguousarray(g.reshape(4, KC, 128).transpose(2, 0, 1).reshape(128, 4 * KC))
    bfg = np.ascontiguousarray(np.broadcast_to(
        np.tile(np.asarray(inputs["b_forget"], np.float32)[0], NB)[None, :], (128, 128)))
    def tiles(w):
        w = np.asarray(w, np.float32)
        K_, N_ = w.shape
        return np.ascontiguousarray(w.reshape(K_ // 128, 128, N_ // 128, 128).transpose(2, 1, 0, 3))

    w_in = np.asarray(inputs["w_in"], np.float32)[0]
    wb = np.concatenate([tiles(np.asarray(inputs["w_branch_sb"], np.float32)[0]),
                         tiles(np.asarray(inputs["w_branch_fox"], np.float32)[0])], axis=2)
    shared = {
        "wqkv_t": tiles(w_in[:, 0:6144]),
        "wf_t": np.ascontiguousarray(w_in[:, COL_F:COL_F + 8].reshape(KC, 128, 8).transpose(1, 0, 2)),
        "wg_t": tiles(w_in[:, COL_GSB:COL_GSB + 4096]),
        "wb_t": np.ascontiguousarray(wb),
        "wo_t": tiles(np.asarray(inputs["w_out"], np.float32)[0]),
        "wfg_t": tiles(np.asarray(inputs["w_ffn_gate"], np.float32)[0]),
        "wfu_t": tiles(np.asarray(inputs["w_ffn_up"], np.float32)[0]),
        "wfd_t": tiles(np.asarray(inputs["w_ffn_down"], np.float32)[0]),
        "gains": gains, "bfg": bfg, "cbf": cb, "cf32": cfm,
    }
    x = np.asarray(inputs["x"], np.float32)
    maps = []
    for b in cores:
        m = dict(shared)
        m["xT"] = np.ascontiguousarray(x[b].T)
        maps.append(m)
    return maps


def kernel(**inputs):
    nc = build_nc()
    in_maps = make_in_maps(inputs, list(range(8)))
    res = run_bass_kernel_spmd(nc, in_maps, core_ids=list(range(8)))
    out = np.stack([np.ascontiguousarray(np.asarray(r["outT"], np.float32).T) for r in res.results])
    return out
```
